# Optimizing a Trainium2 kernel written in Bass

```python
import jax, jax.numpy as jnp
from jax import lax
import numpy as np

D_MODEL = 1024
BATCH = 4
SEQ = 8192
DEPTH = 1

HEAD_DIM = 64
N_HEADS = D_MODEL // HEAD_DIM
N_SB_HEADS = N_HEADS // 2
N_DIL_HEADS = N_HEADS - N_SB_HEADS
SB_WIDTH = N_SB_HEADS * HEAD_DIM
DIL_WIDTH = N_DIL_HEADS * HEAD_DIM
MIX_WIDTH = SB_WIDTH + DIL_WIDTH
IN_WIDTH = 3 * SB_WIDTH + 3 * DIL_WIDTH
DILATED_PATTERNS = ((128, 1), (512, 4), (2048, 16))
Q_BLOCK = 128
D_FF = ((8 * D_MODEL + 3 * 256 - 1) // (3 * 256)) * 256
ALPHA = (2.0 * DEPTH) ** 0.25
BETA = (8.0 * DEPTH) ** -0.25
LN_EPS = 1e-5
RMS_EPS = 1e-6

kernel_name = "stickbreak_dilated_hybrid_deepnorm"


def _layer_norm(x, g, b):
    xf = x.astype(jnp.float32)
    mu = jnp.mean(xf, axis=-1, keepdims=True)
    var = jnp.mean(jnp.square(xf - mu), axis=-1, keepdims=True)
    y = (xf - mu) * lax.rsqrt(var + LN_EPS) * g.astype(jnp.float32) + b.astype(jnp.float32)
    return y.astype(x.dtype)


def _heads(t):
    b, s, w = t.shape
    return t.reshape(b, s, w // HEAD_DIM, HEAD_DIM).transpose(0, 2, 1, 3)


def _head_rms_merge(o, gain):
    of = o.astype(jnp.float32)
    of = of * lax.rsqrt(jnp.mean(jnp.square(of), axis=-1, keepdims=True) + RMS_EPS)
    b, h, s, dh = o.shape
    of = of.transpose(0, 2, 1, 3).reshape(b, s, h * dh)
    return (of * gain.astype(jnp.float32)).astype(o.dtype)


def _alibi_slopes(n_heads):
    return jnp.exp2(-8.0 * (jnp.arange(n_heads, dtype=jnp.float32) + 1.0) / n_heads)


def stick_breaking_attention(q, k, v):
    b, h, s, dh = q.shape
    nb = s // Q_BLOCK
    scale = dh ** -0.5
    key_pos = jnp.arange(s)
    q_blocks = q.reshape(b, h, nb, Q_BLOCK, dh).transpose(2, 0, 1, 3, 4)

    def block(args):
        q_blk, i = args
        q_pos = i * Q_BLOCK + jnp.arange(Q_BLOCK)
        z = jnp.einsum('bhqd,bhkd->bhqk', q_blk, k).astype(jnp.float32) * scale
        mask = key_pos[None, :] < q_pos[:, None]
        log_stay = jnp.where(mask, jax.nn.log_sigmoid(-z), 0.0)
        after = lax.cumsum(log_stay, axis=3, reverse=True) - log_stay
        w = jnp.where(mask, jnp.exp(jax.nn.log_sigmoid(z) + after), 0.0)
        return jnp.einsum('bhqk,bhkd->bhqd', w.astype(v.dtype), v)

    out = lax.map(block, (q_blocks, jnp.arange(nb)))
    return out.transpose(1, 2, 0, 3, 4).reshape(b, h, s, dh)


def dilated_window_attention(q, k, v, slopes):
    b, h, s, dh = q.shape
    nb = s // Q_BLOCK
    scale = dh ** -0.5
    q_blocks = q.reshape(b, h, nb, Q_BLOCK, dh).transpose(2, 0, 1, 3, 4)
    slopes_f = slopes.astype(jnp.float32)[None, :, None, None]

    def block(args):
        q_blk, i = args
        q_pos = i * Q_BLOCK + jnp.arange(Q_BLOCK)
        outs, maxes, denoms = [], [], []
        for window, dil in DILATED_PATTERNS:
            dist = jnp.arange(window // dil + 1) * dil
            k_pos = q_pos[:, None] - dist[None, :]
            valid = k_pos >= 0
            idx = jnp.maximum(k_pos, 0)
            k_g = jnp.take(k, idx, axis=2)
            v_g = jnp.take(v, idx, axis=2)
            sc = jnp.einsum('bhqd,bhqjd->bhqj', q_blk, k_g).astype(jnp.float32) * scale
            sc = sc - slopes_f * dist.astype(jnp.float32)[None, None, None, :]
            sc = jnp.where(valid, sc, -jnp.inf)
            m = jnp.max(sc, axis=-1, keepdims=True)
            p = jnp.exp(sc - m)
            l = jnp.sum(p, axis=-1, keepdims=True)
            outs.append(jnp.einsum('bhqj,bhqjd->bhqd', (p / l).astype(v.dtype), v_g).astype(jnp.float32))
            maxes.append(m)
            denoms.append(l)
        m_all = jnp.stack(maxes)
        l_all = jnp.stack(denoms)
        wts = l_all * jnp.exp(m_all - jnp.max(m_all, axis=0, keepdims=True))
        wts = wts / jnp.sum(wts, axis=0, keepdims=True)
        return jnp.sum(wts * jnp.stack(outs), axis=0).astype(q.dtype)

    out = lax.map(block, (q_blocks, jnp.arange(nb)))
    return out.transpose(1, 2, 0, 3, 4).reshape(b, h, s, dh)


def setup_inputs(seed: int = 0) -> dict:
    key = jax.random.key(seed)
    ks = jax.random.split(key, 12)
    f32 = jnp.float32
    x = jax.random.normal(ks[0], (BATCH, SEQ, D_MODEL), f32)
    col_scale = jnp.concatenate([
        jnp.ones((2 * SB_WIDTH,), f32), jnp.full((SB_WIDTH,), BETA, f32),
        jnp.ones((2 * DIL_WIDTH,), f32), jnp.full((DIL_WIDTH,), BETA, f32)])
    w_in = jax.random.normal(ks[1], (DEPTH, D_MODEL, IN_WIDTH), f32) * (D_MODEL ** -0.5) * col_scale
    g_sb = 1.0 + 0.02 * jax.random.normal(ks[2], (DEPTH, SB_WIDTH), f32)
    g_dil = 1.0 + 0.02 * jax.random.normal(ks[3], (DEPTH, DIL_WIDTH), f32)
    w_out = jax.random.normal(ks[4], (DEPTH, MIX_WIDTH, D_MODEL), f32) * (MIX_WIDTH ** -0.5) * BETA
    ln1_g = 1.0 + 0.02 * jax.random.normal(ks[5], (DEPTH, D_MODEL), f32)
    ln1_b = 0.02 * jax.random.normal(ks[6], (DEPTH, D_MODEL), f32)
    w_gate = jax.random.normal(ks[7], (DEPTH, D_MODEL, D_FF), f32) * (D_MODEL ** -0.5) * BETA
    w_up = jax.random.normal(ks[8], (DEPTH, D_MODEL, D_FF), f32) * (D_MODEL ** -0.5) * BETA
    w_down = jax.random.normal(ks[9], (DEPTH, D_FF, D_MODEL), f32) * (D_FF ** -0.5) * BETA
    ln2_g = 1.0 + 0.02 * jax.random.normal(ks[10], (DEPTH, D_MODEL), f32)
    ln2_b = 0.02 * jax.random.normal(ks[11], (DEPTH, D_MODEL), f32)
    return {"x": x, "w_in": w_in, "g_sb": g_sb, "g_dil": g_dil, "w_out": w_out,
            "ln1_g": ln1_g, "ln1_b": ln1_b, "w_gate": w_gate, "w_up": w_up,
            "w_down": w_down, "ln2_g": ln2_g, "ln2_b": ln2_b}


def reference(x, w_in, g_sb, g_dil, w_out, ln1_g, ln1_b, w_gate, w_up, w_down, ln2_g, ln2_b):
    slopes = _alibi_slopes(N_DIL_HEADS)
    split_at = [SB_WIDTH, 2 * SB_WIDTH, 3 * SB_WIDTH,
                3 * SB_WIDTH + DIL_WIDTH, 3 * SB_WIDTH + 2 * DIL_WIDTH]
    h = x
    for l in range(DEPTH):
        proj = jnp.einsum('bsd,dn->bsn', h, w_in[l])
        q_a, k_a, v_a, q_b, k_b, v_b = jnp.split(proj, split_at, axis=-1)
        o_a = stick_breaking_attention(_heads(q_a), _heads(k_a), _heads(v_a))
        o_b = dilated_window_attention(_heads(q_b), _heads(k_b), _heads(v_b), slopes)
        mixed = jnp.concatenate([_head_rms_merge(o_a, g_sb[l]),
                                 _head_rms_merge(o_b, g_dil[l])], axis=-1)
        mix_out = jnp.einsum('bsm,md->bsd', mixed, w_out[l])
        h = _layer_norm(ALPHA * h + mix_out, ln1_g[l], ln1_b[l])
        gate = jnp.einsum('bsd,df->bsf', h, w_gate[l])
        up = jnp.einsum('bsd,df->bsf', h, w_up[l])
        ffn = jnp.einsum('bsf,fd->bsd', jax.nn.silu(gate) * up, w_down[l])
        h = _layer_norm(ALPHA * h + ffn, ln2_g[l], ln2_b[l])
    return h
```

```python
import numpy as np
from contextlib import ExitStack

import concourse.bass as bass
import concourse.mybir as mybir
from concourse.bass_utils import run_bass_kernel_spmd

F32 = mybir.dt.float32
BF16 = mybir.dt.bfloat16
AF = mybir.ActivationFunctionType
ALU = mybir.AluOpType

D = 1024
S = 8192
NB = 4
DFF = 2816
NF = DFF // 128
CH = 512
NV = S // CH
ALPHA = 2.0 ** 0.25
LN_EPS = 1e-5
RMS_EPS = 1e-6
MARG = 2048
NEG = -30000.0
INTERLEAVE_DIL = False
ZIP_DIL = False
BIGN = 131072.0

C_ID = 0
C_NTRI = 128
C_NONES = 256
C_MASK = 384
C_NP12 = C_MASK + 4 * 512
C_NP3A = C_NP12 + 1024
C_NP3B = C_NP3A + 1024
C_SID = C_NP3B + 1024
C_MEAN = C_SID + 12 * 128
C_ONES = C_MEAN + 64
C_MEANE = C_ONES + 64
NCST = C_MEANE + 64

ENGS = ("sync", "tensor", "scalar", "vector", "gpsimd")


def _consts():
    c = np.zeros((128, NCST), np.float32)
    j = np.arange(128)[:, None]
    s = np.arange(128)[None, :]
    c[:, C_ID:C_ID + 128] = (j == s)
    c[:, C_NTRI:C_NTRI + 128] = -(j >= s).astype(np.float32)
    c[:, C_NONES:C_NONES + 128] = -1.0
    q = np.arange(512)[None, :]
    for mb in range(4):
        c[:, C_MASK + mb * 512:C_MASK + (mb + 1) * 512] = np.where(mb * 128 + j >= q, NEG, 0.0)

    def npat(G, W, qs_of):
        out = np.zeros((128, 1024), np.float32)
        for g in range(G):
            for half in range(2):
                for qi in range(W):
                    qs = qs_of(qi)
                    col = g * 2 * W + half * W + qi
                    if half == 1:
                        n = qs - np.arange(128)
                    else:
                        n = qs - np.arange(128) + 128
                    ok = (n >= 0) & (n <= 128)
                    out[:, col] = np.where(ok, n, BIGN)
        return out

    c[:, C_NP12:C_NP12 + 1024] = npat(4, 128, lambda qi: qi)
    c[:, C_NP3A:C_NP3A + 1024] = npat(16, 32, lambda qi: 32 + qi)
    c[:, C_NP3B:C_NP3B + 1024] = npat(16, 32, lambda qi: 96 + qi)
    for e in range(-8, 4):
        c[:, C_SID + (e + 8) * 128:C_SID + (e + 9) * 128] = -(2.0 ** e) * (j == s)
    c[:, C_MEAN:C_MEAN + 64] = 1.0 / 64.0
    c[:, C_ONES:C_ONES + 64] = 1.0
    c[0:64, C_MEANE:C_MEANE + 64] = 1.0 / 64.0
    c[64, C_MEANE:C_MEANE + 64] = RMS_EPS
    return c


class Buf:
    __slots__ = ("name", "lw", "rd", "rdd", "sem", "dcnt", "uid")
    _n = [0]

    def __init__(self, name):
        Buf._n[0] += 1
        self.uid = Buf._n[0]
        self.name = name
        self.lw = None
        self.rd = {}
        self.rdd = []
        self.sem = None
        self.dcnt = 0


class Op:
    __slots__ = ("eng", "fn", "deps", "needed", "val", "dbuf", "flushed")

    def __init__(self, eng, fn, dbuf):
        self.eng = eng
        self.fn = fn
        self.deps = []
        self.needed = False
        self.val = None
        self.dbuf = dbuf
        self.flushed = False


class Prog:
    def __init__(self, nc, esems, dsems):
        self.nc = nc
        self.esems = esems
        self.dsems = list(dsems)
        self.ops = {e: [] for e in ENGS}
        self.cnt = {e: 0 for e in ENGS}
        self.waited = {e: {} for e in ENGS}
        self.dma_ops = []

    def op(self, eng, fn, reads=(), writes=(), dbuf=None):
        o = Op(eng, fn, dbuf)
        deps = {}
        for b in reads:
            if b.lw is not None:
                deps[id(b.lw)] = b.lw
        for b in writes:
            if b.lw is not None:
                deps[id(b.lw)] = b.lw
            for r in b.rd.values():
                deps[id(r)] = r
            for r in b.rdd:
                deps[id(r)] = r
        for d in deps.values():
            if d is o:
                continue
            if d.dbuf is None and d.eng == "tensor" and eng == "tensor":
                continue
            o.deps.append(d)
            d.needed = True
        for b in reads:
            if dbuf is not None:
                b.rdd.append(o)
            else:
                b.rd[eng] = o
        for b in writes:
            b.lw = o
            b.rd = {}
            b.rdd = []
        if dbuf is not None:
            if dbuf.sem is None:
                dbuf.sem = self.dsems.pop()
            dbuf.dcnt += 16
            o.val = dbuf.dcnt
            self.dma_ops.append(o)
        self.ops[eng].append(o)
        return o

    def flush(self, final=False):
        nc = self.nc
        for e in ENGS:
            for o in self.ops[e]:
                if o.dbuf is None and o.needed:
                    self.cnt[e] += 1
                    o.val = self.cnt[e]
        pending_dma = [o for o in self.dma_ops]
        self.dma_ops = []
        with nc.Block() as block:
            for e in ENGS:
                ops = self.ops[e]
                if not ops and not (e == "gpsimd"):
                    continue

                def body(eng, e=e, ops=ops):
                    waited = self.waited[e]
                    for o in ops:
                        for d in o.deps:
                            if d.dbuf is not None:
                                key = ("d", d.dbuf.uid)
                                sem = d.dbuf.sem
                            else:
                                if d.val is None:
                                    assert d.flushed
                                    continue
                                key = ("e", d.eng)
                                sem = self.esems[d.eng]
                            if waited.get(key, 0) >= d.val:
                                continue
                            waited[key] = d.val
                            eng.wait_ge(sem, d.val)
                        ins = o.fn(eng)
                        if o.dbuf is not None:
                            ins.then_inc(o.dbuf.sem, 16)
                        elif o.needed:
                            ins.then_inc(self.esems[e], 1)
                    if e == "gpsimd":
                        for o in pending_dma:
                            key = ("d", o.dbuf.uid)
                            if waited.get(key, 0) >= o.val:
                                continue
                            waited[key] = o.val
                            eng.wait_ge(o.dbuf.sem, o.val)

                getattr(block, e)(body)
        for e in ENGS:
            for o in self.ops[e]:
                o.flushed = True
            self.ops[e] = []


def build_nc():
    nc = bass.Bass("TRN2", target_bir_lowering=False)

    def din(name, shape, dt=F32):
        return nc.dram_tensor(name, list(shape), dt, kind="ExternalInput").ap()

    xT_d = din("xT", [D, S])
    xn_d = din("xn", [S // 2, D])
    kv_d = din("kv", [128, 64])
    cst_d = din("cst", [128, NCST])
    win_d = din("w_in", [D, 3 * D])
    wout_d = din("w_out", [D, D])
    wg_d = din("w_gate", [D, DFF])
    wu_d = din("w_up", [D, DFF])
    wd_d = din("w_down", [DFF, D])
    gsb_d = din("gsb", [64, 8])
    gdil_d = din("gdil", [64, 8])
    ln1g_d = din("ln1g", [128, D])
    ln1b_d = din("ln1b", [128, D])
    ln2g_d = din("ln2g", [128, D])
    ln2b_d = din("ln2b", [128, D])
    out_d = nc.dram_tensor("out", [S // 2, D], F32, kind="ExternalOutput").ap()
    vscr_d = nc.dram_tensor("vscr", [MARG + S, 130], BF16).ap()
    mixT_d = nc.dram_tensor("mixT", [D, S // 2], BF16).ap()
    wgs_d = nc.dram_tensor("wgs", [NF, 128, 1024], BF16).ap()
    wus_d = nc.dram_tensor("wus", [NF, 128, 1024], BF16).ap()
    wds_d = nc.dram_tensor("wds", [NF, 128, 1024], BF16).ap()

    with ExitStack() as top:
        esems = {e: top.enter_context(nc.semaphore("es_" + e)) for e in ENGS}
        dsems = [top.enter_context(nc.semaphore("ds%d" % i)) for i in range(96)]
        P = Prog(nc, esems, dsems)

        PB = [Buf("bank%d" % i) for i in range(8)]

        vscr_b = Buf("vscr")
        mixT_b = Buf("mixT")
        wsc_b = Buf("wscr")

        with ExitStack() as ph:
            def sb(name, shape, dt):
                return ph.enter_context(nc.sbuf_tensor("a_" + name, list(shape), dt))

            pbig = ph.enter_context(nc.psum_tensor("a_pbig", [128, 1024], F32))
            pbigB = ph.enter_context(nc.psum_tensor("a_pbigB", [128, 1024], F32))
            pbk = [ph.enter_context(nc.psum_tensor("a_pb%d" % i, [128, 512], F32)) for i in range(4, 8)]

            def bank(i):
                if i < 2:
                    return pbig[:, i * 512:(i + 1) * 512]
                if i < 4:
                    return pbigB[:, (i - 2) * 512:(i - 1) * 512]
                return pbk[i - 4][:, :]

            cst = sb("cst", [128, NCST], BF16)
            cst_b = Buf("cst")
            stg = [sb("stg%d" % i, [128, 4096], F32) for i in range(2)]
            stg_b = [Buf("stg%d" % i) for i in range(2)]
            xb = [sb("xb%d" % i, [128, 8, CH], BF16) for i in range(2)]
            xb_b = [Buf("xb%d" % i) for i in range(2)]
            wp = sb("wp", [128, 8, 768], BF16)
            wp_b = Buf("wp")
            KaT = sb("KaT", [128, S], BF16)
            KaT_c = [Buf("KaT%d" % i) for i in range(NV)]
            Va = sb("Va", [128, 64, 128], BF16)
            Va_c = [Buf("Va%d" % i) for i in range(NV)]
            KbT = sb("KbT", [128, MARG + S], BF16)
            KbT_c = [Buf("KbT%d" % i) for i in range(NV)]
            QaT = [sb("QaT%d" % i, [128, CH], BF16) for i in range(2)]
            QaT_b = [Buf("QaT%d" % i) for i in range(2)]
            QbT = [sb("QbT%d" % i, [128, CH], BF16) for i in range(2)]
            QbT_b = [Buf("QbT%d" % i) for i in range(2)]
            kvf = sb("kvf", [128, 64], F32)
            kvb = sb("kvb", [128, 64], BF16)
            kv_b = Buf("kv")
            vst = [sb("vst%d" % i, [128, 4, 130], BF16) for i in range(2)]
            vst_b = [Buf("vst%d" % i) for i in range(2)]
            vb1 = sb("vb1", [128, 5, 130], BF16)
            vb4 = sb("vb4", [128, 8, 130], BF16)
            vb16 = sb("vb16", [128, 32, 130], BF16)
            vb1_b, vb4_b, vb16_b = Buf("vb1"), Buf("vb4"), Buf("vb16")
            e_t = [sb("e_t%d" % i, [128, 2 * CH], F32) for i in range(2)]
            e_b = [Buf("e_t%d" % i) for i in range(2)]
            sp_t = [sb("sp_t%d" % i, [128, 2 * CH], BF16) for i in range(2)]
            sp_b = [Buf("sp_t%d" % i) for i in range(2)]
            w_t = [sb("w_t%d" % i, [128, 2 * CH], BF16) for i in range(2)]
            w_b = [Buf("w_t%d" % i) for i in range(2)]
            Rt = [sb("R%d" % i, [128, 2 * CH], BF16) for i in range(3)]
            R_b = [Buf("R%d" % i) for i in range(3)]
            osb = [sb("osb%d" % i, [64, CH], F32) for i in range(2)]
            osb_b = [Buf("osb%d" % i) for i in range(2)]
            dtc = [sb("dt%d" % i, [128, CH], F32) for i in range(2)]
            dtc_b = [Buf("dt%d" % i) for i in range(2)]
            etc_ = [[sb("etc%d_%d" % (i, j), [128, CH], BF16) for j in range(2)] for i in range(2)]
            etc_b = [[Buf("etc%d_%d" % (i, j)) for j in range(2)] for i in range(2)]
            sqc = [sb("sqc%d" % i, [128, CH], BF16) for i in range(2)]
            sqc_b = [Buf("sqc%d" % i) for i in range(2)]
            lnvc = [sb("lnvc%d" % i, [64, CH], F32) for i in range(2)]
            lnvc_b = [Buf("lnvc%d" % i) for i in range(2)]
            rstdc = [sb("rstdc%d" % i, [64, CH], F32) for i in range(2)]
            rstdc_b = [Buf("rstdc%d" % i) for i in range(2)]
            mixt = [sb("mixt%d" % i, [64, CH], BF16) for i in range(2)]
            mixt_b = [Buf("mixt%d" % i) for i in range(2)]
            gsb_t = sb("gsb_t", [64, 8], F32)
            gdil_t = sb("gdil_t", [64, 8], F32)
            g_b = Buf("g")

            for i in range(2):
                c0 = i * 3616
                P.op("sync", lambda e, i=i, c0=c0: e.dma_start(out=stg[i][:, 0:3616], in_=cst_d[:, c0:c0 + 3616]),
                     writes=[stg_b[i]], dbuf=stg_b[i])
                P.op("vector", lambda e, i=i, c0=c0: e.tensor_copy(out=cst[:, c0:c0 + 3616], in_=stg[i][:, 0:3616]),
                     reads=[stg_b[i]], writes=[cst_b])
            P.op("sync", lambda e: e.dma_start(out=kvf[:, :], in_=kv_d[:, :]), writes=[kv_b], dbuf=kv_b)
            P.op("vector", lambda e: e.tensor_copy(out=kvb[:, :], in_=kvf[:, :]), reads=[kv_b], writes=[kv_b])
            P.op("sync", lambda e: e.dma_start(out=gsb_t[:, :], in_=gsb_d[:, :]), writes=[g_b], dbuf=g_b)
            P.op("sync", lambda e: e.dma_start(out=gdil_t[:, :], in_=gdil_d[:, :]), writes=[g_b], dbuf=g_b)
            P.op("gpsimd", lambda e: e.memset(KbT[:, :], 0.0), writes=list(KbT_c))
            P.op("gpsimd", lambda e: e.memset(vst[0][:, :, :], 0.0), writes=[vst_b[0]])
            for r0 in range(0, MARG + S, 512):
                P.op("sync", lambda e, r0=r0: e.dma_start(
                    out=vscr_d[r0:r0 + 512, :].rearrange("(t p) c -> p t c", p=128), in_=vst[0][:, :, :]),
                    reads=[vst_b[0]], writes=[vscr_b], dbuf=vst_b[0])

            xT_v = xT_d.rearrange("(c p) t -> p c t", p=128)
            win_v = win_d.rearrange("(c p) n -> p c n", p=128)
            ld_n = [0]

            tb = [sb("tb%d" % i, [128, DFF], BF16) for i in range(2)]
            tb_b = [Buf("tb%d" % i) for i in range(2)]
            w0 = []
            n0 = 0
            for (src, dst) in ((wg_d, wgs_d), (wu_d, wus_d)):
                dview = dst.rearrange("f p (c j) -> p f c j", c=8)
                for kc in range(8):
                    sl0 = n0 % 2
                    n0 += 1

                    def ld(sl0=sl0, src=src, kc=kc):
                        P.op("sync", lambda e: e.dma_start(out=stg[sl0][:, 0:DFF], in_=src[kc * 128:(kc + 1) * 128, :]),
                             writes=[stg_b[sl0]], dbuf=stg_b[sl0])

                    def cs_(sl0=sl0, dview=dview, kc=kc):
                        P.op("vector", lambda e: e.tensor_copy(out=tb[sl0][:, :], in_=stg[sl0][:, 0:DFF]),
                             reads=[stg_b[sl0]], writes=[tb_b[sl0]])
                        P.op("sync", lambda e: e.dma_start(
                            out=dview[:, :, kc, :], in_=tb[sl0][:, :].rearrange("p (f j) -> p f j", j=128)),
                            reads=[tb_b[sl0]], writes=[wsc_b], dbuf=tb_b[sl0])
                    w0.append((ld, cs_))
            for f0 in range(0, NF, 2):
                sl0 = n0 % 2
                n0 += 1

                def ld(sl0=sl0, f0=f0):
                    P.op("sync", lambda e: e.dma_start(
                        out=stg[sl0][:, 0:2048].rearrange("p (f n) -> p f n", f=2),
                        in_=wd_d[f0 * 128:(f0 + 2) * 128, :].rearrange("(f p) n -> p f n", p=128)),
                        writes=[stg_b[sl0]], dbuf=stg_b[sl0])

                def cs_(sl0=sl0, f0=f0):
                    P.op("vector", lambda e: e.tensor_copy(out=tb[sl0][:, 0:2048], in_=stg[sl0][:, 0:2048]),
                         reads=[stg_b[sl0]], writes=[tb_b[sl0]])
                    P.op("sync", lambda e: e.dma_start(
                        out=wds_d[f0:f0 + 2, :, :].rearrange("f p n -> p f n"),
                        in_=tb[sl0][:, 0:2048].rearrange("p (f n) -> p f n", f=2)),
                        reads=[tb_b[sl0]], writes=[wsc_b], dbuf=tb_b[sl0])
                w0.append((ld, cs_))
            def p0_group(pieces):
                out = []
                for i in range(len(pieces) + 1):
                    if i < len(pieces):
                        out.append(pieces[i][0])
                    if i >= 1:
                        out.append(pieces[i - 1][1])
                return out

            def norm_thunks(src, src_b, nrow, h_glob, is_dil, kown, gtile, ch=0, mbk=6):
                th = []
                lcol = C_MEANE if nrow == 65 else C_MEAN

                def t1():
                    P.op("scalar", lambda e: e.activation(out=sqc[ch][0:nrow, :], in_=src[0:nrow, :], func=AF.Square),
                         reads=[src_b], writes=[sqc_b[ch]])
                    P.op("tensor", lambda e: e.matmul(bank(mbk)[0:64, :], lhsT=cst[0:nrow, lcol:lcol + 64], rhs=sqc[ch][0:nrow, :],
                                                      start=True, stop=True),
                         reads=[sqc_b[ch], cst_b], writes=[PB[mbk]])

                def t2():
                    if nrow == 65:
                        P.op("scalar", lambda e: e.activation(out=lnvc[ch][:, :], in_=bank(mbk)[0:64, :], func=AF.Ln),
                             reads=[PB[mbk]], writes=[lnvc_b[ch]])
                    else:
                        P.op("scalar", lambda e: e.activation(out=lnvc[ch][:, :], in_=bank(mbk)[0:64, :], func=AF.Ln,
                                                              bias=RMS_EPS, scale=1.0),
                             reads=[PB[mbk]], writes=[lnvc_b[ch]])
                    P.op("scalar", lambda e: e.activation(out=rstdc[ch][:, :], in_=lnvc[ch][:, :], func=AF.Exp, scale=-0.5),
                         reads=[lnvc_b[ch]], writes=[rstdc_b[ch]])

                def t3():
                    ms = ld_n[0] % 2
                    ld_n[0] += 1
                    P.op("vector", lambda e, ms=ms: e.scalar_tensor_tensor(
                        out=mixt[ms][:, :], in0=src[0:64, :], scalar=gtile[0:64, h_glob:h_glob + 1], in1=rstdc[ch][:, :],
                        op0=ALU.mult, op1=ALU.mult),
                        reads=[src_b, rstdc_b[ch], g_b], writes=[mixt_b[ms]])
                    row0 = (512 if is_dil else 0) + h_glob * 64
                    P.op("sync", lambda e, ms=ms, row0=row0: e.dma_start(
                        out=mixT_d[row0:row0 + 64, kown * CH:(kown + 1) * CH], in_=mixt[ms][:, :]),
                        reads=[mixt_b[ms]], writes=[mixT_b], dbuf=mixt_b[ms])
                return [t1, t2, t3]

            def dil_thunks(v, p, QB, QB_b, kown):
                chains = []
                for hd in range(2):
                    th = []
                    sbk, abk = (6, 7) if hd == 0 else (0, 1)
                    dt_, dt_b = dtc[hd], dtc_b[hd]
                    et, et_b = etc_[hd], etc_b[hd]
                    r = slice(hd * 64, hd * 64 + 64)
                    hg = 2 * p + hd
                    for pi in range(3):
                        ex = -(hg + 1) + 2 * pi
                        sid0 = C_SID + (ex + 8) * 128
                        np0 = C_NP12 if pi < 2 else (C_NP3A if v % 4 == 1 else C_NP3B)
                        G = 4 if pi < 2 else 16
                        W = 128 if pi < 2 else 32
                        for hb in range(2):
                            groups = list(range(hb * G // 2, (hb + 1) * G // 2))

                            def t1(hb=hb, groups=groups, sid0=sid0, np0=np0, pi=pi, W=W, r=r, sbk=sbk):
                                P.op("tensor", lambda e: e.matmul(
                                    bank(sbk), lhsT=cst[:, sid0:sid0 + 128], rhs=cst[:, np0 + hb * 512:np0 + (hb + 1) * 512],
                                    start=True, stop=False, skip_group_check=True),
                                    reads=[cst_b], writes=[PB[sbk]])
                                for g in groups:
                                    for half in range(2):
                                        col0 = g * 2 * W + half * W - hb * 512
                                        if pi == 0:
                                            blk = 4 * v + g - 1 + half
                                            k0 = MARG + blk * 128
                                            kap = KbT[r, k0:k0 + 128]
                                            qap = QB[r, g * 128:(g + 1) * 128]
                                            kbufs = [KbT_c[blk // 4]]
                                        elif pi == 1:
                                            k0 = MARG + (v - 1 + half) * 512 + g
                                            kap = KbT[r, k0:k0 + 509:4]
                                            qap = QB[r, g:g + 509:4]
                                            kbufs = [KbT_c[v - 1 + half]]
                                        else:
                                            U = v // 4 - 1 + half
                                            k0 = MARG + U * 2048 + g
                                            kap = KbT[r, k0:k0 + 2033:16]
                                            qap = QB[r, g:g + 497:16]
                                            kbufs = [KbT_c[c] for c in range(4 * U, 4 * U + 4) if 0 <= c <= v]
                                        P.op("tensor", lambda e, kap=kap, qap=qap, col0=col0: e.matmul(
                                            bank(sbk)[:, col0:col0 + W], lhsT=kap, rhs=qap, start=False, stop=False,
                                            skip_group_check=True),
                                            reads=kbufs + [QB_b], writes=[PB[sbk]])

                            def t2(hb=hb, sbk=sbk, et=et, et_b=et_b):
                                P.op("scalar", lambda e: e.activation(out=et[hb][:, :], in_=bank(sbk), func=AF.Exp),
                                     reads=[PB[sbk]], writes=[et_b[hb]])

                            def t3(hb=hb, groups=groups, pi=pi, W=W, hd=hd, abk=abk, et=et, et_b=et_b):
                                first = (hb == 0)
                                for g in groups:
                                    for half in range(2):
                                        col0 = g * 2 * W + half * W - hb * 512
                                        if pi == 0:
                                            vt = vb1[:, g + half, hd * 65:(hd + 1) * 65]
                                            vbuf = vb1_b
                                        elif pi == 1:
                                            vt = vb4[:, half * 4 + g, hd * 65:(hd + 1) * 65]
                                            vbuf = vb4_b
                                        else:
                                            vt = vb16[:, half * 16 + g, hd * 65:(hd + 1) * 65]
                                            vbuf = vb16_b
                                        P.op("tensor", lambda e, vt=vt, col0=col0, g=g, first=first: e.matmul(
                                            bank(abk)[0:65, g * W:(g + 1) * W], lhsT=vt, rhs=et[hb][:, col0:col0 + W],
                                            start=first, stop=False, skip_group_check=True),
                                            reads=[vbuf, et_b[hb]], writes=[PB[abk]])
                                        first = False
                            th += [t1, t2, t3]

                        def t4(pi=pi, abk=abk, dt_=dt_, dt_b=dt_b):
                            if pi == 0:
                                P.op("vector", lambda e: e.tensor_copy(out=dt_[0:65, :], in_=bank(abk)[0:65, :]),
                                     reads=[PB[abk]], writes=[dt_b])
                            else:
                                rr = 4 if pi == 1 else 16
                                P.op("vector", lambda e: e.tensor_tensor(
                                    out=dt_[0:65, :].rearrange("p (u r) -> p u r", r=rr),
                                    in0=dt_[0:65, :].rearrange("p (u r) -> p u r", r=rr),
                                    in1=bank(abk)[0:65, :].rearrange("p (r u) -> p u r", r=rr), op=ALU.add),
                                    reads=[PB[abk], dt_b], writes=[dt_b])
                        th.append(t4)
                    th += norm_thunks(dt_, dt_b, 65, hg, True, kown, gdil_t, ch=hd, mbk=sbk)
                    chains.append(th)
                if not ZIP_DIL:
                    return chains[0] + chains[1]
                out = []
                for i in range(max(len(c) for c in chains)):
                    for c in chains:
                        if i < len(c):
                            out.append(c[i])
                return out

            def load_x(v):
                s = v % 2
                P.op("sync", lambda e, s=s, v=v: e.dma_start(
                    out=stg[s][:, :].rearrange("p (c t) -> p c t", c=8), in_=xT_v[:, :, v * CH:(v + 1) * CH]),
                    writes=[stg_b[s]], dbuf=stg_b[s])
                P.op("vector", lambda e, s=s: e.tensor_copy(
                    out=xb[s][:, :, :], in_=stg[s][:, :].rearrange("p (c t) -> p c t", c=8)),
                    reads=[stg_b[s]], writes=[xb_b[s]])

            pj_n = [0]

            def proj_thunks(v):
                s = v % 2
                own = (v % 2 == 1)
                qs = ((v - 1) // 2) % 2
                th = []

                def fm(j, evac):
                    bk = 6 + pj_n[0] % 2
                    pj_n[0] += 1
                    for k0 in (0, 4):
                        def t_(k0=k0, bk=bk):
                            for kc in range(k0, k0 + 4):
                                P.op("tensor", lambda e, kc=kc: e.matmul(
                                    bank(bk), lhsT=wp[:, kc, j * 128:(j + 1) * 128], rhs=xb[s][:, kc, :],
                                    start=(kc == 0), stop=(kc == 7)),
                                    reads=[wp_b, xb_b[s]], writes=[PB[bk]])
                        th.append(t_)
                    th.append(lambda bk=bk: evac(bk))

                def tm(j, evac):
                    bk = 6 + pj_n[0] % 2
                    pj_n[0] += 1
                    for t in range(4):
                        def t_(t=t, bk=bk):
                            for kc in range(8):
                                P.op("tensor", lambda e, kc=kc: e.matmul(
                                    bank(bk)[:, t * 128:(t + 1) * 128], lhsT=xb[s][:, kc, t * 128:(t + 1) * 128],
                                    rhs=wp[:, kc, j * 128:(j + 1) * 128], start=(kc == 0), stop=(kc == 7)),
                                    reads=[wp_b, xb_b[s]], writes=[PB[bk]])
                        th.append(t_)
                    th.append(lambda bk=bk: evac(bk))

                fm(1, lambda bk: P.op("vector", lambda e: e.tensor_copy(out=KaT[:, v * CH:(v + 1) * CH], in_=bank(bk)),
                                      reads=[PB[bk]], writes=[KaT_c[v]]))
                fm(4, lambda bk: P.op("vector", lambda e: e.tensor_copy(
                    out=KbT[:, MARG + v * CH:MARG + (v + 1) * CH], in_=bank(bk)), reads=[PB[bk]], writes=[KbT_c[v]]))
                tm(2, lambda bk: P.op("vector", lambda e: e.tensor_copy(
                    out=Va[:, v * 4:(v + 1) * 4, :], in_=bank(bk).rearrange("p (t n) -> p t n", t=4)),
                    reads=[PB[bk]], writes=[Va_c[v]]))

                def evac_vb(bk):
                    P.op("vector", lambda e: e.tensor_copy(
                        out=vst[s][:, :, :].rearrange("p t (h c) -> p t h c", h=2)[:, :, :, 0:64],
                        in_=bank(bk).rearrange("p (t h c) -> p t h c", t=4, h=2)),
                        reads=[PB[bk]], writes=[vst_b[s]])
                    for hc in (64, 129):
                        P.op("vector", lambda e, hc=hc: e.tensor_copy(
                            out=vst[s][:, :, hc:hc + 1], in_=kvb[:, v * 4:(v + 1) * 4].rearrange("p (t o) -> p t o", o=1)),
                            reads=[kv_b], writes=[vst_b[s]])
                    r0 = MARG + v * CH
                    P.op("sync", lambda e: e.dma_start(
                        out=vscr_d[r0:r0 + 512, :].rearrange("(t p) c -> p t c", p=128), in_=vst[s][:, :, :]),
                        reads=[vst_b[s]], writes=[vscr_b], dbuf=vst_b[s])
                tm(5, evac_vb)
                if own:
                    fm(0, lambda bk: P.op("vector", lambda e: e.tensor_scalar(
                        out=QaT[qs][:, :], in0=bank(bk), scalar1=0.125, scalar2=None, op0=ALU.mult),
                        reads=[PB[bk]], writes=[QaT_b[qs]]))
                    fm(3, lambda bk: P.op("vector", lambda e: e.tensor_scalar(
                        out=QbT[qs][:, :], in0=bank(bk), scalar1=0.125, scalar2=None, op0=ALU.mult),
                        reads=[PB[bk]], writes=[QbT_b[qs]]))
                return th

            for p in range(4):
                for j, base in enumerate((0, 512, 1024, 1536, 2048, 2560)):
                    s = ld_n[0] % 2
                    ld_n[0] += 1
                    c0 = base + p * 128
                    P.op("sync", lambda e, s=s, c0=c0: e.dma_start(
                        out=stg[s][:, 0:1024].rearrange("p (c n) -> p c n", c=8), in_=win_v[:, :, c0:c0 + 128]),
                        writes=[stg_b[s]], dbuf=stg_b[s])
                    P.op("vector", lambda e, s=s, j=j: e.tensor_copy(
                        out=wp[:, :, j * 128:(j + 1) * 128], in_=stg[s][:, 0:1024].rearrange("p (c n) -> p c n", c=8)),
                        reads=[stg_b[s]], writes=[wp_b])
                load_x(0)
                load_x(1)
                for t_ in proj_thunks(0) + proj_thunks(1):
                    t_()
                carry = []
                for v in range(1, NV, 2):
                    kown = (v - 1) // 2
                    qs = kown % 2
                    QA, QB = QaT[qs], QbT[qs]
                    QA_b, QB_b = QaT_b[qs], QbT_b[qs]
                    b1 = MARG + (4 * v - 1) * 128
                    P.op("sync", lambda e, b1=b1: e.dma_start(
                        out=vb1[:, :, :], in_=vscr_d[b1:b1 + 640, :].rearrange("(t p) c -> p t c", p=128)),
                        reads=[vscr_b], writes=[vb1_b], dbuf=vb1_b)
                    for half in range(2):
                        b4 = MARG + (v - 1 + half) * 512
                        P.op("sync", lambda e, b4=b4, half=half: e.dma_start(
                            out=vb4[:, half * 4:(half + 1) * 4, :],
                            in_=vscr_d[b4:b4 + 512, :].rearrange("(s r) c -> s r c", r=4)),
                            reads=[vscr_b], writes=[vb4_b], dbuf=vb4_b)
                        b16 = MARG + (v // 4 - 1 + half) * 2048
                        P.op("sync", lambda e, b16=b16, half=half: e.dma_start(
                            out=vb16[:, half * 16:(half + 1) * 16, :],
                            in_=vscr_d[b16:b16 + 2048, :].rearrange("(s r) c -> s r c", r=16)),
                            reads=[vscr_b], writes=[vb16_b], dbuf=vb16_b)
                    side = carry
                    carry = []
                    if v + 2 < NV:
                        load_x(v + 1)
                        load_x(v + 2)
                        side = side + proj_thunks(v + 1) + proj_thunks(v + 2)
                    if p == 0:
                        take = len(w0) if v + 2 >= NV else min(len(w0), 4)
                        side = side + p0_group(w0[:take])
                        w0 = w0[take:]
                    side_b = dil_thunks(v, p, QB, QB_b, kown)
                    if INTERLEAVE_DIL:
                        side = side + side_b
                        side_b = []

                    nkb = 4 * (v + 1)

                    def st_A(i, b0, last):
                        kb = nkb - 1 - i
                        diag = kb >= 4 * v
                        mb = kb - 4 * v
                        ks = slice(kb * 128, (kb + 1) * 128)
                        for hd in range(2):
                            r = slice(hd * 64, hd * 64 + 64)
                            bk = b0 + hd
                            P.op("tensor", lambda e, bk=bk, r=r, ks=ks, diag=diag, last=last, QA=QA: e.matmul(
                                bank(bk), lhsT=KaT[r, ks], rhs=QA[r, :], start=True, stop=(last and not diag)),
                                reads=[KaT_c[kb // 4], QA_b], writes=[PB[bk]])
                            if diag:
                                P.op("tensor", lambda e, bk=bk, mb=mb, last=last: e.matmul(
                                    bank(bk), lhsT=cst[:, C_ID:C_ID + 128],
                                    rhs=cst[:, C_MASK + mb * 512:C_MASK + (mb + 1) * 512], start=False, stop=last),
                                    reads=[cst_b], writes=[PB[bk]])

                    def st_act1(i):
                        sl = i % 2
                        P.op("scalar", lambda e, sl=sl: e.activation(out=e_t[sl][:, :], in_=pbig[:, :], func=AF.Exp),
                             reads=[PB[0], PB[1]], writes=[e_b[sl]])
                        P.op("scalar", lambda e, sl=sl: e.activation(out=sp_t[sl][:, :], in_=e_t[sl][:, :], func=AF.Ln,
                                                                     bias=1.0, scale=1.0),
                             reads=[e_b[sl]], writes=[sp_b[sl]])

                    def st_R(i):
                        sl = i % 2
                        if i == 0:
                            P.op("vector", lambda e, sl=sl: e.tensor_copy(out=Rt[1][:, :], in_=sp_t[sl][:, :]),
                                 reads=[sp_b[sl]], writes=[R_b[1]])
                        elif i < nkb - 1:
                            P.op("vector", lambda e, sl=sl, i=i: e.tensor_tensor(
                                out=Rt[(i + 1) % 3][:, :], in0=Rt[i % 3][:, :], in1=sp_t[sl][:, :], op=ALU.add),
                                reads=[sp_b[sl], R_b[i % 3]], writes=[R_b[(i + 1) % 3]])

                    def st_B(i):
                        sl = i % 2
                        st_A(i, 2, False)
                        for hd in range(2):
                            cs = slice(hd * CH, (hd + 1) * CH)
                            P.op("tensor", lambda e, sl=sl, i=i, hd=hd, cs=cs: e.matmul(
                                bank(2 + hd), lhsT=cst[:, C_NTRI:C_NTRI + 128], rhs=sp_t[sl][:, cs],
                                start=False, stop=(i == 0)),
                                reads=[cst_b, sp_b[sl]], writes=[PB[2 + hd]])
                            if i > 0:
                                P.op("tensor", lambda e, sl=sl, i=i, hd=hd, cs=cs: e.matmul(
                                    bank(2 + hd), lhsT=cst[:, C_NONES:C_NONES + 128], rhs=Rt[i % 3][:, cs],
                                    start=False, stop=True),
                                    reads=[cst_b, R_b[i % 3]], writes=[PB[2 + hd]])

                    def st_act2(i):
                        sl = i % 2
                        P.op("scalar", lambda e, sl=sl: e.activation(out=w_t[sl][:, :], in_=pbigB[:, :], func=AF.Exp),
                             reads=[PB[2], PB[3]], writes=[w_b[sl]])

                    def st_AV(i):
                        sl = i % 2
                        kb = nkb - 1 - i
                        for hd in range(2):
                            cs = slice(hd * CH, (hd + 1) * CH)
                            P.op("tensor", lambda e, sl=sl, kb=kb, hd=hd, i=i, cs=cs, nkb=nkb: e.matmul(
                                bank(4 + hd)[0:64, :], lhsT=Va[:, kb, hd * 64:(hd + 1) * 64], rhs=w_t[sl][:, cs],
                                start=(i == 0), stop=(i == nkb - 1)),
                                reads=[Va_c[kb // 4], w_b[sl]], writes=[PB[4 + hd]])

                    nside = len(side)
                    per = -(-nside // max(1, nkb - 2)) if nside else 0
                    si = 0
                    for st in range(nkb + 2):
                        if st < nkb:
                            st_A(st, 0, True)
                            st_act1(st)
                            st_R(st)
                        if 0 <= st - 1 < nkb:
                            st_B(st - 1)
                            st_act2(st - 1)
                        if 0 <= st - 2 < nkb:
                            st_AV(st - 2)
                        for _ in range(per):
                            if si < nside:
                                side[si]()
                                si += 1
                    while si < nside:
                        side[si]()
                        si += 1
                    for t_ in side_b:
                        t_()
                    for hd in range(2):
                        ob = 4 + hd
                        P.op("vector", lambda e, ob=ob, hd=hd: e.tensor_copy(out=osb[hd][:, :], in_=bank(ob)[0:64, :]),
                             reads=[PB[ob]], writes=[osb_b[hd]])
                        carry += norm_thunks(osb[hd], osb_b[hd], 64, 2 * p + hd, False, kown, gsb_t, ch=hd, mbk=6 + hd)
                    if v + 2 >= NV:
                        for t_ in carry:
                            t_()
                        carry = []
            P.flush()

        with ExitStack() as ph:
            def sb(name, shape, dt):
                return ph.enter_context(nc.sbuf_tensor("b_" + name, list(shape), dt))

            pbig = ph.enter_context(nc.psum_tensor("b_pbig", [128, 1024], F32))
            pbk = [ph.enter_context(nc.psum_tensor("b_pb%d" % i, [128, 512], F32)) for i in range(2, 7)]
            pT = ph.enter_context(nc.psum_tensor("b_pT", [128, 1024], BF16))

            def bank(i):
                if i < 2:
                    return pbig[:, i * 512:(i + 1) * 512]
                return pbk[i - 2][:, :]

            idt = sb("idt", [128, 128], BF16)
            idf = sb("idf", [128, 128], F32)
            id_b = Buf("id")
            stg = [sb("stg2_%d" % i, [128, 2048], F32) for i in range(2)]
            stg_b = [Buf("stg2_%d" % i) for i in range(2)]
            wo = sb("wo", [128, 8, D], BF16)
            wo_b = Buf("wo")
            wdr = sb("wdr", [128, NF, D], BF16)
            wdr_b = Buf("wdr")
            lnp = [sb("lnp%d" % i, [128, D], F32) for i in range(4)]
            lnp_b = Buf("lnp")
            wgt = [sb("wgt%d" % i, [128, 8, 128], BF16) for i in range(3)]
            wut = [sb("wut%d" % i, [128, 8, 128], BF16) for i in range(3)]
            wgt_b = [Buf("wgt%d" % i) for i in range(3)]
            wut_b = [Buf("wut%d" % i) for i in range(3)]
            mxs = sb("mxs", [128, 8, CH], BF16)
            mxs_b = Buf("mxs")
            xt = [sb("xt%d" % i, [128, D], F32) for i in range(2)]
            xt_b = [Buf("xt%d" % i) for i in range(2)]
            hh = sb("hh", [128, D], F32)
            hh_b = Buf("hh")
            h1 = sb("h1", [128, 4, D], F32)
            h1_b = [Buf("h1_%d" % i) for i in range(4)]
            h1b = sb("h1b", [128, D], BF16)
            h1b_b = Buf("h1b")
            h1T = sb("h1T", [128, 8, CH], BF16)
            h1T_b = Buf("h1T")
            sg = [sb("sg%d" % i, [128, CH], F32) for i in range(2)]
            sg_b = [Buf("sg%d" % i) for i in range(2)]
            aT = sb("aT", [128, NF, CH], BF16)
            aT_b = Buf("aT")
            ot = [sb("ot%d" % i, [128, D], F32) for i in range(2)]
            ot_b = [Buf("ot%d" % i) for i in range(2)]
            st6 = sb("st6", [128, 12], F32)
            mv = sb("mv", [128, 2], F32)
            rs = sb("rs", [128, 1], F32)
            st_b = Buf("stats")

            P.op("sync", lambda e: e.dma_start(out=idf[:, :], in_=cst_d[:, C_ID:C_ID + 128]), writes=[id_b], dbuf=id_b)
            P.op("vector", lambda e: e.tensor_copy(out=idt[:, :], in_=idf[:, :]), reads=[id_b], writes=[id_b])
            for i, src in enumerate((ln1g_d, ln1b_d, ln2g_d, ln2b_d)):
                P.op("sync", lambda e, i=i, src=src: e.dma_start(out=lnp[i][:, :], in_=src[:, :]),
                     writes=[lnp_b], dbuf=lnp_b)
            wout_v = wout_d.rearrange("(c p) n -> p c n", p=128)
            n = 0
            for c0 in range(0, 8, 2):
                s = n % 2
                n += 1
                P.op("sync", lambda e, s=s, c0=c0: e.dma_start(
                    out=stg[s][:, :].rearrange("p (c n) -> p c n", c=2), in_=wout_v[:, c0:c0 + 2, :]),
                    writes=[stg_b[s]], dbuf=stg_b[s])
                P.op("gpsimd", lambda e, s=s, c0=c0: e.tensor_copy(
                    out=wo[:, c0:c0 + 2, :], in_=stg[s][:, :].rearrange("p (c n) -> p c n", c=2)),
                    reads=[stg_b[s]], writes=[wo_b])
            P.op("sync", lambda e: e.dma_start(out=wdr[:, :, :], in_=wds_d.rearrange("f p n -> p f n")),
                 reads=[wsc_b], writes=[wdr_b], dbuf=wdr_b)

            def layer_norm(src, src_b, gi, dst, dst_b):
                for c in range(2):
                    P.op("vector", lambda e, c=c: e.bn_stats(out=st6[:, c * 6:(c + 1) * 6], in_=src[:, c * 512:(c + 1) * 512]),
                         reads=[src_b], writes=[st_b])
                P.op("vector", lambda e: e.bn_aggr(out=mv[:, :], in_=st6[:, :]), reads=[st_b], writes=[st_b])
                P.op("scalar", lambda e: e.activation(out=rs[:, :], in_=mv[:, 1:2], func=AF.Sqrt, bias=LN_EPS, scale=1.0),
                     reads=[st_b], writes=[st_b])
                P.op("vector", lambda e: e.reciprocal(out=rs[:, :], in_=rs[:, :]), reads=[st_b], writes=[st_b])
                P.op("vector", lambda e: e.tensor_scalar(out=src[:, :], in0=src[:, :], scalar1=mv[:, 0:1], scalar2=rs[:, 0:1],
                                                         op0=ALU.subtract, op1=ALU.mult),
                     reads=[src_b, st_b], writes=[src_b])
                P.op("gpsimd", lambda e: e.tensor_tensor(out=src[:, :], in0=src[:, :], in1=lnp[gi][:, :], op=ALU.mult),
                     reads=[src_b, lnp_b], writes=[src_b])
                P.op("gpsimd", lambda e: e.tensor_tensor(out=dst, in0=src[:, :], in1=lnp[gi + 1][:, :], op=ALU.add),
                     reads=[src_b, lnp_b], writes=[dst_b])

            mix_v = mixT_d.rearrange("(c p) t -> p c t", p=128)
            wn = 0
            for k in range(8):
                P.op("sync", lambda e, k=k: e.dma_start(out=mxs[:, :, :], in_=mix_v[:, :, k * CH:(k + 1) * CH]),
                     reads=[mixT_b], writes=[mxs_b], dbuf=mxs_b)
                for t in range(4):
                    tok0 = (k * 4 + t) * 128
                    xs_ = (k * 4 + t) % 2
                    P.op("sync", lambda e, xs_=xs_, tok0=tok0: e.dma_start(out=xt[xs_][:, :], in_=xn_d[tok0:tok0 + 128, :]),
                         writes=[xt_b[xs_]], dbuf=xt_b[xs_])
                    for half in range(2):
                        bk = 2 + half
                        for kc in range(8):
                            P.op("tensor", lambda e, kc=kc, t=t, half=half, bk=bk: e.matmul(
                                bank(bk), lhsT=mxs[:, kc, t * 128:(t + 1) * 128], rhs=wo[:, kc, half * 512:(half + 1) * 512],
                                start=(kc == 0), stop=(kc == 7)),
                                reads=[mxs_b, wo_b], writes=[PB[bk]])
                        P.op("vector", lambda e, xs_=xs_, half=half, bk=bk: e.scalar_tensor_tensor(
                            out=hh[:, half * 512:(half + 1) * 512], in0=xt[xs_][:, half * 512:(half + 1) * 512], scalar=ALPHA,
                            in1=bank(bk), op0=ALU.mult, op1=ALU.add),
                            reads=[xt_b[xs_], PB[bk]], writes=[hh_b])
                    layer_norm(hh, hh_b, 0, h1[:, t, :], h1_b[t])
                    P.op("scalar", lambda e, t=t: e.activation(out=h1b[:, :], in_=h1[:, t, :], func=AF.Copy),
                         reads=[h1_b[t]], writes=[h1b_b])
                    for kc in range(8):
                        P.op("tensor", lambda e, kc=kc: e.transpose(
                            out=pT[:, kc * 128:(kc + 1) * 128], in_=h1b[:, kc * 128:(kc + 1) * 128], identity=idt[:, :]),
                            reads=[h1b_b, id_b], writes=[PB[7]])
                    P.op("vector", lambda e, t=t: e.tensor_copy(
                        out=h1T[:, :, t * 128:(t + 1) * 128], in_=pT[:, :].rearrange("p (c n) -> p c n", c=8)),
                        reads=[PB[7]], writes=[h1T_b])
                for f in range(NF):
                    ws = wn % 3
                    wn += 1
                    P.op("sync", lambda e, ws=ws, f=f: e.dma_start(
                        out=wgt[ws][:, :, :], in_=wgs_d[f, :, :].rearrange("p (c j) -> p c j", c=8)),
                        reads=[wsc_b], writes=[wgt_b[ws]], dbuf=wgt_b[ws])
                    P.op("sync", lambda e, ws=ws, f=f: e.dma_start(
                        out=wut[ws][:, :, :], in_=wus_d[f, :, :].rearrange("p (c j) -> p c j", c=8)),
                        reads=[wsc_b], writes=[wut_b[ws]], dbuf=wut_b[ws])
                    gb = 0 + (f % 2)
                    ub = 4 + (f % 2)
                    for kc in range(8):
                        P.op("tensor", lambda e, kc=kc, ws=ws, gb=gb: e.matmul(
                            bank(gb), lhsT=wgt[ws][:, kc, :], rhs=h1T[:, kc, :], start=(kc == 0), stop=(kc == 7)),
                            reads=[wgt_b[ws], h1T_b], writes=[PB[gb]])
                    for kc in range(8):
                        P.op("tensor", lambda e, kc=kc, ws=ws, ub=ub: e.matmul(
                            bank(ub), lhsT=wut[ws][:, kc, :], rhs=h1T[:, kc, :], start=(kc == 0), stop=(kc == 7)),
                            reads=[wut_b[ws], h1T_b], writes=[PB[ub]])
                    ss = f % 2
                    P.op("scalar", lambda e, ss=ss, gb=gb: e.activation(out=sg[ss][:, :], in_=bank(gb), func=AF.Silu),
                         reads=[PB[gb]], writes=[sg_b[ss]])
                    P.op("vector", lambda e, ss=ss, ub=ub, f=f: e.tensor_tensor(
                        out=aT[:, f, :], in0=sg[ss][:, :], in1=bank(ub), op=ALU.mult),
                        reads=[sg_b[ss], PB[ub]], writes=[aT_b])
                for t in range(4):
                    tok0 = (k * 4 + t) * 128
                    for half in range(2):
                        bk = 2 + half
                        for f in range(NF):
                            P.op("tensor", lambda e, f=f, t=t, half=half, bk=bk: e.matmul(
                                bank(bk), lhsT=aT[:, f, t * 128:(t + 1) * 128], rhs=wdr[:, f, half * 512:(half + 1) * 512],
                                start=(f == 0), stop=(f == NF - 1)),
                                reads=[aT_b, wdr_b], writes=[PB[bk]])
                        P.op("vector", lambda e, t=t, half=half, bk=bk: e.scalar_tensor_tensor(
                            out=hh[:, half * 512:(half + 1) * 512], in0=h1[:, t, half * 512:(half + 1) * 512], scalar=ALPHA,
                            in1=bank(bk), op0=ALU.mult, op1=ALU.add),
                            reads=[h1_b[t], PB[bk]], writes=[hh_b])
                    os_ = (k * 4 + t) % 2
                    layer_norm(hh, hh_b, 2, ot[os_][:, :], ot_b[os_])
                    P.op("sync", lambda e, os_=os_, tok0=tok0: e.dma_start(out=out_d[tok0:tok0 + 128, :], in_=ot[os_][:, :]),
                         reads=[ot_b[os_]], dbuf=ot_b[os_])
            P.flush(final=True)
    return nc


_NC = None


def kernel(x, w_in, g_sb, g_dil, w_out, ln1_g, ln1_b, w_gate, w_up, w_down, ln2_g, ln2_b):
    global _NC
    x = np.asarray(x, np.float32)
    if _NC is None:
        _NC = build_nc()
    nc = _NC
    cst = _consts()
    f = lambda a: np.ascontiguousarray(np.asarray(a, np.float32))
    shared = {
        "cst": cst,
        "w_in": f(w_in[0]), "w_out": f(w_out[0]), "w_gate": f(w_gate[0]), "w_up": f(w_up[0]), "w_down": f(w_down[0]),
        "gsb": f(np.asarray(g_sb[0]).reshape(8, 64).T), "gdil": f(np.asarray(g_dil[0]).reshape(8, 64).T),
        "ln1g": f(np.broadcast_to(np.asarray(ln1_g[0]), (128, D))), "ln1b": f(np.broadcast_to(np.asarray(ln1_b[0]), (128, D))),
        "ln2g": f(np.broadcast_to(np.asarray(ln2_g[0]), (128, D))), "ln2b": f(np.broadcast_to(np.asarray(ln2_b[0]), (128, D))),
    }
    in_maps = []
    for c in range(8):
        b, par = c // 2, c % 2
        xb_ = x[b]
        kvv = np.ones(S, np.float32)
        if par == 0:
            xv = np.concatenate([np.zeros((CH, D), np.float32), xb_[:S - CH]], axis=0)
            kvv[:CH] = 0.0
        else:
            xv = xb_
        m = dict(shared)
        m["xT"] = np.ascontiguousarray(xv.T)
        m["xn"] = np.ascontiguousarray(xb_.reshape(NV, CH, D)[par::2].reshape(S // 2, D))
        m["kv"] = np.ascontiguousarray(kvv.reshape(64, 128).T)
        in_maps.append(m)
    res = run_bass_kernel_spmd(nc, in_maps, core_ids=list(range(8)))
    out = np.empty((NB, S, D), np.float32)
    for c in range(8):
        b, par = c // 2, c % 2
        out[b].reshape(NV, CH, D)[par::2] = np.asarray(res.results[c]["out"], np.float32).reshape(8, CH, D)
    return out
```

```python
import numpy as np
from contextlib import ExitStack

import concourse.bass as bass
import concourse.mybir as mybir
from concourse.bass_utils import run_bass_kernel_spmd

F32 = mybir.dt.float32
BF16 = mybir.dt.bfloat16
AF = mybir.ActivationFunctionType
ALU = mybir.AluOpType

D = 1024
S = 8192
NB = 4
DFF = 2816
NF = DFF // 128
CH = 512
NV = S // CH
ALPHA = 2.0 ** 0.25
LN_EPS = 1e-5
RMS_EPS = 1e-6
MARG = 2048
NEG = -30000.0
INTERLEAVE_DIL = True
ZIP_DIL = False
BIGN = 131072.0

C_ID = 0
C_NTRI = 128
C_NONES = 256
C_MASK = 384
C_NP12 = C_MASK + 4 * 512
C_NP3A = C_NP12 + 1024
C_NP3B = C_NP3A + 1024
C_SID = C_NP3B + 1024
C_MEAN = C_SID + 12 * 128
C_ONES = C_MEAN + 64
C_MEANE = C_ONES + 64
NCST = C_MEANE + 64

ENGS = ("sync", "tensor", "scalar", "vector", "gpsimd")


def _consts():
    c = np.zeros((128, NCST), np.float32)
    j = np.arange(128)[:, None]
    s = np.arange(128)[None, :]
    c[:, C_ID:C_ID + 128] = (j == s)
    c[:, C_NTRI:C_NTRI + 128] = -(j >= s).astype(np.float32)
    c[:, C_NONES:C_NONES + 128] = -1.0
    q = np.arange(512)[None, :]
    for mb in range(4):
        c[:, C_MASK + mb * 512:C_MASK + (mb + 1) * 512] = np.where(mb * 128 + j >= q, NEG, 0.0)

    def npat(G, W, qs_of):
        out = np.zeros((128, 1024), np.float32)
        for g in range(G):
            for half in range(2):
                for qi in range(W):
                    qs = qs_of(qi)
                    col = g * 2 * W + half * W + qi
                    if half == 1:
                        n = qs - np.arange(128)
                    else:
                        n = qs - np.arange(128) + 128
                    ok = (n >= 0) & (n <= 128)
                    out[:, col] = np.where(ok, n, BIGN)
        return out

    c[:, C_NP12:C_NP12 + 1024] = npat(4, 128, lambda qi: qi)
    c[:, C_NP3A:C_NP3A + 1024] = npat(16, 32, lambda qi: 32 + qi)
    c[:, C_NP3B:C_NP3B + 1024] = npat(16, 32, lambda qi: 96 + qi)
    for e in range(-8, 4):
        c[:, C_SID + (e + 8) * 128:C_SID + (e + 9) * 128] = -(2.0 ** e) * (j == s)
    c[:, C_MEAN:C_MEAN + 64] = 1.0 / 64.0
    c[:, C_ONES:C_ONES + 64] = 1.0
    c[0:64, C_MEANE:C_MEANE + 64] = 1.0 / 64.0
    c[64, C_MEANE:C_MEANE + 64] = RMS_EPS
    return c


class Buf:
    __slots__ = ("name", "lw", "rd", "rdd", "sem", "dcnt", "uid")
    _n = [0]

    def __init__(self, name):
        Buf._n[0] += 1
        self.uid = Buf._n[0]
        self.name = name
        self.lw = None
        self.rd = {}
        self.rdd = []
        self.sem = None
        self.dcnt = 0


class Op:
    __slots__ = ("eng", "fn", "deps", "needed", "val", "dbuf", "flushed")

    def __init__(self, eng, fn, dbuf):
        self.eng = eng
        self.fn = fn
        self.deps = []
        self.needed = False
        self.val = None
        self.dbuf = dbuf
        self.flushed = False


class Prog:
    def __init__(self, nc, esems, dsems):
        self.nc = nc
        self.esems = esems
        self.dsems = list(dsems)
        self.ops = {e: [] for e in ENGS}
        self.cnt = {e: 0 for e in ENGS}
        self.waited = {e: {} for e in ENGS}
        self.dma_ops = []

    def op(self, eng, fn, reads=(), writes=(), dbuf=None):
        o = Op(eng, fn, dbuf)
        deps = {}
        for b in reads:
            if b.lw is not None:
                deps[id(b.lw)] = b.lw
        for b in writes:
            if b.lw is not None:
                deps[id(b.lw)] = b.lw
            for r in b.rd.values():
                deps[id(r)] = r
            for r in b.rdd:
                deps[id(r)] = r
        for d in deps.values():
            if d is o:
                continue
            if d.dbuf is None and d.eng == "tensor" and eng == "tensor":
                continue
            o.deps.append(d)
            d.needed = True
        for b in reads:
            if dbuf is not None:
                b.rdd.append(o)
            else:
                b.rd[eng] = o
        for b in writes:
            b.lw = o
            b.rd = {}
            b.rdd = []
        if dbuf is not None:
            if dbuf.sem is None:
                dbuf.sem = self.dsems.pop()
            dbuf.dcnt += 16
            o.val = dbuf.dcnt
            self.dma_ops.append(o)
        self.ops[eng].append(o)
        return o

    def flush(self, final=False):
        nc = self.nc
        for e in ENGS:
            for o in self.ops[e]:
                if o.dbuf is None and o.needed:
                    self.cnt[e] += 1
                    o.val = self.cnt[e]
        pending_dma = [o for o in self.dma_ops]
        self.dma_ops = []
        with nc.Block() as block:
            for e in ENGS:
                ops = self.ops[e]
                if not ops and not (e == "gpsimd"):
                    continue

                def body(eng, e=e, ops=ops):
                    waited = self.waited[e]
                    for o in ops:
                        for d in o.deps:
                            if d.dbuf is not None:
                                key = ("d", d.dbuf.uid)
                                sem = d.dbuf.sem
                            else:
                                if d.val is None:
                                    assert d.flushed
                                    continue
                                key = ("e", d.eng)
                                sem = self.esems[d.eng]
                            if waited.get(key, 0) >= d.val:
                                continue
                            waited[key] = d.val
                            eng.wait_ge(sem, d.val)
                        ins = o.fn(eng)
                        if o.dbuf is not None:
                            ins.then_inc(o.dbuf.sem, 16)
                        elif o.needed:
                            ins.then_inc(self.esems[e], 1)
                    if e == "gpsimd":
                        for o in pending_dma:
                            key = ("d", o.dbuf.uid)
                            if waited.get(key, 0) >= o.val:
                                continue
                            waited[key] = o.val
                            eng.wait_ge(o.dbuf.sem, o.val)

                getattr(block, e)(body)
        for e in ENGS:
            for o in self.ops[e]:
                o.flushed = True
            self.ops[e] = []


def build_nc():
    nc = bass.Bass("TRN2", target_bir_lowering=False)

    def din(name, shape, dt=F32):
        return nc.dram_tensor(name, list(shape), dt, kind="ExternalInput").ap()

    xT_d = din("xT", [D, S])
    xn_d = din("xn", [S // 2, D])
    kv_d = din("kv", [128, 64])
    cst_d = din("cst", [128, NCST])
    win_d = din("w_in", [D, 3 * D])
    wout_d = din("w_out", [D, D])
    wg_d = din("w_gate", [D, DFF])
    wu_d = din("w_up", [D, DFF])
    wd_d = din("w_down", [DFF, D])
    gsb_d = din("gsb", [64, 8])
    gdil_d = din("gdil", [64, 8])
    ln1g_d = din("ln1g", [128, D])
    ln1b_d = din("ln1b", [128, D])
    ln2g_d = din("ln2g", [128, D])
    ln2b_d = din("ln2b", [128, D])
    out_d = nc.dram_tensor("out", [S // 2, D], F32, kind="ExternalOutput").ap()
    vscr_d = nc.dram_tensor("vscr", [MARG + S, 130], BF16).ap()
    mixT_d = nc.dram_tensor("mixT", [D, S // 2], BF16).ap()
    wgs_d = nc.dram_tensor("wgs", [NF, 128, 1024], BF16).ap()
    wus_d = nc.dram_tensor("wus", [NF, 128, 1024], BF16).ap()
    wds_d = nc.dram_tensor("wds", [NF, 128, 1024], BF16).ap()

    with ExitStack() as top:
        esems = {e: top.enter_context(nc.semaphore("es_" + e)) for e in ENGS}
        dsems = [top.enter_context(nc.semaphore("ds%d" % i)) for i in range(96)]
        P = Prog(nc, esems, dsems)

        PB = [Buf("bank%d" % i) for i in range(8)]

        vscr_b = Buf("vscr")
        mixT_b = Buf("mixT")
        wsc_b = Buf("wscr")

        with ExitStack() as ph:
            def sb(name, shape, dt):
                return ph.enter_context(nc.sbuf_tensor("a_" + name, list(shape), dt))

            pbig = ph.enter_context(nc.psum_tensor("a_pbig", [128, 1024], F32))
            pbigB = ph.enter_context(nc.psum_tensor("a_pbigB", [128, 1024], F32))
            pbk = [ph.enter_context(nc.psum_tensor("a_pb%d" % i, [128, 512], F32)) for i in range(4, 8)]

            def bank(i):
                if i < 2:
                    return pbig[:, i * 512:(i + 1) * 512]
                if i < 4:
                    return pbigB[:, (i - 2) * 512:(i - 1) * 512]
                return pbk[i - 4][:, :]

            cst = sb("cst", [128, NCST], BF16)
            cst_b = Buf("cst")
            stg = [sb("stg%d" % i, [128, 4096], F32) for i in range(2)]
            stg_b = [Buf("stg%d" % i) for i in range(2)]
            xb = [sb("xb%d" % i, [128, 8, CH], BF16) for i in range(2)]
            xb_b = [Buf("xb%d" % i) for i in range(2)]
            wp = sb("wp", [128, 8, 768], BF16)
            wp_b = Buf("wp")
            KaT = sb("KaT", [128, S], BF16)
            KaT_c = [Buf("KaT%d" % i) for i in range(NV)]
            Va = sb("Va", [128, 64, 128], BF16)
            Va_c = [Buf("Va%d" % i) for i in range(NV)]
            KbT = sb("KbT", [128, MARG + S], BF16)
            KbT_c = [Buf("KbT%d" % i) for i in range(NV)]
            QaT = [sb("QaT%d" % i, [128, CH], BF16) for i in range(2)]
            QaT_b = [Buf("QaT%d" % i) for i in range(2)]
            QbT = [sb("QbT%d" % i, [128, CH], BF16) for i in range(2)]
            QbT_b = [Buf("QbT%d" % i) for i in range(2)]
            kvf = sb("kvf", [128, 64], F32)
            kvb = sb("kvb", [128, 64], BF16)
            kv_b = Buf("kv")
            vst = [sb("vst%d" % i, [128, 4, 130], BF16) for i in range(2)]
            vst_b = [Buf("vst%d" % i) for i in range(2)]
            vb1 = sb("vb1", [128, 5, 130], BF16)
            vb4 = sb("vb4", [128, 8, 130], BF16)
            vb16 = sb("vb16", [128, 32, 130], BF16)
            vb1_b, vb4_b, vb16_b = Buf("vb1"), Buf("vb4"), Buf("vb16")
            e_t = [sb("e_t%d" % i, [128, 2 * CH], F32) for i in range(2)]
            e_b = [Buf("e_t%d" % i) for i in range(2)]
            sp_t = [sb("sp_t%d" % i, [128, 2 * CH], BF16) for i in range(2)]
            sp_b = [Buf("sp_t%d" % i) for i in range(2)]
            w_t = [sb("w_t%d" % i, [128, 2 * CH], BF16) for i in range(2)]
            w_b = [Buf("w_t%d" % i) for i in range(2)]
            Rt = [sb("R%d" % i, [128, 2 * CH], BF16) for i in range(3)]
            R_b = [Buf("R%d" % i) for i in range(3)]
            osb = [sb("osb%d" % i, [64, CH], F32) for i in range(2)]
            osb_b = [Buf("osb%d" % i) for i in range(2)]
            dtc = [sb("dt%d" % i, [128, CH], F32) for i in range(2)]
            dtc_b = [Buf("dt%d" % i) for i in range(2)]
            etc_ = [[sb("etc%d_%d" % (i, j), [128, CH], BF16) for j in range(2)] for i in range(2)]
            etc_b = [[Buf("etc%d_%d" % (i, j)) for j in range(2)] for i in range(2)]
            sqc = [sb("sqc%d" % i, [128, CH], BF16) for i in range(2)]
            sqc_b = [Buf("sqc%d" % i) for i in range(2)]
            lnvc = [sb("lnvc%d" % i, [64, CH], F32) for i in range(2)]
            lnvc_b = [Buf("lnvc%d" % i) for i in range(2)]
            rstdc = [sb("rstdc%d" % i, [64, CH], F32) for i in range(2)]
            rstdc_b = [Buf("rstdc%d" % i) for i in range(2)]
            mixt = [sb("mixt%d" % i, [64, CH], BF16) for i in range(2)]
            mixt_b = [Buf("mixt%d" % i) for i in range(2)]
            gsb_t = sb("gsb_t", [64, 8], F32)
            gdil_t = sb("gdil_t", [64, 8], F32)
            g_b = Buf("g")

            for i in range(2):
                c0 = i * 3616
                P.op("sync", lambda e, i=i, c0=c0: e.dma_start(out=stg[i][:, 0:3616], in_=cst_d[:, c0:c0 + 3616]),
                     writes=[stg_b[i]], dbuf=stg_b[i])
                P.op("vector", lambda e, i=i, c0=c0: e.tensor_copy(out=cst[:, c0:c0 + 3616], in_=stg[i][:, 0:3616]),
                     reads=[stg_b[i]], writes=[cst_b])
            P.op("sync", lambda e: e.dma_start(out=kvf[:, :], in_=kv_d[:, :]), writes=[kv_b], dbuf=kv_b)
            P.op("vector", lambda e: e.tensor_copy(out=kvb[:, :], in_=kvf[:, :]), reads=[kv_b], writes=[kv_b])
            P.op("sync", lambda e: e.dma_start(out=gsb_t[:, :], in_=gsb_d[:, :]), writes=[g_b], dbuf=g_b)
            P.op("sync", lambda e: e.dma_start(out=gdil_t[:, :], in_=gdil_d[:, :]), writes=[g_b], dbuf=g_b)
            P.op("gpsimd", lambda e: e.memset(KbT[:, :], 0.0), writes=list(KbT_c))
            P.op("gpsimd", lambda e: e.memset(vst[0][:, :, :], 0.0), writes=[vst_b[0]])
            for r0 in range(0, MARG + S, 512):
                P.op("sync", lambda e, r0=r0: e.dma_start(
                    out=vscr_d[r0:r0 + 512, :].rearrange("(t p) c -> p t c", p=128), in_=vst[0][:, :, :]),
                    reads=[vst_b[0]], writes=[vscr_b], dbuf=vst_b[0])

            xT_v = xT_d.rearrange("(c p) t -> p c t", p=128)
            win_v = win_d.rearrange("(c p) n -> p c n", p=128)
            ld_n = [0]

            tb = [sb("tb%d" % i, [128, DFF], BF16) for i in range(2)]
            tb_b = [Buf("tb%d" % i) for i in range(2)]
            w0 = []
            n0 = 0
            for (src, dst) in ((wg_d, wgs_d), (wu_d, wus_d)):
                dview = dst.rearrange("f p (c j) -> p f c j", c=8)
                for kc in range(8):
                    sl0 = n0 % 2
                    n0 += 1

                    def ld(sl0=sl0, src=src, kc=kc):
                        P.op("sync", lambda e: e.dma_start(out=stg[sl0][:, 0:DFF], in_=src[kc * 128:(kc + 1) * 128, :]),
                             writes=[stg_b[sl0]], dbuf=stg_b[sl0])

                    def cs_(sl0=sl0, dview=dview, kc=kc):
                        P.op("vector", lambda e: e.tensor_copy(out=tb[sl0][:, :], in_=stg[sl0][:, 0:DFF]),
                             reads=[stg_b[sl0]], writes=[tb_b[sl0]])
                        P.op("sync", lambda e: e.dma_start(
                            out=dview[:, :, kc, :], in_=tb[sl0][:, :].rearrange("p (f j) -> p f j", j=128)),
                            reads=[tb_b[sl0]], writes=[wsc_b], dbuf=tb_b[sl0])
                    w0.append((ld, cs_))
            for f0 in range(0, NF, 2):
                sl0 = n0 % 2
                n0 += 1

                def ld(sl0=sl0, f0=f0):
                    P.op("sync", lambda e: e.dma_start(
                        out=stg[sl0][:, 0:2048].rearrange("p (f n) -> p f n", f=2),
                        in_=wd_d[f0 * 128:(f0 + 2) * 128, :].rearrange("(f p) n -> p f n", p=128)),
                        writes=[stg_b[sl0]], dbuf=stg_b[sl0])

                def cs_(sl0=sl0, f0=f0):
                    P.op("vector", lambda e: e.tensor_copy(out=tb[sl0][:, 0:2048], in_=stg[sl0][:, 0:2048]),
                         reads=[stg_b[sl0]], writes=[tb_b[sl0]])
                    P.op("sync", lambda e: e.dma_start(
                        out=wds_d[f0:f0 + 2, :, :].rearrange("f p n -> p f n"),
                        in_=tb[sl0][:, 0:2048].rearrange("p (f n) -> p f n", f=2)),
                        reads=[tb_b[sl0]], writes=[wsc_b], dbuf=tb_b[sl0])
                w0.append((ld, cs_))
            def p0_group(pieces):
                out = []
                for i in range(len(pieces) + 1):
                    if i < len(pieces):
                        out.append(pieces[i][0])
                    if i >= 1:
                        out.append(pieces[i - 1][1])
                return out

            def norm_thunks(src, src_b, nrow, h_glob, is_dil, kown, gtile, ch=0, mbk=6):
                th = []
                lcol = C_MEANE if nrow == 65 else C_MEAN

                def t1():
                    P.op("scalar", lambda e: e.activation(out=sqc[ch][0:nrow, :], in_=src[0:nrow, :], func=AF.Square),
                         reads=[src_b], writes=[sqc_b[ch]])
                    P.op("tensor", lambda e: e.matmul(bank(mbk)[0:64, :], lhsT=cst[0:nrow, lcol:lcol + 64], rhs=sqc[ch][0:nrow, :],
                                                      start=True, stop=True),
                         reads=[sqc_b[ch], cst_b], writes=[PB[mbk]])

                def t2():
                    if nrow == 65:
                        P.op("scalar", lambda e: e.activation(out=lnvc[ch][:, :], in_=bank(mbk)[0:64, :], func=AF.Ln),
                             reads=[PB[mbk]], writes=[lnvc_b[ch]])
                    else:
                        P.op("scalar", lambda e: e.activation(out=lnvc[ch][:, :], in_=bank(mbk)[0:64, :], func=AF.Ln,
                                                              bias=RMS_EPS, scale=1.0),
                             reads=[PB[mbk]], writes=[lnvc_b[ch]])
                    P.op("scalar", lambda e: e.activation(out=rstdc[ch][:, :], in_=lnvc[ch][:, :], func=AF.Exp, scale=-0.5),
                         reads=[lnvc_b[ch]], writes=[rstdc_b[ch]])

                def t3():
                    ms = ld_n[0] % 2
                    ld_n[0] += 1
                    P.op("vector", lambda e, ms=ms: e.scalar_tensor_tensor(
                        out=mixt[ms][:, :], in0=src[0:64, :], scalar=gtile[0:64, h_glob:h_glob + 1], in1=rstdc[ch][:, :],
                        op0=ALU.mult, op1=ALU.mult),
                        reads=[src_b, rstdc_b[ch], g_b], writes=[mixt_b[ms]])
                    row0 = (512 if is_dil else 0) + h_glob * 64
                    P.op("sync", lambda e, ms=ms, row0=row0: e.dma_start(
                        out=mixT_d[row0:row0 + 64, kown * CH:(kown + 1) * CH], in_=mixt[ms][:, :]),
                        reads=[mixt_b[ms]], writes=[mixT_b], dbuf=mixt_b[ms])
                return [t1, t2, t3]

            def dil_thunks(v, p, QB, QB_b, kown):
                chains = []
                for hd in range(2):
                    th = []
                    sbk, abk = (6, 7) if (hd == 0 or INTERLEAVE_DIL) else (0, 1)
                    dt_, dt_b = dtc[hd], dtc_b[hd]
                    et, et_b = etc_[hd], etc_b[hd]
                    r = slice(hd * 64, hd * 64 + 64)
                    hg = 2 * p + hd
                    for pi in range(3):
                        ex = -(hg + 1) + 2 * pi
                        sid0 = C_SID + (ex + 8) * 128
                        np0 = C_NP12 if pi < 2 else (C_NP3A if v % 4 == 1 else C_NP3B)
                        G = 4 if pi < 2 else 16
                        W = 128 if pi < 2 else 32
                        for hb in range(2):
                            groups = list(range(hb * G // 2, (hb + 1) * G // 2))

                            def t1(hb=hb, groups=groups, sid0=sid0, np0=np0, pi=pi, W=W, r=r, sbk=sbk):
                                P.op("tensor", lambda e: e.matmul(
                                    bank(sbk), lhsT=cst[:, sid0:sid0 + 128], rhs=cst[:, np0 + hb * 512:np0 + (hb + 1) * 512],
                                    start=True, stop=False, skip_group_check=True),
                                    reads=[cst_b], writes=[PB[sbk]])
                                for g in groups:
                                    for half in range(2):
                                        col0 = g * 2 * W + half * W - hb * 512
                                        if pi == 0:
                                            blk = 4 * v + g - 1 + half
                                            k0 = MARG + blk * 128
                                            kap = KbT[r, k0:k0 + 128]
                                            qap = QB[r, g * 128:(g + 1) * 128]
                                            kbufs = [KbT_c[blk // 4]]
                                        elif pi == 1:
                                            k0 = MARG + (v - 1 + half) * 512 + g
                                            kap = KbT[r, k0:k0 + 509:4]
                                            qap = QB[r, g:g + 509:4]
                                            kbufs = [KbT_c[v - 1 + half]]
                                        else:
                                            U = v // 4 - 1 + half
                                            k0 = MARG + U * 2048 + g
                                            kap = KbT[r, k0:k0 + 2033:16]
                                            qap = QB[r, g:g + 497:16]
                                            kbufs = [KbT_c[c] for c in range(4 * U, 4 * U + 4) if 0 <= c <= v]
                                        P.op("tensor", lambda e, kap=kap, qap=qap, col0=col0: e.matmul(
                                            bank(sbk)[:, col0:col0 + W], lhsT=kap, rhs=qap, start=False, stop=False,
                                            skip_group_check=True),
                                            reads=kbufs + [QB_b], writes=[PB[sbk]])

                            def t2(hb=hb, sbk=sbk, et=et, et_b=et_b):
                                P.op("scalar", lambda e: e.activation(out=et[hb][:, :], in_=bank(sbk), func=AF.Exp),
                                     reads=[PB[sbk]], writes=[et_b[hb]])

                            def t3(hb=hb, groups=groups, pi=pi, W=W, hd=hd, abk=abk, et=et, et_b=et_b):
                                first = (hb == 0)
                                for g in groups:
                                    for half in range(2):
                                        col0 = g * 2 * W + half * W - hb * 512
                                        if pi == 0:
                                            vt = vb1[:, g + half, hd * 65:(hd + 1) * 65]
                                            vbuf = vb1_b
                                        elif pi == 1:
                                            vt = vb4[:, half * 4 + g, hd * 65:(hd + 1) * 65]
                                            vbuf = vb4_b
                                        else:
                                            vt = vb16[:, half * 16 + g, hd * 65:(hd + 1) * 65]
                                            vbuf = vb16_b
                                        P.op("tensor", lambda e, vt=vt, col0=col0, g=g, first=first: e.matmul(
                                            bank(abk)[0:65, g * W:(g + 1) * W], lhsT=vt, rhs=et[hb][:, col0:col0 + W],
                                            start=first, stop=False, skip_group_check=True),
                                            reads=[vbuf, et_b[hb]], writes=[PB[abk]])
                                        first = False
                            th += [t1, t2, t3]

                        def t4(pi=pi, abk=abk, dt_=dt_, dt_b=dt_b):
                            if pi == 0:
                                P.op("vector", lambda e: e.tensor_copy(out=dt_[0:65, :], in_=bank(abk)[0:65, :]),
                                     reads=[PB[abk]], writes=[dt_b])
                            else:
                                rr = 4 if pi == 1 else 16
                                P.op("vector", lambda e: e.tensor_tensor(
                                    out=dt_[0:65, :].rearrange("p (u r) -> p u r", r=rr),
                                    in0=dt_[0:65, :].rearrange("p (u r) -> p u r", r=rr),
                                    in1=bank(abk)[0:65, :].rearrange("p (r u) -> p u r", r=rr), op=ALU.add),
                                    reads=[PB[abk], dt_b], writes=[dt_b])
                        th.append(t4)
                    th += norm_thunks(dt_, dt_b, 65, hg, True, kown, gdil_t, ch=hd, mbk=sbk)
                    chains.append(th)
                if not ZIP_DIL:
                    return chains[0] + chains[1]
                out = []
                for i in range(max(len(c) for c in chains)):
                    for c in chains:
                        if i < len(c):
                            out.append(c[i])
                return out

            def load_x(v):
                s = v % 2
                P.op("sync", lambda e, s=s, v=v: e.dma_start(
                    out=stg[s][:, :].rearrange("p (c t) -> p c t", c=8), in_=xT_v[:, :, v * CH:(v + 1) * CH]),
                    writes=[stg_b[s]], dbuf=stg_b[s])
                P.op("vector", lambda e, s=s: e.tensor_copy(
                    out=xb[s][:, :, :], in_=stg[s][:, :].rearrange("p (c t) -> p c t", c=8)),
                    reads=[stg_b[s]], writes=[xb_b[s]])

            pj_n = [0]

            def proj_thunks(v):
                s = v % 2
                own = (v % 2 == 1)
                qs = ((v - 1) // 2) % 2
                th = []

                def fm(j, evac):
                    bk = 6 + pj_n[0] % 2
                    pj_n[0] += 1
                    for k0 in (0, 4):
                        def t_(k0=k0, bk=bk):
                            for kc in range(k0, k0 + 4):
                                P.op("tensor", lambda e, kc=kc: e.matmul(
                                    bank(bk), lhsT=wp[:, kc, j * 128:(j + 1) * 128], rhs=xb[s][:, kc, :],
                                    start=(kc == 0), stop=(kc == 7)),
                                    reads=[wp_b, xb_b[s]], writes=[PB[bk]])
                        th.append(t_)
                    th.append(lambda bk=bk: evac(bk))

                def tm(j, evac):
                    bk = 6 + pj_n[0] % 2
                    pj_n[0] += 1
                    for t in range(4):
                        def t_(t=t, bk=bk):
                            for kc in range(8):
                                P.op("tensor", lambda e, kc=kc: e.matmul(
                                    bank(bk)[:, t * 128:(t + 1) * 128], lhsT=xb[s][:, kc, t * 128:(t + 1) * 128],
                                    rhs=wp[:, kc, j * 128:(j + 1) * 128], start=(kc == 0), stop=(kc == 7)),
                                    reads=[wp_b, xb_b[s]], writes=[PB[bk]])
                        th.append(t_)
                    th.append(lambda bk=bk: evac(bk))

                fm(1, lambda bk: P.op("vector", lambda e: e.tensor_copy(out=KaT[:, v * CH:(v + 1) * CH], in_=bank(bk)),
                                      reads=[PB[bk]], writes=[KaT_c[v]]))
                fm(4, lambda bk: P.op("vector", lambda e: e.tensor_copy(
                    out=KbT[:, MARG + v * CH:MARG + (v + 1) * CH], in_=bank(bk)), reads=[PB[bk]], writes=[KbT_c[v]]))
                tm(2, lambda bk: P.op("vector", lambda e: e.tensor_copy(
                    out=Va[:, v * 4:(v + 1) * 4, :], in_=bank(bk).rearrange("p (t n) -> p t n", t=4)),
                    reads=[PB[bk]], writes=[Va_c[v]]))

                def evac_vb(bk):
                    P.op("vector", lambda e: e.tensor_copy(
                        out=vst[s][:, :, :].rearrange("p t (h c) -> p t h c", h=2)[:, :, :, 0:64],
                        in_=bank(bk).rearrange("p (t h c) -> p t h c", t=4, h=2)),
                        reads=[PB[bk]], writes=[vst_b[s]])
                    for hc in (64, 129):
                        P.op("vector", lambda e, hc=hc: e.tensor_copy(
                            out=vst[s][:, :, hc:hc + 1], in_=kvb[:, v * 4:(v + 1) * 4].rearrange("p (t o) -> p t o", o=1)),
                            reads=[kv_b], writes=[vst_b[s]])
                    r0 = MARG + v * CH
                    P.op("sync", lambda e: e.dma_start(
                        out=vscr_d[r0:r0 + 512, :].rearrange("(t p) c -> p t c", p=128), in_=vst[s][:, :, :]),
                        reads=[vst_b[s]], writes=[vscr_b], dbuf=vst_b[s])
                tm(5, evac_vb)
                if own:
                    fm(0, lambda bk: P.op("vector", lambda e: e.tensor_scalar(
                        out=QaT[qs][:, :], in0=bank(bk), scalar1=0.125, scalar2=None, op0=ALU.mult),
                        reads=[PB[bk]], writes=[QaT_b[qs]]))
                    fm(3, lambda bk: P.op("vector", lambda e: e.tensor_scalar(
                        out=QbT[qs][:, :], in0=bank(bk), scalar1=0.125, scalar2=None, op0=ALU.mult),
                        reads=[PB[bk]], writes=[QbT_b[qs]]))
                return th

            def pass_prologue(p):
                th = []
                for j, base in enumerate((0, 512, 1024, 1536, 2048, 2560)):
                    def t_(j=j, base=base):
                        s = ld_n[0] % 2
                        ld_n[0] += 1
                        c0 = base + p * 128
                        P.op("sync", lambda e: e.dma_start(
                            out=stg[s][:, 0:1024].rearrange("p (c n) -> p c n", c=8), in_=win_v[:, :, c0:c0 + 128]),
                            writes=[stg_b[s]], dbuf=stg_b[s])
                        P.op("vector", lambda e: e.tensor_copy(
                            out=wp[:, :, j * 128:(j + 1) * 128], in_=stg[s][:, 0:1024].rearrange("p (c n) -> p c n", c=8)),
                            reads=[stg_b[s]], writes=[wp_b])
                    th.append(t_)
                th.append(lambda: load_x(0))
                th.append(lambda: load_x(1))
                return th

            for p in range(4):
                if p == 0:
                    for t_ in pass_prologue(0):
                        t_()
                for t_ in proj_thunks(0) + proj_thunks(1):
                    t_()
                carry = []
                for v in range(1, NV, 2):
                    kown = (v - 1) // 2
                    qs = kown % 2
                    QA, QB = QaT[qs], QbT[qs]
                    QA_b, QB_b = QaT_b[qs], QbT_b[qs]
                    b1 = MARG + (4 * v - 1) * 128
                    P.op("sync", lambda e, b1=b1: e.dma_start(
                        out=vb1[:, :, :], in_=vscr_d[b1:b1 + 640, :].rearrange("(t p) c -> p t c", p=128)),
                        reads=[vscr_b], writes=[vb1_b], dbuf=vb1_b)
                    for half in range(2):
                        b4 = MARG + (v - 1 + half) * 512
                        P.op("sync", lambda e, b4=b4, half=half: e.dma_start(
                            out=vb4[:, half * 4:(half + 1) * 4, :],
                            in_=vscr_d[b4:b4 + 512, :].rearrange("(s r) c -> s r c", r=4)),
                            reads=[vscr_b], writes=[vb4_b], dbuf=vb4_b)
                        b16 = MARG + (v // 4 - 1 + half) * 2048
                        P.op("sync", lambda e, b16=b16, half=half: e.dma_start(
                            out=vb16[:, half * 16:(half + 1) * 16, :],
                            in_=vscr_d[b16:b16 + 2048, :].rearrange("(s r) c -> s r c", r=16)),
                            reads=[vscr_b], writes=[vb16_b], dbuf=vb16_b)
                    side = carry
                    carry = []
                    if v + 2 < NV:
                        load_x(v + 1)
                        load_x(v + 2)
                        side = side + proj_thunks(v + 1) + proj_thunks(v + 2)
                    if v + 2 >= NV and p < 3:
                        side = side + pass_prologue(p + 1)
                    if p == 0:
                        take = len(w0) if v + 2 >= NV else min(len(w0), 4)
                        side = side + p0_group(w0[:take])
                        w0 = w0[take:]
                    side_b = dil_thunks(v, p, QB, QB_b, kown)
                    if INTERLEAVE_DIL:
                        side = side + side_b
                        side_b = []

                    nkb = 4 * (v + 1)

                    def st_A(i, b0, last):
                        kb = nkb - 1 - i
                        diag = kb >= 4 * v
                        mb = kb - 4 * v
                        ks = slice(kb * 128, (kb + 1) * 128)
                        for hd in range(2):
                            r = slice(hd * 64, hd * 64 + 64)
                            bk = b0 + hd
                            P.op("tensor", lambda e, bk=bk, r=r, ks=ks, diag=diag, last=last, QA=QA: e.matmul(
                                bank(bk), lhsT=KaT[r, ks], rhs=QA[r, :], start=True, stop=(last and not diag)),
                                reads=[KaT_c[kb // 4], QA_b], writes=[PB[bk]])
                            if diag:
                                P.op("tensor", lambda e, bk=bk, mb=mb, last=last: e.matmul(
                                    bank(bk), lhsT=cst[:, C_ID:C_ID + 128],
                                    rhs=cst[:, C_MASK + mb * 512:C_MASK + (mb + 1) * 512], start=False, stop=last),
                                    reads=[cst_b], writes=[PB[bk]])

                    def st_act1(i):
                        sl = i % 2
                        P.op("scalar", lambda e, sl=sl: e.activation(out=e_t[sl][:, :], in_=pbig[:, :], func=AF.Exp),
                             reads=[PB[0], PB[1]], writes=[e_b[sl]])
                        P.op("scalar", lambda e, sl=sl: e.activation(out=sp_t[sl][:, :], in_=e_t[sl][:, :], func=AF.Ln,
                                                                     bias=1.0, scale=1.0),
                             reads=[e_b[sl]], writes=[sp_b[sl]])

                    def st_R(i):
                        sl = i % 2
                        if i == 0:
                            P.op("vector", lambda e, sl=sl: e.tensor_copy(out=Rt[1][:, :], in_=sp_t[sl][:, :]),
                                 reads=[sp_b[sl]], writes=[R_b[1]])
                        elif i < nkb - 1:
                            P.op("vector", lambda e, sl=sl, i=i: e.tensor_tensor(
                                out=Rt[(i + 1) % 3][:, :], in0=Rt[i % 3][:, :], in1=sp_t[sl][:, :], op=ALU.add),
                                reads=[sp_b[sl], R_b[i % 3]], writes=[R_b[(i + 1) % 3]])

                    def st_B(i):
                        sl = i % 2
                        st_A(i, 2, False)
                        for hd in range(2):
                            cs = slice(hd * CH, (hd + 1) * CH)
                            P.op("tensor", lambda e, sl=sl, i=i, hd=hd, cs=cs: e.matmul(
                                bank(2 + hd), lhsT=cst[:, C_NTRI:C_NTRI + 128], rhs=sp_t[sl][:, cs],
                                start=False, stop=(i == 0)),
                                reads=[cst_b, sp_b[sl]], writes=[PB[2 + hd]])
                            if i > 0:
                                P.op("tensor", lambda e, sl=sl, i=i, hd=hd, cs=cs: e.matmul(
                                    bank(2 + hd), lhsT=cst[:, C_NONES:C_NONES + 128], rhs=Rt[i % 3][:, cs],
                                    start=False, stop=True),
                                    reads=[cst_b, R_b[i % 3]], writes=[PB[2 + hd]])

                    def st_act2(i):
                        sl = i % 2
                        P.op("scalar", lambda e, sl=sl: e.activation(out=w_t[sl][:, :], in_=pbigB[:, :], func=AF.Exp),
                             reads=[PB[2], PB[3]], writes=[w_b[sl]])

                    def st_AV(i):
                        sl = i % 2
                        kb = nkb - 1 - i
                        for hd in range(2):
                            cs = slice(hd * CH, (hd + 1) * CH)
                            P.op("tensor", lambda e, sl=sl, kb=kb, hd=hd, i=i, cs=cs, nkb=nkb: e.matmul(
                                bank(4 + hd)[0:64, :], lhsT=Va[:, kb, hd * 64:(hd + 1) * 64], rhs=w_t[sl][:, cs],
                                start=(i == 0), stop=(i == nkb - 1)),
                                reads=[Va_c[kb // 4], w_b[sl]], writes=[PB[4 + hd]])

                    nside = len(side)
                    per = -(-nside // max(1, nkb - 2)) if nside else 0
                    si = 0
                    for st in range(nkb + 2):
                        if st < nkb:
                            st_A(st, 0, True)
                            st_act1(st)
                            st_R(st)
                        if 0 <= st - 1 < nkb:
                            st_B(st - 1)
                            st_act2(st - 1)
                        if 0 <= st - 2 < nkb:
                            st_AV(st - 2)
                        for _ in range(per):
                            if si < nside:
                                side[si]()
                                si += 1
                    while si < nside:
                        side[si]()
                        si += 1
                    for t_ in side_b:
                        t_()
                    for hd in range(2):
                        ob = 4 + hd
                        P.op("vector", lambda e, ob=ob, hd=hd: e.tensor_copy(out=osb[hd][:, :], in_=bank(ob)[0:64, :]),
                             reads=[PB[ob]], writes=[osb_b[hd]])
                        carry += norm_thunks(osb[hd], osb_b[hd], 64, 2 * p + hd, False, kown, gsb_t, ch=hd, mbk=6 + hd)
                    if v + 2 >= NV:
                        for t_ in carry:
                            t_()
                        carry = []
            P.flush()

        with ExitStack() as ph:
            def sb(name, shape, dt):
                return ph.enter_context(nc.sbuf_tensor("b_" + name, list(shape), dt))

            pbig = ph.enter_context(nc.psum_tensor("b_pbig", [128, 1024], F32))
            pbk = [ph.enter_context(nc.psum_tensor("b_pb%d" % i, [128, 512], F32)) for i in range(2, 7)]
            pT = ph.enter_context(nc.psum_tensor("b_pT", [128, 1024], BF16))

            def bank(i):
                if i < 2:
                    return pbig[:, i * 512:(i + 1) * 512]
                return pbk[i - 2][:, :]

            idt = sb("idt", [128, 128], BF16)
            idf = sb("idf", [128, 128], F32)
            id_b = Buf("id")
            stg = [sb("stg2_%d" % i, [128, 2048], F32) for i in range(2)]
            stg_b = [Buf("stg2_%d" % i) for i in range(2)]
            wo = sb("wo", [128, 8, D], BF16)
            wo_b = Buf("wo")
            wdr = sb("wdr", [128, NF, D], BF16)
            wdr_b = Buf("wdr")
            lnp = [sb("lnp%d" % i, [128, D], F32) for i in range(4)]
            lnp_b = Buf("lnp")
            wgt = [sb("wgt%d" % i, [128, 8, 128], BF16) for i in range(3)]
            wut = [sb("wut%d" % i, [128, 8, 128], BF16) for i in range(3)]
            wgt_b = [Buf("wgt%d" % i) for i in range(3)]
            wut_b = [Buf("wut%d" % i) for i in range(3)]
            mxs = sb("mxs", [128, 8, CH], BF16)
            mxs_b = Buf("mxs")
            xt = [sb("xt%d" % i, [128, D], F32) for i in range(2)]
            xt_b = [Buf("xt%d" % i) for i in range(2)]
            hhs = [sb("hh%d" % i, [128, D], F32) for i in range(2)]
            hhs_b = [Buf("hh%d" % i) for i in range(2)]
            h1 = sb("h1", [128, 4, D], F32)
            h1_b = [Buf("h1_%d" % i) for i in range(4)]
            h1bs = [sb("h1b%d" % i, [128, D], BF16) for i in range(2)]
            h1bs_b = [Buf("h1b%d" % i) for i in range(2)]
            h1T = sb("h1T", [128, 8, CH], BF16)
            h1T_b = Buf("h1T")
            sg = [sb("sg%d" % i, [128, CH], F32) for i in range(2)]
            sg_b = [Buf("sg%d" % i) for i in range(2)]
            aT = sb("aT", [128, NF, CH], BF16)
            aT_b = Buf("aT")
            ot = [sb("ot%d" % i, [128, D], F32) for i in range(2)]
            ot_b = [Buf("ot%d" % i) for i in range(2)]
            st6 = [sb("st6_%d" % i, [128, 12], F32) for i in range(2)]
            mv = [sb("mv%d" % i, [128, 2], F32) for i in range(2)]
            rs = [sb("rs%d" % i, [128, 1], F32) for i in range(2)]
            st_b = [Buf("stats%d" % i) for i in range(2)]

            P.op("sync", lambda e: e.dma_start(out=idf[:, :], in_=cst_d[:, C_ID:C_ID + 128]), writes=[id_b], dbuf=id_b)
            P.op("vector", lambda e: e.tensor_copy(out=idt[:, :], in_=idf[:, :]), reads=[id_b], writes=[id_b])
            for i, src in enumerate((ln1g_d, ln1b_d, ln2g_d, ln2b_d)):
                P.op("sync", lambda e, i=i, src=src: e.dma_start(out=lnp[i][:, :], in_=src[:, :]),
                     writes=[lnp_b], dbuf=lnp_b)
            wout_v = wout_d.rearrange("(c p) n -> p c n", p=128)
            n = 0
            for c0 in range(0, 8, 2):
                s = n % 2
                n += 1
                P.op("sync", lambda e, s=s, c0=c0: e.dma_start(
                    out=stg[s][:, :].rearrange("p (c n) -> p c n", c=2), in_=wout_v[:, c0:c0 + 2, :]),
                    writes=[stg_b[s]], dbuf=stg_b[s])
                P.op("gpsimd", lambda e, s=s, c0=c0: e.tensor_copy(
                    out=wo[:, c0:c0 + 2, :], in_=stg[s][:, :].rearrange("p (c n) -> p c n", c=2)),
                    reads=[stg_b[s]], writes=[wo_b])
            P.op("sync", lambda e: e.dma_start(out=wdr[:, :, :], in_=wds_d.rearrange("f p n -> p f n")),
                 reads=[wsc_b], writes=[wdr_b], dbuf=wdr_b)

            def layer_norm(src, src_b, gi, dst, dst_b, sl):
                for c in range(2):
                    P.op("vector", lambda e, c=c: e.bn_stats(out=st6[sl][:, c * 6:(c + 1) * 6], in_=src[:, c * 512:(c + 1) * 512]),
                         reads=[src_b], writes=[st_b[sl]])
                P.op("vector", lambda e: e.bn_aggr(out=mv[sl][:, :], in_=st6[sl][:, :]), reads=[st_b[sl]], writes=[st_b[sl]])
                P.op("scalar", lambda e: e.activation(out=rs[sl][:, :], in_=mv[sl][:, 1:2], func=AF.Sqrt, bias=LN_EPS, scale=1.0),
                     reads=[st_b[sl]], writes=[st_b[sl]])
                P.op("vector", lambda e: e.reciprocal(out=rs[sl][:, :], in_=rs[sl][:, :]), reads=[st_b[sl]], writes=[st_b[sl]])
                P.op("vector", lambda e: e.scalar_tensor_tensor(out=src[:, :], in0=src[:, :], scalar=mv[sl][:, 0:1],
                                                                in1=lnp[gi][:, :], op0=ALU.subtract, op1=ALU.mult),
                     reads=[src_b, st_b[sl], lnp_b], writes=[src_b])
                P.op("vector", lambda e: e.scalar_tensor_tensor(out=dst, in0=src[:, :], scalar=rs[sl][:, 0:1],
                                                                in1=lnp[gi + 1][:, :], op0=ALU.mult, op1=ALU.add),
                     reads=[src_b, st_b[sl], lnp_b], writes=[dst_b])

            mix_v = mixT_d.rearrange("(c p) t -> p c t", p=128)
            wn = 0
            for k in range(8):
                P.op("sync", lambda e, k=k: e.dma_start(out=mxs[:, :, :], in_=mix_v[:, :, k * CH:(k + 1) * CH]),
                     reads=[mixT_b], writes=[mxs_b], dbuf=mxs_b)
                for t in range(4):
                    tok0 = (k * 4 + t) * 128
                    xs_ = (k * 4 + t) % 2
                    P.op("sync", lambda e, xs_=xs_, tok0=tok0: e.dma_start(out=xt[xs_][:, :], in_=xn_d[tok0:tok0 + 128, :]),
                         writes=[xt_b[xs_]], dbuf=xt_b[xs_])
                    for half in range(2):
                        bk = 2 + half
                        for kc in range(8):
                            P.op("tensor", lambda e, kc=kc, t=t, half=half, bk=bk: e.matmul(
                                bank(bk), lhsT=mxs[:, kc, t * 128:(t + 1) * 128], rhs=wo[:, kc, half * 512:(half + 1) * 512],
                                start=(kc == 0), stop=(kc == 7)),
                                reads=[mxs_b, wo_b], writes=[PB[bk]])
                        P.op("vector", lambda e, xs_=xs_, half=half, bk=bk: e.scalar_tensor_tensor(
                            out=hhs[xs_][:, half * 512:(half + 1) * 512], in0=xt[xs_][:, half * 512:(half + 1) * 512], scalar=ALPHA,
                            in1=bank(bk), op0=ALU.mult, op1=ALU.add),
                            reads=[xt_b[xs_], PB[bk]], writes=[hhs_b[xs_]])
                    layer_norm(hhs[xs_], hhs_b[xs_], 0, h1[:, t, :], h1_b[t], xs_)
                    P.op("scalar", lambda e, t=t, xs_=xs_: e.activation(out=h1bs[xs_][:, :], in_=h1[:, t, :], func=AF.Copy),
                         reads=[h1_b[t]], writes=[h1bs_b[xs_]])
                    for kc in range(8):
                        P.op("tensor", lambda e, kc=kc, xs_=xs_: e.transpose(
                            out=pT[:, kc * 128:(kc + 1) * 128], in_=h1bs[xs_][:, kc * 128:(kc + 1) * 128], identity=idt[:, :]),
                            reads=[h1bs_b[xs_], id_b], writes=[PB[7]])
                    P.op("vector", lambda e, t=t: e.tensor_copy(
                        out=h1T[:, :, t * 128:(t + 1) * 128], in_=pT[:, :].rearrange("p (c n) -> p c n", c=8)),
                        reads=[PB[7]], writes=[h1T_b])
                for f in range(NF):
                    ws = wn % 3
                    wn += 1
                    P.op("sync", lambda e, ws=ws, f=f: e.dma_start(
                        out=wgt[ws][:, :, :], in_=wgs_d[f, :, :].rearrange("p (c j) -> p c j", c=8)),
                        reads=[wsc_b], writes=[wgt_b[ws]], dbuf=wgt_b[ws])
                    P.op("sync", lambda e, ws=ws, f=f: e.dma_start(
                        out=wut[ws][:, :, :], in_=wus_d[f, :, :].rearrange("p (c j) -> p c j", c=8)),
                        reads=[wsc_b], writes=[wut_b[ws]], dbuf=wut_b[ws])
                    gb = 0 + (f % 2)
                    ub = 4 + (f % 2)
                    for kc in range(8):
                        P.op("tensor", lambda e, kc=kc, ws=ws, gb=gb: e.matmul(
                            bank(gb), lhsT=wgt[ws][:, kc, :], rhs=h1T[:, kc, :], start=(kc == 0), stop=(kc == 7)),
                            reads=[wgt_b[ws], h1T_b], writes=[PB[gb]])
                    for kc in range(8):
                        P.op("tensor", lambda e, kc=kc, ws=ws, ub=ub: e.matmul(
                            bank(ub), lhsT=wut[ws][:, kc, :], rhs=h1T[:, kc, :], start=(kc == 0), stop=(kc == 7)),
                            reads=[wut_b[ws], h1T_b], writes=[PB[ub]])
                    ss = f % 2
                    P.op("scalar", lambda e, ss=ss, gb=gb: e.activation(out=sg[ss][:, :], in_=bank(gb), func=AF.Silu),
                         reads=[PB[gb]], writes=[sg_b[ss]])
                    P.op("vector", lambda e, ss=ss, ub=ub, f=f: e.tensor_tensor(
                        out=aT[:, f, :], in0=sg[ss][:, :], in1=bank(ub), op=ALU.mult),
                        reads=[sg_b[ss], PB[ub]], writes=[aT_b])
                for t in range(4):
                    tok0 = (k * 4 + t) * 128
                    for half in range(2):
                        bk = 2 + half
                        for f in range(NF):
                            P.op("tensor", lambda e, f=f, t=t, half=half, bk=bk: e.matmul(
                                bank(bk), lhsT=aT[:, f, t * 128:(t + 1) * 128], rhs=wdr[:, f, half * 512:(half + 1) * 512],
                                start=(f == 0), stop=(f == NF - 1)),
                                reads=[aT_b, wdr_b], writes=[PB[bk]])
                        os_ = (k * 4 + t) % 2
                        P.op("vector", lambda e, t=t, half=half, bk=bk, os_=os_: e.scalar_tensor_tensor(
                            out=hhs[os_][:, half * 512:(half + 1) * 512], in0=h1[:, t, half * 512:(half + 1) * 512], scalar=ALPHA,
                            in1=bank(bk), op0=ALU.mult, op1=ALU.add),
                            reads=[h1_b[t], PB[bk]], writes=[hhs_b[os_]])
                    os_ = (k * 4 + t) % 2
                    layer_norm(hhs[os_], hhs_b[os_], 2, ot[os_][:, :], ot_b[os_], os_)
                    P.op("sync", lambda e, os_=os_, tok0=tok0: e.dma_start(out=out_d[tok0:tok0 + 128, :], in_=ot[os_][:, :]),
                         reads=[ot_b[os_]], dbuf=ot_b[os_])
            P.flush(final=True)
    return nc


_NC = None


def kernel(x, w_in, g_sb, g_dil, w_out, ln1_g, ln1_b, w_gate, w_up, w_down, ln2_g, ln2_b):
    global _NC
    x = np.asarray(x, np.float32)
    if _NC is None:
        _NC = build_nc()
    nc = _NC
    cst = _consts()
    f = lambda a: np.ascontiguousarray(np.asarray(a, np.float32))
    shared = {
        "cst": cst,
        "w_in": f(w_in[0]), "w_out": f(w_out[0]), "w_gate": f(w_gate[0]), "w_up": f(w_up[0]), "w_down": f(w_down[0]),
        "gsb": f(np.asarray(g_sb[0]).reshape(8, 64).T), "gdil": f(np.asarray(g_dil[0]).reshape(8, 64).T),
        "ln1g": f(np.broadcast_to(np.asarray(ln1_g[0]), (128, D))), "ln1b": f(np.broadcast_to(np.asarray(ln1_b[0]), (128, D))),
        "ln2g": f(np.broadcast_to(np.asarray(ln2_g[0]), (128, D))), "ln2b": f(np.broadcast_to(np.asarray(ln2_b[0]), (128, D))),
    }
    in_maps = []
    for c in range(8):
        b, par = c // 2, c % 2
        xb_ = x[b]
        kvv = np.ones(S, np.float32)
        if par == 0:
            xv = np.concatenate([np.zeros((CH, D), np.float32), xb_[:S - CH]], axis=0)
            kvv[:CH] = 0.0
        else:
            xv = xb_
        m = dict(shared)
        m["xT"] = np.ascontiguousarray(xv.T)
        m["xn"] = np.ascontiguousarray(xb_.reshape(NV, CH, D)[par::2].reshape(S // 2, D))
        m["kv"] = np.ascontiguousarray(kvv.reshape(64, 128).T)
        in_maps.append(m)
    res = run_bass_kernel_spmd(nc, in_maps, core_ids=list(range(8)))
    out = np.empty((NB, S, D), np.float32)
    for c in range(8):
        b, par = c // 2, c % 2
        out[b].reshape(NV, CH, D)[par::2] = np.asarray(res.results[c]["out"], np.float32).reshape(8, CH, D)
    return out
```

```python
import numpy as np
from contextlib import ExitStack

import concourse.bass as bass
import concourse.mybir as mybir
from concourse.bass_utils import run_bass_kernel_spmd

F32 = mybir.dt.float32
BF16 = mybir.dt.bfloat16
AF = mybir.ActivationFunctionType
ALU = mybir.AluOpType

D = 1024
S = 8192
NB = 4
DFF = 2816
NF = DFF // 128
CH = 512
NV = S // CH
ALPHA = 2.0 ** 0.25
LN_EPS = 1e-5
RMS_EPS = 1e-6
MARG = 2048
NEG = -30000.0
INTERLEAVE_DIL = True
ZIP_DIL = False
BIGN = 131072.0

C_ID = 0
C_NTRI = 128
C_NONES = 256
C_MASK = 384
C_NP12 = C_MASK + 4 * 512
C_NP3A = C_NP12 + 1024
C_NP3B = C_NP3A + 1024
C_SID = C_NP3B + 1024
C_MEAN = C_SID + 12 * 128
C_ONES = C_MEAN + 64
C_MEANE = C_ONES + 64
C_MEANB = C_MEANE + 64
NCST = C_MEANB + 128

ENGS = ("sync", "tensor", "scalar", "vector", "gpsimd")


def _consts():
    c = np.zeros((128, NCST), np.float32)
    j = np.arange(128)[:, None]
    s = np.arange(128)[None, :]
    c[:, C_ID:C_ID + 128] = (j == s)
    c[:, C_NTRI:C_NTRI + 128] = -(j >= s).astype(np.float32)
    c[:, C_NONES:C_NONES + 128] = -1.0
    q = np.arange(512)[None, :]
    for mb in range(4):
        c[:, C_MASK + mb * 512:C_MASK + (mb + 1) * 512] = np.where(mb * 128 + j >= q, NEG, 0.0)

    def npat(G, W, qs_of):
        out = np.zeros((128, 1024), np.float32)
        for g in range(G):
            for half in range(2):
                for qi in range(W):
                    qs = qs_of(qi)
                    col = g * 2 * W + half * W + qi
                    if half == 1:
                        n = qs - np.arange(128)
                    else:
                        n = qs - np.arange(128) + 128
                    ok = (n >= 0) & (n <= 128)
                    out[:, col] = np.where(ok, n, BIGN)
        return out

    c[:, C_NP12:C_NP12 + 1024] = npat(4, 128, lambda qi: qi)
    c[:, C_NP3A:C_NP3A + 1024] = npat(16, 32, lambda qi: 32 + qi)
    c[:, C_NP3B:C_NP3B + 1024] = npat(16, 32, lambda qi: 96 + qi)
    for e in range(-8, 4):
        c[:, C_SID + (e + 8) * 128:C_SID + (e + 9) * 128] = -(2.0 ** e) * (j == s)
    c[:, C_MEAN:C_MEAN + 64] = 1.0 / 64.0
    c[:, C_ONES:C_ONES + 64] = 1.0
    c[0:64, C_MEANE:C_MEANE + 64] = 1.0 / 64.0
    c[64, C_MEANE:C_MEANE + 64] = RMS_EPS
    c[:, C_MEANB:C_MEANB + 128] = ((j // 64) == (s // 64)) / 64.0
    return c


class Buf:
    __slots__ = ("name", "lw", "rd", "rdd", "sem", "dcnt", "uid")
    _n = [0]

    def __init__(self, name):
        Buf._n[0] += 1
        self.uid = Buf._n[0]
        self.name = name
        self.lw = None
        self.rd = {}
        self.rdd = []
        self.sem = None
        self.dcnt = 0


class Op:
    __slots__ = ("eng", "fn", "deps", "needed", "val", "dbuf", "flushed")

    def __init__(self, eng, fn, dbuf):
        self.eng = eng
        self.fn = fn
        self.deps = []
        self.needed = False
        self.val = None
        self.dbuf = dbuf
        self.flushed = False


class Prog:
    def __init__(self, nc, esems, dsems):
        self.nc = nc
        self.esems = esems
        self.dsems = list(dsems)
        self.ops = {e: [] for e in ENGS}
        self.cnt = {e: 0 for e in ENGS}
        self.waited = {e: {} for e in ENGS}
        self.dma_ops = []

    def op(self, eng, fn, reads=(), writes=(), dbuf=None):
        o = Op(eng, fn, dbuf)
        deps = {}
        for b in reads:
            if b.lw is not None:
                deps[id(b.lw)] = b.lw
        for b in writes:
            if b.lw is not None:
                deps[id(b.lw)] = b.lw
            for r in b.rd.values():
                deps[id(r)] = r
            for r in b.rdd:
                deps[id(r)] = r
        for d in deps.values():
            if d is o:
                continue
            if d.dbuf is None and d.eng == "tensor" and eng == "tensor":
                continue
            o.deps.append(d)
            d.needed = True
        for b in reads:
            if dbuf is not None:
                b.rdd.append(o)
            else:
                b.rd[eng] = o
        for b in writes:
            b.lw = o
            b.rd = {}
            b.rdd = []
        if dbuf is not None:
            if dbuf.sem is None:
                dbuf.sem = self.dsems.pop()
            dbuf.dcnt += 16
            o.val = dbuf.dcnt
            self.dma_ops.append(o)
        self.ops[eng].append(o)
        return o

    def flush(self, final=False):
        nc = self.nc
        for e in ENGS:
            for o in self.ops[e]:
                if o.dbuf is None and o.needed:
                    self.cnt[e] += 1
                    o.val = self.cnt[e]
        pending_dma = [o for o in self.dma_ops]
        self.dma_ops = []
        with nc.Block() as block:
            for e in ENGS:
                ops = self.ops[e]
                if not ops and not (e == "gpsimd"):
                    continue

                def body(eng, e=e, ops=ops):
                    waited = self.waited[e]
                    for o in ops:
                        for d in o.deps:
                            if d.dbuf is not None:
                                key = ("d", d.dbuf.uid)
                                sem = d.dbuf.sem
                            else:
                                if d.val is None:
                                    assert d.flushed
                                    continue
                                key = ("e", d.eng)
                                sem = self.esems[d.eng]
                            if waited.get(key, 0) >= d.val:
                                continue
                            waited[key] = d.val
                            eng.wait_ge(sem, d.val)
                        ins = o.fn(eng)
                        if o.dbuf is not None:
                            ins.then_inc(o.dbuf.sem, 16)
                        elif o.needed:
                            ins.then_inc(self.esems[e], 1)
                    if e == "gpsimd":
                        for o in pending_dma:
                            key = ("d", o.dbuf.uid)
                            if waited.get(key, 0) >= o.val:
                                continue
                            waited[key] = o.val
                            eng.wait_ge(o.dbuf.sem, o.val)

                getattr(block, e)(body)
        for e in ENGS:
            for o in self.ops[e]:
                o.flushed = True
            self.ops[e] = []


def build_nc():
    nc = bass.Bass("TRN2", target_bir_lowering=False)

    def din(name, shape, dt=F32):
        return nc.dram_tensor(name, list(shape), dt, kind="ExternalInput").ap()

    xT_d = din("xT", [D, S])
    xn_d = din("xn", [S // 2, D])
    kv_d = din("kv", [128, 64])
    cst_d = din("cst", [128, NCST])
    win_d = din("w_in", [D, 3 * D])
    wout_d = din("w_out", [D, D])
    wg_d = din("w_gate", [D, DFF])
    wu_d = din("w_up", [D, DFF])
    wd_d = din("w_down", [DFF, D])
    gsb_d = din("gsb", [64, 8])
    gdil_d = din("gdil", [64, 8])
    gsbp_d = din("gsbp", [128, 4])
    ln1g_d = din("ln1g", [128, D])
    ln1b_d = din("ln1b", [128, D])
    ln2g_d = din("ln2g", [128, D])
    ln2b_d = din("ln2b", [128, D])
    out_d = nc.dram_tensor("out", [S // 2, D], F32, kind="ExternalOutput").ap()
    vscr_d = nc.dram_tensor("vscr", [MARG + S, 130], BF16).ap()
    mixT_d = nc.dram_tensor("mixT", [D, S // 2], BF16).ap()
    wgs_d = nc.dram_tensor("wgs", [NF, 128, 1024], BF16).ap()
    wus_d = nc.dram_tensor("wus", [NF, 128, 1024], BF16).ap()
    wds_d = nc.dram_tensor("wds", [NF, 128, 1024], BF16).ap()

    with ExitStack() as top:
        esems = {e: top.enter_context(nc.semaphore("es_" + e)) for e in ENGS}
        dsems = [top.enter_context(nc.semaphore("ds%d" % i)) for i in range(96)]
        P = Prog(nc, esems, dsems)

        PB = [Buf("bank%d" % i) for i in range(8)]

        vscr_b = Buf("vscr")
        mixT_b = Buf("mixT")
        wsc_b = Buf("wscr")

        with ExitStack() as ph:
            def sb(name, shape, dt):
                return ph.enter_context(nc.sbuf_tensor("a_" + name, list(shape), dt))

            pbig = ph.enter_context(nc.psum_tensor("a_pbig", [128, 1024], F32))
            pbigB = ph.enter_context(nc.psum_tensor("a_pbigB", [128, 1024], F32))
            pbk = [ph.enter_context(nc.psum_tensor("a_pb%d" % i, [128, 512], F32)) for i in range(4, 8)]

            def bank(i):
                if i < 2:
                    return pbig[:, i * 512:(i + 1) * 512]
                if i < 4:
                    return pbigB[:, (i - 2) * 512:(i - 1) * 512]
                return pbk[i - 4][:, :]

            cst = sb("cst", [128, NCST], BF16)
            cst_b = Buf("cst")
            stg = [sb("stg%d" % i, [128, 4096], F32) for i in range(2)]
            stg_b = [Buf("stg%d" % i) for i in range(2)]
            xb = [sb("xb%d" % i, [128, 8, CH], BF16) for i in range(2)]
            xb_b = [Buf("xb%d" % i) for i in range(2)]
            wp = sb("wp", [128, 8, 768], BF16)
            wp_b = Buf("wp")
            KaT = sb("KaT", [128, S], BF16)
            KaT_c = [Buf("KaT%d" % i) for i in range(NV)]
            Va = sb("Va", [128, 64, 128], BF16)
            Va_c = [Buf("Va%d" % i) for i in range(NV)]
            KbT = sb("KbT", [128, MARG + S], BF16)
            KbT_c = [Buf("KbT%d" % i) for i in range(NV)]
            QaT = [sb("QaT%d" % i, [128, CH], BF16) for i in range(2)]
            QaT_b = [Buf("QaT%d" % i) for i in range(2)]
            QbT = [sb("QbT%d" % i, [128, CH], BF16) for i in range(2)]
            QbT_b = [Buf("QbT%d" % i) for i in range(2)]
            kvf = sb("kvf", [128, 64], F32)
            kvb = sb("kvb", [128, 64], BF16)
            kv_b = Buf("kv")
            vst = [sb("vst%d" % i, [128, 4, 130], BF16) for i in range(2)]
            vst_b = [Buf("vst%d" % i) for i in range(2)]
            vb1 = sb("vb1", [128, 5, 130], BF16)
            vb4 = sb("vb4", [128, 8, 130], BF16)
            vb16 = sb("vb16", [128, 32, 130], BF16)
            vb1_b, vb4_b, vb16_b = Buf("vb1"), Buf("vb4"), Buf("vb16")
            e_t = [sb("e_t%d" % i, [128, 2 * CH], F32) for i in range(2)]
            e_b = [Buf("e_t%d" % i) for i in range(2)]
            sp_t = [sb("sp_t%d" % i, [128, 2 * CH], BF16) for i in range(2)]
            sp_b = [Buf("sp_t%d" % i) for i in range(2)]
            w_t = [sb("w_t%d" % i, [128, 2 * CH], BF16) for i in range(2)]
            w_b = [Buf("w_t%d" % i) for i in range(2)]
            Rt = [sb("R%d" % i, [128, 2 * CH], BF16) for i in range(3)]
            R_b = [Buf("R%d" % i) for i in range(3)]
            osbj = sb("osbj", [128, CH], F32)
            osbj_b = Buf("osbj")
            lnvj = sb("lnvj", [128, CH], F32)
            rstdj = sb("rstdj", [128, CH], F32)
            mixtj = sb("mixtj", [128, CH], BF16)
            nj_b = Buf("normj")
            mixtj_b = Buf("mixtj")
            gsbp_t = sb("gsbp_t", [128, 4], F32)
            dtc = [sb("dt%d" % i, [128, CH], F32) for i in range(2)]
            dtc_b = [Buf("dt%d" % i) for i in range(2)]
            etc_ = [[sb("etc%d_%d" % (i, j), [128, CH], BF16) for j in range(2)] for i in range(2)]
            etc_b = [[Buf("etc%d_%d" % (i, j)) for j in range(2)] for i in range(2)]
            sqc = [sb("sqc%d" % i, [128, CH], BF16) for i in range(2)]
            sqc_b = [Buf("sqc%d" % i) for i in range(2)]
            lnvc = [sb("lnvc%d" % i, [64, CH], F32) for i in range(2)]
            lnvc_b = [Buf("lnvc%d" % i) for i in range(2)]
            rstdc = [sb("rstdc%d" % i, [64, CH], F32) for i in range(2)]
            rstdc_b = [Buf("rstdc%d" % i) for i in range(2)]
            mixt = [sb("mixt%d" % i, [64, CH], BF16) for i in range(2)]
            mixt_b = [Buf("mixt%d" % i) for i in range(2)]
            gsb_t = sb("gsb_t", [64, 8], F32)
            gdil_t = sb("gdil_t", [64, 8], F32)
            g_b = Buf("g")

            for i in range(2):
                c0 = i * 3680
                P.op("sync", lambda e, i=i, c0=c0: e.dma_start(out=stg[i][:, 0:3680], in_=cst_d[:, c0:c0 + 3680]),
                     writes=[stg_b[i]], dbuf=stg_b[i])
                P.op("vector", lambda e, i=i, c0=c0: e.tensor_copy(out=cst[:, c0:c0 + 3680], in_=stg[i][:, 0:3680]),
                     reads=[stg_b[i]], writes=[cst_b])
            P.op("sync", lambda e: e.dma_start(out=kvf[:, :], in_=kv_d[:, :]), writes=[kv_b], dbuf=kv_b)
            P.op("vector", lambda e: e.tensor_copy(out=kvb[:, :], in_=kvf[:, :]), reads=[kv_b], writes=[kv_b])
            P.op("sync", lambda e: e.dma_start(out=gsb_t[:, :], in_=gsb_d[:, :]), writes=[g_b], dbuf=g_b)
            P.op("sync", lambda e: e.dma_start(out=gdil_t[:, :], in_=gdil_d[:, :]), writes=[g_b], dbuf=g_b)
            P.op("sync", lambda e: e.dma_start(out=gsbp_t[:, :], in_=gsbp_d[:, :]), writes=[g_b], dbuf=g_b)
            P.op("gpsimd", lambda e: e.memset(KbT[:, :], 0.0), writes=list(KbT_c))
            P.op("gpsimd", lambda e: e.memset(vst[0][:, :, :], 0.0), writes=[vst_b[0]])
            for r0 in range(0, MARG + S, 512):
                P.op("sync", lambda e, r0=r0: e.dma_start(
                    out=vscr_d[r0:r0 + 512, :].rearrange("(t p) c -> p t c", p=128), in_=vst[0][:, :, :]),
                    reads=[vst_b[0]], writes=[vscr_b], dbuf=vst_b[0])

            xT_v = xT_d.rearrange("(c p) t -> p c t", p=128)
            win_v = win_d.rearrange("(c p) n -> p c n", p=128)
            ld_n = [0]

            tb = [sb("tb%d" % i, [128, DFF], BF16) for i in range(2)]
            tb_b = [Buf("tb%d" % i) for i in range(2)]
            w0 = []
            n0 = 0
            for (src, dst) in ((wg_d, wgs_d), (wu_d, wus_d)):
                dview = dst.rearrange("f p (c j) -> p f c j", c=8)
                for kc in range(8):
                    sl0 = n0 % 2
                    n0 += 1

                    def ld(sl0=sl0, src=src, kc=kc):
                        P.op("sync", lambda e: e.dma_start(out=stg[sl0][:, 0:DFF], in_=src[kc * 128:(kc + 1) * 128, :]),
                             writes=[stg_b[sl0]], dbuf=stg_b[sl0])

                    def cs_(sl0=sl0, dview=dview, kc=kc):
                        P.op("vector", lambda e: e.tensor_copy(out=tb[sl0][:, :], in_=stg[sl0][:, 0:DFF]),
                             reads=[stg_b[sl0]], writes=[tb_b[sl0]])
                        P.op("sync", lambda e: e.dma_start(
                            out=dview[:, :, kc, :], in_=tb[sl0][:, :].rearrange("p (f j) -> p f j", j=128)),
                            reads=[tb_b[sl0]], writes=[wsc_b], dbuf=tb_b[sl0])
                    w0.append((ld, cs_))
            for f0 in range(0, NF, 2):
                sl0 = n0 % 2
                n0 += 1

                def ld(sl0=sl0, f0=f0):
                    P.op("sync", lambda e: e.dma_start(
                        out=stg[sl0][:, 0:2048].rearrange("p (f n) -> p f n", f=2),
                        in_=wd_d[f0 * 128:(f0 + 2) * 128, :].rearrange("(f p) n -> p f n", p=128)),
                        writes=[stg_b[sl0]], dbuf=stg_b[sl0])

                def cs_(sl0=sl0, f0=f0):
                    P.op("vector", lambda e: e.tensor_copy(out=tb[sl0][:, 0:2048], in_=stg[sl0][:, 0:2048]),
                         reads=[stg_b[sl0]], writes=[tb_b[sl0]])
                    P.op("sync", lambda e: e.dma_start(
                        out=wds_d[f0:f0 + 2, :, :].rearrange("f p n -> p f n"),
                        in_=tb[sl0][:, 0:2048].rearrange("p (f n) -> p f n", f=2)),
                        reads=[tb_b[sl0]], writes=[wsc_b], dbuf=tb_b[sl0])
                w0.append((ld, cs_))
            def p0_group(pieces):
                out = []
                for i in range(len(pieces) + 1):
                    if i < len(pieces):
                        out.append(pieces[i][0])
                    if i >= 1:
                        out.append(pieces[i - 1][1])
                return out

            def norm_thunks(src, src_b, nrow, h_glob, is_dil, kown, gtile, ch=0, mbk=6):
                th = []
                lcol = C_MEANE if nrow == 65 else C_MEAN

                def t1():
                    P.op("scalar", lambda e: e.activation(out=sqc[ch][0:nrow, :], in_=src[0:nrow, :], func=AF.Square),
                         reads=[src_b], writes=[sqc_b[ch]])
                    P.op("tensor", lambda e: e.matmul(bank(mbk)[0:64, :], lhsT=cst[0:nrow, lcol:lcol + 64], rhs=sqc[ch][0:nrow, :],
                                                      start=True, stop=True),
                         reads=[sqc_b[ch], cst_b], writes=[PB[mbk]])

                def t2():
                    if nrow == 65:
                        P.op("scalar", lambda e: e.activation(out=lnvc[ch][:, :], in_=bank(mbk)[0:64, :], func=AF.Ln),
                             reads=[PB[mbk]], writes=[lnvc_b[ch]])
                    else:
                        P.op("scalar", lambda e: e.activation(out=lnvc[ch][:, :], in_=bank(mbk)[0:64, :], func=AF.Ln,
                                                              bias=RMS_EPS, scale=1.0),
                             reads=[PB[mbk]], writes=[lnvc_b[ch]])
                    P.op("scalar", lambda e: e.activation(out=rstdc[ch][:, :], in_=lnvc[ch][:, :], func=AF.Exp, scale=-0.5),
                         reads=[lnvc_b[ch]], writes=[rstdc_b[ch]])

                def t3():
                    ms = ld_n[0] % 2
                    ld_n[0] += 1
                    P.op("vector", lambda e, ms=ms: e.scalar_tensor_tensor(
                        out=mixt[ms][:, :], in0=src[0:64, :], scalar=gtile[0:64, h_glob:h_glob + 1], in1=rstdc[ch][:, :],
                        op0=ALU.mult, op1=ALU.mult),
                        reads=[src_b, rstdc_b[ch], g_b], writes=[mixt_b[ms]])
                    row0 = (512 if is_dil else 0) + h_glob * 64
                    P.op("sync", lambda e, ms=ms, row0=row0: e.dma_start(
                        out=mixT_d[row0:row0 + 64, kown * CH:(kown + 1) * CH], in_=mixt[ms][:, :]),
                        reads=[mixt_b[ms]], writes=[mixT_b], dbuf=mixt_b[ms])
                return [t1, t2, t3]

            def norm_joint_thunks(p, kown):
                def t1():
                    P.op("scalar", lambda e: e.activation(out=sqc[0][:, :], in_=osbj[:, :], func=AF.Square),
                         reads=[osbj_b], writes=[sqc_b[0]])
                    P.op("tensor", lambda e: e.matmul(bank(6), lhsT=cst[:, C_MEANB:C_MEANB + 128], rhs=sqc[0][:, :],
                                                      start=True, stop=True),
                         reads=[sqc_b[0], cst_b], writes=[PB[6]])

                def t2():
                    P.op("scalar", lambda e: e.activation(out=lnvj[:, :], in_=bank(6), func=AF.Ln, bias=RMS_EPS, scale=1.0),
                         reads=[PB[6]], writes=[nj_b])
                    P.op("scalar", lambda e: e.activation(out=rstdj[:, :], in_=lnvj[:, :], func=AF.Exp, scale=-0.5),
                         reads=[nj_b], writes=[nj_b])

                def t3():
                    P.op("vector", lambda e: e.scalar_tensor_tensor(
                        out=mixtj[:, :], in0=osbj[:, :], scalar=gsbp_t[:, p:p + 1], in1=rstdj[:, :],
                        op0=ALU.mult, op1=ALU.mult),
                        reads=[osbj_b, nj_b, g_b], writes=[mixtj_b])
                    P.op("sync", lambda e: e.dma_start(
                        out=mixT_d[p * 128:(p + 1) * 128, kown * CH:(kown + 1) * CH], in_=mixtj[:, :]),
                        reads=[mixtj_b], writes=[mixT_b], dbuf=mixtj_b)
                return [t1, t2, t3]

            def dil_thunks(v, p, QB, QB_b, kown):
                chains = []
                for hd in range(2):
                    th = []
                    sbk, abk = (6, 7) if (hd == 0 or INTERLEAVE_DIL) else (0, 1)
                    dt_, dt_b = dtc[hd], dtc_b[hd]
                    et, et_b = etc_[hd], etc_b[hd]
                    r = slice(hd * 64, hd * 64 + 64)
                    hg = 2 * p + hd
                    for pi in range(3):
                        ex = -(hg + 1) + 2 * pi
                        sid0 = C_SID + (ex + 8) * 128
                        np0 = C_NP12 if pi < 2 else (C_NP3A if v % 4 == 1 else C_NP3B)
                        G = 4 if pi < 2 else 16
                        W = 128 if pi < 2 else 32
                        for hb in range(2):
                            groups = list(range(hb * G // 2, (hb + 1) * G // 2))

                            def t1(hb=hb, groups=groups, sid0=sid0, np0=np0, pi=pi, W=W, r=r, sbk=sbk):
                                P.op("tensor", lambda e: e.matmul(
                                    bank(sbk), lhsT=cst[:, sid0:sid0 + 128], rhs=cst[:, np0 + hb * 512:np0 + (hb + 1) * 512],
                                    start=True, stop=False, skip_group_check=True),
                                    reads=[cst_b], writes=[PB[sbk]])
                                for g in groups:
                                    for half in range(2):
                                        col0 = g * 2 * W + half * W - hb * 512
                                        if pi == 0:
                                            blk = 4 * v + g - 1 + half
                                            k0 = MARG + blk * 128
                                            kap = KbT[r, k0:k0 + 128]
                                            qap = QB[r, g * 128:(g + 1) * 128]
                                            kbufs = [KbT_c[blk // 4]]
                                        elif pi == 1:
                                            k0 = MARG + (v - 1 + half) * 512 + g
                                            kap = KbT[r, k0:k0 + 509:4]
                                            qap = QB[r, g:g + 509:4]
                                            kbufs = [KbT_c[v - 1 + half]]
                                        else:
                                            U = v // 4 - 1 + half
                                            k0 = MARG + U * 2048 + g
                                            kap = KbT[r, k0:k0 + 2033:16]
                                            qap = QB[r, g:g + 497:16]
                                            kbufs = [KbT_c[c] for c in range(4 * U, 4 * U + 4) if 0 <= c <= v]
                                        P.op("tensor", lambda e, kap=kap, qap=qap, col0=col0: e.matmul(
                                            bank(sbk)[:, col0:col0 + W], lhsT=kap, rhs=qap, start=False, stop=False,
                                            skip_group_check=True),
                                            reads=kbufs + [QB_b], writes=[PB[sbk]])

                            def t2(hb=hb, sbk=sbk, et=et, et_b=et_b):
                                P.op("scalar", lambda e: e.activation(out=et[hb][:, :], in_=bank(sbk), func=AF.Exp),
                                     reads=[PB[sbk]], writes=[et_b[hb]])

                            def t3(hb=hb, groups=groups, pi=pi, W=W, hd=hd, abk=abk, et=et, et_b=et_b):
                                first = (hb == 0)
                                for g in groups:
                                    for half in range(2):
                                        col0 = g * 2 * W + half * W - hb * 512
                                        if pi == 0:
                                            vt = vb1[:, g + half, hd * 65:(hd + 1) * 65]
                                            vbuf = vb1_b
                                        elif pi == 1:
                                            vt = vb4[:, half * 4 + g, hd * 65:(hd + 1) * 65]
                                            vbuf = vb4_b
                                        else:
                                            vt = vb16[:, half * 16 + g, hd * 65:(hd + 1) * 65]
                                            vbuf = vb16_b
                                        P.op("tensor", lambda e, vt=vt, col0=col0, g=g, first=first: e.matmul(
                                            bank(abk)[0:65, g * W:(g + 1) * W], lhsT=vt, rhs=et[hb][:, col0:col0 + W],
                                            start=first, stop=False, skip_group_check=True),
                                            reads=[vbuf, et_b[hb]], writes=[PB[abk]])
                                        first = False
                            th += [t1, t2, t3]

                        def t4(pi=pi, abk=abk, dt_=dt_, dt_b=dt_b):
                            if pi == 0:
                                P.op("vector", lambda e: e.tensor_copy(out=dt_[0:65, :], in_=bank(abk)[0:65, :]),
                                     reads=[PB[abk]], writes=[dt_b])
                            else:
                                rr = 4 if pi == 1 else 16
                                P.op("vector", lambda e: e.tensor_tensor(
                                    out=dt_[0:65, :].rearrange("p (u r) -> p u r", r=rr),
                                    in0=dt_[0:65, :].rearrange("p (u r) -> p u r", r=rr),
                                    in1=bank(abk)[0:65, :].rearrange("p (r u) -> p u r", r=rr), op=ALU.add),
                                    reads=[PB[abk], dt_b], writes=[dt_b])
                        th.append(t4)
                    th += norm_thunks(dt_, dt_b, 65, hg, True, kown, gdil_t, ch=hd, mbk=sbk)
                    chains.append(th)
                if not ZIP_DIL:
                    return chains[0] + chains[1]
                out = []
                for i in range(max(len(c) for c in chains)):
                    for c in chains:
                        if i < len(c):
                            out.append(c[i])
                return out

            def load_x(v):
                s = v % 2
                P.op("sync", lambda e, s=s, v=v: e.dma_start(
                    out=stg[s][:, :].rearrange("p (c t) -> p c t", c=8), in_=xT_v[:, :, v * CH:(v + 1) * CH]),
                    writes=[stg_b[s]], dbuf=stg_b[s])
                P.op("vector", lambda e, s=s: e.tensor_copy(
                    out=xb[s][:, :, :], in_=stg[s][:, :].rearrange("p (c t) -> p c t", c=8)),
                    reads=[stg_b[s]], writes=[xb_b[s]])

            pj_n = [0]

            def proj_thunks(v):
                s = v % 2
                own = (v % 2 == 1)
                qs = ((v - 1) // 2) % 2
                th = []

                def fm(j, evac):
                    bk = 6 + pj_n[0] % 2
                    pj_n[0] += 1
                    for k0 in (0, 4):
                        def t_(k0=k0, bk=bk):
                            for kc in range(k0, k0 + 4):
                                P.op("tensor", lambda e, kc=kc: e.matmul(
                                    bank(bk), lhsT=wp[:, kc, j * 128:(j + 1) * 128], rhs=xb[s][:, kc, :],
                                    start=(kc == 0), stop=(kc == 7)),
                                    reads=[wp_b, xb_b[s]], writes=[PB[bk]])
                        th.append(t_)
                    th.append(lambda bk=bk: evac(bk))

                def tm(j, evac):
                    bk = 6 + pj_n[0] % 2
                    pj_n[0] += 1
                    for t in range(4):
                        def t_(t=t, bk=bk):
                            for kc in range(8):
                                P.op("tensor", lambda e, kc=kc: e.matmul(
                                    bank(bk)[:, t * 128:(t + 1) * 128], lhsT=xb[s][:, kc, t * 128:(t + 1) * 128],
                                    rhs=wp[:, kc, j * 128:(j + 1) * 128], start=(kc == 0), stop=(kc == 7)),
                                    reads=[wp_b, xb_b[s]], writes=[PB[bk]])
                        th.append(t_)
                    th.append(lambda bk=bk: evac(bk))

                fm(1, lambda bk: P.op("vector", lambda e: e.tensor_copy(out=KaT[:, v * CH:(v + 1) * CH], in_=bank(bk)),
                                      reads=[PB[bk]], writes=[KaT_c[v]]))
                fm(4, lambda bk: P.op("vector", lambda e: e.tensor_copy(
                    out=KbT[:, MARG + v * CH:MARG + (v + 1) * CH], in_=bank(bk)), reads=[PB[bk]], writes=[KbT_c[v]]))
                tm(2, lambda bk: P.op("vector", lambda e: e.tensor_copy(
                    out=Va[:, v * 4:(v + 1) * 4, :], in_=bank(bk).rearrange("p (t n) -> p t n", t=4)),
                    reads=[PB[bk]], writes=[Va_c[v]]))

                def evac_vb(bk):
                    P.op("vector", lambda e: e.tensor_copy(
                        out=vst[s][:, :, :].rearrange("p t (h c) -> p t h c", h=2)[:, :, :, 0:64],
                        in_=bank(bk).rearrange("p (t h c) -> p t h c", t=4, h=2)),
                        reads=[PB[bk]], writes=[vst_b[s]])
                    for hc in (64, 129):
                        P.op("vector", lambda e, hc=hc: e.tensor_copy(
                            out=vst[s][:, :, hc:hc + 1], in_=kvb[:, v * 4:(v + 1) * 4].rearrange("p (t o) -> p t o", o=1)),
                            reads=[kv_b], writes=[vst_b[s]])
                    r0 = MARG + v * CH
                    P.op("sync", lambda e: e.dma_start(
                        out=vscr_d[r0:r0 + 512, :].rearrange("(t p) c -> p t c", p=128), in_=vst[s][:, :, :]),
                        reads=[vst_b[s]], writes=[vscr_b], dbuf=vst_b[s])
                tm(5, evac_vb)
                if own:
                    fm(0, lambda bk: P.op("vector", lambda e: e.tensor_scalar(
                        out=QaT[qs][:, :], in0=bank(bk), scalar1=0.125, scalar2=None, op0=ALU.mult),
                        reads=[PB[bk]], writes=[QaT_b[qs]]))
                    fm(3, lambda bk: P.op("vector", lambda e: e.tensor_scalar(
                        out=QbT[qs][:, :], in0=bank(bk), scalar1=0.125, scalar2=None, op0=ALU.mult),
                        reads=[PB[bk]], writes=[QbT_b[qs]]))
                return th

            def pass_prologue(p):
                th = []
                for j, base in enumerate((0, 512, 1024, 1536, 2048, 2560)):
                    def t_(j=j, base=base):
                        s = ld_n[0] % 2
                        ld_n[0] += 1
                        c0 = base + p * 128
                        P.op("sync", lambda e: e.dma_start(
                            out=stg[s][:, 0:1024].rearrange("p (c n) -> p c n", c=8), in_=win_v[:, :, c0:c0 + 128]),
                            writes=[stg_b[s]], dbuf=stg_b[s])
                        P.op("vector", lambda e: e.tensor_copy(
                            out=wp[:, :, j * 128:(j + 1) * 128], in_=stg[s][:, 0:1024].rearrange("p (c n) -> p c n", c=8)),
                            reads=[stg_b[s]], writes=[wp_b])
                    th.append(t_)
                th.append(lambda: load_x(0))
                th.append(lambda: load_x(1))
                return th

            for p in range(4):
                if p == 0:
                    for t_ in pass_prologue(0):
                        t_()
                for t_ in proj_thunks(0) + proj_thunks(1):
                    t_()
                carry = []
                for v in range(1, NV, 2):
                    kown = (v - 1) // 2
                    qs = kown % 2
                    QA, QB = QaT[qs], QbT[qs]
                    QA_b, QB_b = QaT_b[qs], QbT_b[qs]
                    b1 = MARG + (4 * v - 1) * 128
                    P.op("sync", lambda e, b1=b1: e.dma_start(
                        out=vb1[:, :, :], in_=vscr_d[b1:b1 + 640, :].rearrange("(t p) c -> p t c", p=128)),
                        reads=[vscr_b], writes=[vb1_b], dbuf=vb1_b)
                    for half in range(2):
                        b4 = MARG + (v - 1 + half) * 512
                        P.op("sync", lambda e, b4=b4, half=half: e.dma_start(
                            out=vb4[:, half * 4:(half + 1) * 4, :],
                            in_=vscr_d[b4:b4 + 512, :].rearrange("(s r) c -> s r c", r=4)),
                            reads=[vscr_b], writes=[vb4_b], dbuf=vb4_b)
                        b16 = MARG + (v // 4 - 1 + half) * 2048
                        P.op("sync", lambda e, b16=b16, half=half: e.dma_start(
                            out=vb16[:, half * 16:(half + 1) * 16, :],
                            in_=vscr_d[b16:b16 + 2048, :].rearrange("(s r) c -> s r c", r=16)),
                            reads=[vscr_b], writes=[vb16_b], dbuf=vb16_b)
                    side = carry
                    carry = []
                    if v + 2 < NV:
                        load_x(v + 1)
                        load_x(v + 2)
                        side = side + proj_thunks(v + 1) + proj_thunks(v + 2)
                    if v + 2 >= NV and p < 3:
                        side = side + pass_prologue(p + 1)
                    if p == 0:
                        take = len(w0) if v + 2 >= NV else min(len(w0), 4)
                        side = side + p0_group(w0[:take])
                        w0 = w0[take:]
                    side_b = dil_thunks(v, p, QB, QB_b, kown)
                    if INTERLEAVE_DIL:
                        side = side + side_b
                        side_b = []

                    nkb = 4 * (v + 1)

                    def st_A(i, b0, last):
                        kb = nkb - 1 - i
                        diag = kb >= 4 * v
                        mb = kb - 4 * v
                        ks = slice(kb * 128, (kb + 1) * 128)
                        for hd in range(2):
                            r = slice(hd * 64, hd * 64 + 64)
                            bk = b0 + hd
                            P.op("tensor", lambda e, bk=bk, r=r, ks=ks, diag=diag, last=last, QA=QA: e.matmul(
                                bank(bk), lhsT=KaT[r, ks], rhs=QA[r, :], start=True, stop=(last and not diag)),
                                reads=[KaT_c[kb // 4], QA_b], writes=[PB[bk]])
                            if diag:
                                P.op("tensor", lambda e, bk=bk, mb=mb, last=last: e.matmul(
                                    bank(bk), lhsT=cst[:, C_ID:C_ID + 128],
                                    rhs=cst[:, C_MASK + mb * 512:C_MASK + (mb + 1) * 512], start=False, stop=last),
                                    reads=[cst_b], writes=[PB[bk]])

                    def st_act1(i):
                        sl = i % 2
                        P.op("scalar", lambda e, sl=sl: e.activation(out=e_t[sl][:, :], in_=pbig[:, :], func=AF.Exp),
                             reads=[PB[0], PB[1]], writes=[e_b[sl]])
                        P.op("scalar", lambda e, sl=sl: e.activation(out=sp_t[sl][:, :], in_=e_t[sl][:, :], func=AF.Ln,
                                                                     bias=1.0, scale=1.0),
                             reads=[e_b[sl]], writes=[sp_b[sl]])

                    def st_R(i):
                        sl = i % 2
                        if i == 0:
                            P.op("vector", lambda e, sl=sl: e.tensor_copy(out=Rt[1][:, :], in_=sp_t[sl][:, :]),
                                 reads=[sp_b[sl]], writes=[R_b[1]])
                        elif i < nkb - 1:
                            P.op("vector", lambda e, sl=sl, i=i: e.tensor_tensor(
                                out=Rt[(i + 1) % 3][:, :], in0=Rt[i % 3][:, :], in1=sp_t[sl][:, :], op=ALU.add),
                                reads=[sp_b[sl], R_b[i % 3]], writes=[R_b[(i + 1) % 3]])

                    def st_B(i):
                        sl = i % 2
                        st_A(i, 2, False)
                        for hd in range(2):
                            cs = slice(hd * CH, (hd + 1) * CH)
                            P.op("tensor", lambda e, sl=sl, i=i, hd=hd, cs=cs: e.matmul(
                                bank(2 + hd), lhsT=cst[:, C_NTRI:C_NTRI + 128], rhs=sp_t[sl][:, cs],
                                start=False, stop=(i == 0)),
                                reads=[cst_b, sp_b[sl]], writes=[PB[2 + hd]])
                            if i > 0:
                                P.op("tensor", lambda e, sl=sl, i=i, hd=hd, cs=cs: e.matmul(
                                    bank(2 + hd), lhsT=cst[:, C_NONES:C_NONES + 128], rhs=Rt[i % 3][:, cs],
                                    start=False, stop=True),
                                    reads=[cst_b, R_b[i % 3]], writes=[PB[2 + hd]])

                    def st_act2(i):
                        sl = i % 2
                        P.op("scalar", lambda e, sl=sl: e.activation(out=w_t[sl][:, :], in_=pbigB[:, :], func=AF.Exp),
                             reads=[PB[2], PB[3]], writes=[w_b[sl]])

                    def st_AV(i):
                        sl = i % 2
                        kb = nkb - 1 - i
                        for hd in range(2):
                            cs = slice(hd * CH, (hd + 1) * CH)
                            P.op("tensor", lambda e, sl=sl, kb=kb, hd=hd, i=i, cs=cs, nkb=nkb: e.matmul(
                                bank(4)[hd * 64:(hd + 1) * 64, :], lhsT=Va[:, kb, hd * 64:(hd + 1) * 64], rhs=w_t[sl][:, cs],
                                start=(i == 0), stop=(i == nkb - 1)),
                                reads=[Va_c[kb // 4], w_b[sl]], writes=[PB[4]])

                    nside = len(side)
                    per = -(-nside // max(1, nkb - 2)) if nside else 0
                    si = 0
                    for st in range(nkb + 2):
                        if st < nkb:
                            st_A(st, 0, True)
                            st_act1(st)
                            st_R(st)
                        if 0 <= st - 1 < nkb:
                            st_B(st - 1)
                            st_act2(st - 1)
                        if 0 <= st - 2 < nkb:
                            st_AV(st - 2)
                        for _ in range(per):
                            if si < nside:
                                side[si]()
                                si += 1
                    while si < nside:
                        side[si]()
                        si += 1
                    for t_ in side_b:
                        t_()
                    P.op("vector", lambda e: e.tensor_copy(out=osbj[:, :], in_=bank(4)),
                         reads=[PB[4]], writes=[osbj_b])
                    carry += norm_joint_thunks(p, kown)
                    if v + 2 >= NV:
                        for t_ in carry:
                            t_()
                        carry = []
            P.flush()

        with ExitStack() as ph:
            def sb(name, shape, dt):
                return ph.enter_context(nc.sbuf_tensor("b_" + name, list(shape), dt))

            pbig = ph.enter_context(nc.psum_tensor("b_pbig", [128, 1024], F32))
            pbk = [ph.enter_context(nc.psum_tensor("b_pb%d" % i, [128, 512], F32)) for i in range(2, 7)]
            pT = ph.enter_context(nc.psum_tensor("b_pT", [128, 1024], BF16))

            def bank(i):
                if i < 2:
                    return pbig[:, i * 512:(i + 1) * 512]
                return pbk[i - 2][:, :]

            idt = sb("idt", [128, 128], BF16)
            idf = sb("idf", [128, 128], F32)
            id_b = Buf("id")
            stg = [sb("stg2_%d" % i, [128, 2048], F32) for i in range(2)]
            stg_b = [Buf("stg2_%d" % i) for i in range(2)]
            wo = sb("wo", [128, 8, D], BF16)
            wo_b = Buf("wo")
            wdr = sb("wdr", [128, NF, D], BF16)
            wdr_b = Buf("wdr")
            lnp = [sb("lnp%d" % i, [128, D], F32) for i in range(4)]
            lnp_b = Buf("lnp")
            wgt = [sb("wgt%d" % i, [128, 8, 128], BF16) for i in range(3)]
            wut = [sb("wut%d" % i, [128, 8, 128], BF16) for i in range(3)]
            wgt_b = [Buf("wgt%d" % i) for i in range(3)]
            wut_b = [Buf("wut%d" % i) for i in range(3)]
            mxs = sb("mxs", [128, 8, CH], BF16)
            mxs_b = Buf("mxs")
            xt = [sb("xt%d" % i, [128, D], F32) for i in range(2)]
            xt_b = [Buf("xt%d" % i) for i in range(2)]
            hhs = [sb("hh%d" % i, [128, D], F32) for i in range(2)]
            hhs_b = [Buf("hh%d" % i) for i in range(2)]
            h1 = sb("h1", [128, 4, D], F32)
            h1_b = [Buf("h1_%d" % i) for i in range(4)]
            h1bs = [sb("h1b%d" % i, [128, D], BF16) for i in range(2)]
            h1bs_b = [Buf("h1b%d" % i) for i in range(2)]
            h1T = sb("h1T", [128, 8, CH], BF16)
            h1T_b = Buf("h1T")
            sg = [sb("sg%d" % i, [128, CH], F32) for i in range(2)]
            sg_b = [Buf("sg%d" % i) for i in range(2)]
            aT = sb("aT", [128, NF, CH], BF16)
            aT_b = Buf("aT")
            ot = [sb("ot%d" % i, [128, D], F32) for i in range(2)]
            ot_b = [Buf("ot%d" % i) for i in range(2)]
            st6 = [sb("st6_%d" % i, [128, 12], F32) for i in range(2)]
            mv = [sb("mv%d" % i, [128, 2], F32) for i in range(2)]
            rs = [sb("rs%d" % i, [128, 1], F32) for i in range(2)]
            st_b = [Buf("stats%d" % i) for i in range(2)]

            P.op("sync", lambda e: e.dma_start(out=idf[:, :], in_=cst_d[:, C_ID:C_ID + 128]), writes=[id_b], dbuf=id_b)
            P.op("vector", lambda e: e.tensor_copy(out=idt[:, :], in_=idf[:, :]), reads=[id_b], writes=[id_b])
            for i, src in enumerate((ln1g_d, ln1b_d, ln2g_d, ln2b_d)):
                P.op("sync", lambda e, i=i, src=src: e.dma_start(out=lnp[i][:, :], in_=src[:, :]),
                     writes=[lnp_b], dbuf=lnp_b)
            wout_v = wout_d.rearrange("(c p) n -> p c n", p=128)
            n = 0
            for c0 in range(0, 8, 2):
                s = n % 2
                n += 1
                P.op("sync", lambda e, s=s, c0=c0: e.dma_start(
                    out=stg[s][:, :].rearrange("p (c n) -> p c n", c=2), in_=wout_v[:, c0:c0 + 2, :]),
                    writes=[stg_b[s]], dbuf=stg_b[s])
                P.op("gpsimd", lambda e, s=s, c0=c0: e.tensor_copy(
                    out=wo[:, c0:c0 + 2, :], in_=stg[s][:, :].rearrange("p (c n) -> p c n", c=2)),
                    reads=[stg_b[s]], writes=[wo_b])
            P.op("sync", lambda e: e.dma_start(out=wdr[:, :, :], in_=wds_d.rearrange("f p n -> p f n")),
                 reads=[wsc_b], writes=[wdr_b], dbuf=wdr_b)

            def layer_norm(src, src_b, gi, dst, dst_b, sl):
                for c in range(2):
                    P.op("vector", lambda e, c=c: e.bn_stats(out=st6[sl][:, c * 6:(c + 1) * 6], in_=src[:, c * 512:(c + 1) * 512]),
                         reads=[src_b], writes=[st_b[sl]])
                P.op("vector", lambda e: e.bn_aggr(out=mv[sl][:, :], in_=st6[sl][:, :]), reads=[st_b[sl]], writes=[st_b[sl]])
                P.op("scalar", lambda e: e.activation(out=rs[sl][:, :], in_=mv[sl][:, 1:2], func=AF.Sqrt, bias=LN_EPS, scale=1.0),
                     reads=[st_b[sl]], writes=[st_b[sl]])
                P.op("vector", lambda e: e.reciprocal(out=rs[sl][:, :], in_=rs[sl][:, :]), reads=[st_b[sl]], writes=[st_b[sl]])
                P.op("vector", lambda e: e.scalar_tensor_tensor(out=src[:, :], in0=src[:, :], scalar=mv[sl][:, 0:1],
                                                                in1=lnp[gi][:, :], op0=ALU.subtract, op1=ALU.mult),
                     reads=[src_b, st_b[sl], lnp_b], writes=[src_b])
                P.op("vector", lambda e: e.scalar_tensor_tensor(out=dst, in0=src[:, :], scalar=rs[sl][:, 0:1],
                                                                in1=lnp[gi + 1][:, :], op0=ALU.mult, op1=ALU.add),
                     reads=[src_b, st_b[sl], lnp_b], writes=[dst_b])

            mix_v = mixT_d.rearrange("(c p) t -> p c t", p=128)
            wn = 0
            for k in range(8):
                P.op("sync", lambda e, k=k: e.dma_start(out=mxs[:, :, :], in_=mix_v[:, :, k * CH:(k + 1) * CH]),
                     reads=[mixT_b], writes=[mxs_b], dbuf=mxs_b)
                for t in range(4):
                    tok0 = (k * 4 + t) * 128
                    xs_ = (k * 4 + t) % 2
                    P.op("sync", lambda e, xs_=xs_, tok0=tok0: e.dma_start(out=xt[xs_][:, :], in_=xn_d[tok0:tok0 + 128, :]),
                         writes=[xt_b[xs_]], dbuf=xt_b[xs_])
                    for half in range(2):
                        bk = 2 + half
                        for kc in range(8):
                            P.op("tensor", lambda e, kc=kc, t=t, half=half, bk=bk: e.matmul(
                                bank(bk), lhsT=mxs[:, kc, t * 128:(t + 1) * 128], rhs=wo[:, kc, half * 512:(half + 1) * 512],
                                start=(kc == 0), stop=(kc == 7)),
                                reads=[mxs_b, wo_b], writes=[PB[bk]])
                        P.op("vector", lambda e, xs_=xs_, half=half, bk=bk: e.scalar_tensor_tensor(
                            out=hhs[xs_][:, half * 512:(half + 1) * 512], in0=xt[xs_][:, half * 512:(half + 1) * 512], scalar=ALPHA,
                            in1=bank(bk), op0=ALU.mult, op1=ALU.add),
                            reads=[xt_b[xs_], PB[bk]], writes=[hhs_b[xs_]])
                    layer_norm(hhs[xs_], hhs_b[xs_], 0, h1[:, t, :], h1_b[t], xs_)
                    P.op("scalar", lambda e, t=t, xs_=xs_: e.activation(out=h1bs[xs_][:, :], in_=h1[:, t, :], func=AF.Copy),
                         reads=[h1_b[t]], writes=[h1bs_b[xs_]])
                    for kc in range(8):
                        P.op("tensor", lambda e, kc=kc, xs_=xs_: e.transpose(
                            out=pT[:, kc * 128:(kc + 1) * 128], in_=h1bs[xs_][:, kc * 128:(kc + 1) * 128], identity=idt[:, :]),
                            reads=[h1bs_b[xs_], id_b], writes=[PB[7]])
                    P.op("vector", lambda e, t=t: e.tensor_copy(
                        out=h1T[:, :, t * 128:(t + 1) * 128], in_=pT[:, :].rearrange("p (c n) -> p c n", c=8)),
                        reads=[PB[7]], writes=[h1T_b])
                for f in range(NF):
                    ws = wn % 3
                    wn += 1
                    P.op("sync", lambda e, ws=ws, f=f: e.dma_start(
                        out=wgt[ws][:, :, :], in_=wgs_d[f, :, :].rearrange("p (c j) -> p c j", c=8)),
                        reads=[wsc_b], writes=[wgt_b[ws]], dbuf=wgt_b[ws])
                    P.op("sync", lambda e, ws=ws, f=f: e.dma_start(
                        out=wut[ws][:, :, :], in_=wus_d[f, :, :].rearrange("p (c j) -> p c j", c=8)),
                        reads=[wsc_b], writes=[wut_b[ws]], dbuf=wut_b[ws])
                    gb = 0 + (f % 2)
                    ub = 4 + (f % 2)
                    for kc in range(8):
                        P.op("tensor", lambda e, kc=kc, ws=ws, gb=gb: e.matmul(
                            bank(gb), lhsT=wgt[ws][:, kc, :], rhs=h1T[:, kc, :], start=(kc == 0), stop=(kc == 7)),
                            reads=[wgt_b[ws], h1T_b], writes=[PB[gb]])
                    for kc in range(8):
                        P.op("tensor", lambda e, kc=kc, ws=ws, ub=ub: e.matmul(
                            bank(ub), lhsT=wut[ws][:, kc, :], rhs=h1T[:, kc, :], start=(kc == 0), stop=(kc == 7)),
                            reads=[wut_b[ws], h1T_b], writes=[PB[ub]])
                    ss = f % 2
                    P.op("scalar", lambda e, ss=ss, gb=gb: e.activation(out=sg[ss][:, :], in_=bank(gb), func=AF.Silu),
                         reads=[PB[gb]], writes=[sg_b[ss]])
                    P.op("vector", lambda e, ss=ss, ub=ub, f=f: e.tensor_tensor(
                        out=aT[:, f, :], in0=sg[ss][:, :], in1=bank(ub), op=ALU.mult),
                        reads=[sg_b[ss], PB[ub]], writes=[aT_b])
                for t in range(4):
                    tok0 = (k * 4 + t) * 128
                    for half in range(2):
                        bk = 2 + half
                        for f in range(NF):
                            P.op("tensor", lambda e, f=f, t=t, half=half, bk=bk: e.matmul(
                                bank(bk), lhsT=aT[:, f, t * 128:(t + 1) * 128], rhs=wdr[:, f, half * 512:(half + 1) * 512],
                                start=(f == 0), stop=(f == NF - 1)),
                                reads=[aT_b, wdr_b], writes=[PB[bk]])
                        os_ = (k * 4 + t) % 2
                        P.op("vector", lambda e, t=t, half=half, bk=bk, os_=os_: e.scalar_tensor_tensor(
                            out=hhs[os_][:, half * 512:(half + 1) * 512], in0=h1[:, t, half * 512:(half + 1) * 512], scalar=ALPHA,
                            in1=bank(bk), op0=ALU.mult, op1=ALU.add),
                            reads=[h1_b[t], PB[bk]], writes=[hhs_b[os_]])
                    os_ = (k * 4 + t) % 2
                    layer_norm(hhs[os_], hhs_b[os_], 2, ot[os_][:, :], ot_b[os_], os_)
                    P.op("sync", lambda e, os_=os_, tok0=tok0: e.dma_start(out=out_d[tok0:tok0 + 128, :], in_=ot[os_][:, :]),
                         reads=[ot_b[os_]], dbuf=ot_b[os_])
            P.flush(final=True)
    return nc


_NC = None


def kernel(x, w_in, g_sb, g_dil, w_out, ln1_g, ln1_b, w_gate, w_up, w_down, ln2_g, ln2_b):
    global _NC
    x = np.asarray(x, np.float32)
    if _NC is None:
        _NC = build_nc()
    nc = _NC
    cst = _consts()
    f = lambda a: np.ascontiguousarray(np.asarray(a, np.float32))
    shared = {
        "cst": cst,
        "w_in": f(w_in[0]), "w_out": f(w_out[0]), "w_gate": f(w_gate[0]), "w_up": f(w_up[0]), "w_down": f(w_down[0]),
        "gsb": f(np.asarray(g_sb[0]).reshape(8, 64).T), "gdil": f(np.asarray(g_dil[0]).reshape(8, 64).T),
        "gsbp": f(np.asarray(g_sb[0]).reshape(4, 128).T),
        "ln1g": f(np.broadcast_to(np.asarray(ln1_g[0]), (128, D))), "ln1b": f(np.broadcast_to(np.asarray(ln1_b[0]), (128, D))),
        "ln2g": f(np.broadcast_to(np.asarray(ln2_g[0]), (128, D))), "ln2b": f(np.broadcast_to(np.asarray(ln2_b[0]), (128, D))),
    }
    in_maps = []
    for c in range(8):
        b, par = c // 2, c % 2
        xb_ = x[b]
        kvv = np.ones(S, np.float32)
        if par == 0:
            xv = np.concatenate([np.zeros((CH, D), np.float32), xb_[:S - CH]], axis=0)
            kvv[:CH] = 0.0
        else:
            xv = xb_
        m = dict(shared)
        m["xT"] = np.ascontiguousarray(xv.T)
        m["xn"] = np.ascontiguousarray(xb_.reshape(NV, CH, D)[par::2].reshape(S // 2, D))
        m["kv"] = np.ascontiguousarray(kvv.reshape(64, 128).T)
        in_maps.append(m)
    res = run_bass_kernel_spmd(nc, in_maps, core_ids=list(range(8)))
    out = np.empty((NB, S, D), np.float32)
    for c in range(8):
        b, par = c // 2, c % 2
        out[b].reshape(NV, CH, D)[par::2] = np.asarray(res.results[c]["out"], np.float32).reshape(8, CH, D)
    return out
```

```python
import numpy as np
from contextlib import ExitStack

import concourse.bass as bass
import concourse.mybir as mybir
from concourse.bass_utils import run_bass_kernel_spmd

F32 = mybir.dt.float32
BF16 = mybir.dt.bfloat16
AF = mybir.ActivationFunctionType
ALU = mybir.AluOpType

D = 1024
S = 8192
NB = 4
DFF = 2816
NF = DFF // 128
CH = 512
NV = S // CH
ALPHA = 2.0 ** 0.25
LN_EPS = 1e-5
RMS_EPS = 1e-6
MARG = 2048
NEG = -30000.0
INTERLEAVE_DIL = True
ZIP_DIL = False
BIGN = 131072.0

C_ID = 0
C_NTRI = 128
C_NONES = 256
C_MASK = 384
C_NP12 = C_MASK + 4 * 512
C_NP3A = C_NP12 + 1024
C_NP3B = C_NP3A + 1024
C_SID = C_NP3B + 1024
C_MEAN = C_SID + 12 * 128
C_ONES = C_MEAN + 64
C_MEANE = C_ONES + 64
C_MEANB = C_MEANE + 64
NCST = C_MEANB + 128

ENGS = ("sync", "tensor", "scalar", "vector", "gpsimd")


def _consts():
    c = np.zeros((128, NCST), np.float32)
    j = np.arange(128)[:, None]
    s = np.arange(128)[None, :]
    c[:, C_ID:C_ID + 128] = (j == s)
    c[:, C_NTRI:C_NTRI + 128] = -(j >= s).astype(np.float32)
    c[:, C_NONES:C_NONES + 128] = -1.0
    q = np.arange(512)[None, :]
    for mb in range(4):
        c[:, C_MASK + mb * 512:C_MASK + (mb + 1) * 512] = np.where(mb * 128 + j >= q, NEG, 0.0)

    def npat(G, W, qs_of):
        out = np.zeros((128, 1024), np.float32)
        for g in range(G):
            for half in range(2):
                for qi in range(W):
                    qs = qs_of(qi)
                    col = g * 2 * W + half * W + qi
                    if half == 1:
                        n = qs - np.arange(128)
                    else:
                        n = qs - np.arange(128) + 128
                    ok = (n >= 0) & (n <= 128)
                    out[:, col] = np.where(ok, n, BIGN)
        return out

    c[:, C_NP12:C_NP12 + 1024] = npat(4, 128, lambda qi: qi)
    c[:, C_NP3A:C_NP3A + 1024] = npat(16, 32, lambda qi: 32 + qi)
    c[:, C_NP3B:C_NP3B + 1024] = npat(16, 32, lambda qi: 96 + qi)
    for e in range(-8, 4):
        c[:, C_SID + (e + 8) * 128:C_SID + (e + 9) * 128] = -(2.0 ** e) * (j == s)
    c[:, C_MEAN:C_MEAN + 64] = 1.0 / 64.0
    c[:, C_ONES:C_ONES + 64] = 1.0
    c[0:64, C_MEANE:C_MEANE + 64] = 1.0 / 64.0
    c[64, C_MEANE:C_MEANE + 64] = RMS_EPS
    c[:, C_MEANB:C_MEANB + 128] = ((j // 64) == (s // 64)) / 64.0
    return c


class Buf:
    __slots__ = ("name", "lw", "rd", "rdd", "sem", "dcnt", "uid")
    _n = [0]

    def __init__(self, name):
        Buf._n[0] += 1
        self.uid = Buf._n[0]
        self.name = name
        self.lw = None
        self.rd = {}
        self.rdd = []
        self.sem = None
        self.dcnt = 0


class Op:
    __slots__ = ("eng", "fn", "deps", "needed", "val", "dbuf", "flushed")

    def __init__(self, eng, fn, dbuf):
        self.eng = eng
        self.fn = fn
        self.deps = []
        self.needed = False
        self.val = None
        self.dbuf = dbuf
        self.flushed = False


class Prog:
    def __init__(self, nc, esems, dsems):
        self.nc = nc
        self.esems = esems
        self.dsems = list(dsems)
        self.ops = {e: [] for e in ENGS}
        self.cnt = {e: 0 for e in ENGS}
        self.waited = {e: {} for e in ENGS}
        self.dma_ops = []

    def op(self, eng, fn, reads=(), writes=(), dbuf=None):
        o = Op(eng, fn, dbuf)
        deps = {}
        for b in reads:
            if b.lw is not None:
                deps[id(b.lw)] = b.lw
        for b in writes:
            if b.lw is not None:
                deps[id(b.lw)] = b.lw
            for r in b.rd.values():
                deps[id(r)] = r
            for r in b.rdd:
                deps[id(r)] = r
        for d in deps.values():
            if d is o:
                continue
            if d.dbuf is None and d.eng == "tensor" and eng == "tensor":
                continue
            o.deps.append(d)
            d.needed = True
        for b in reads:
            if dbuf is not None:
                b.rdd.append(o)
            else:
                b.rd[eng] = o
        for b in writes:
            b.lw = o
            b.rd = {}
            b.rdd = []
        if dbuf is not None:
            if dbuf.sem is None:
                dbuf.sem = self.dsems.pop()
            dbuf.dcnt += 16
            o.val = dbuf.dcnt
            self.dma_ops.append(o)
        self.ops[eng].append(o)
        return o

    def flush(self, final=False):
        nc = self.nc
        for e in ENGS:
            for o in self.ops[e]:
                if o.dbuf is None and o.needed:
                    self.cnt[e] += 1
                    o.val = self.cnt[e]
        pending_dma = [o for o in self.dma_ops]
        self.dma_ops = []
        with nc.Block() as block:
            for e in ENGS:
                ops = self.ops[e]
                if not ops and not (e == "gpsimd"):
                    continue

                def body(eng, e=e, ops=ops):
                    waited = self.waited[e]
                    for o in ops:
                        for d in o.deps:
                            if d.dbuf is not None:
                                key = ("d", d.dbuf.uid)
                                sem = d.dbuf.sem
                            else:
                                if d.val is None:
                                    assert d.flushed
                                    continue
                                key = ("e", d.eng)
                                sem = self.esems[d.eng]
                            if waited.get(key, 0) >= d.val:
                                continue
                            waited[key] = d.val
                            eng.wait_ge(sem, d.val)
                        ins = o.fn(eng)
                        if o.dbuf is not None:
                            ins.then_inc(o.dbuf.sem, 16)
                        elif o.needed:
                            ins.then_inc(self.esems[e], 1)
                    if e == "gpsimd":
                        for o in pending_dma:
                            key = ("d", o.dbuf.uid)
                            if waited.get(key, 0) >= o.val:
                                continue
                            waited[key] = o.val
                            eng.wait_ge(o.dbuf.sem, o.val)

                getattr(block, e)(body)
        for e in ENGS:
            for o in self.ops[e]:
                o.flushed = True
            self.ops[e] = []


def build_nc():
    nc = bass.Bass("TRN2", target_bir_lowering=False)

    def din(name, shape, dt=F32):
        return nc.dram_tensor(name, list(shape), dt, kind="ExternalInput").ap()

    xT_d = din("xT", [D, S])
    xn_d = din("xn", [S // 2, D])
    kv_d = din("kv", [128, 64])
    cst_d = din("cst", [128, NCST])
    win_d = din("w_in", [D, 3 * D])
    wout_d = din("w_out", [D, D])
    wg_d = din("w_gate", [D, DFF])
    wu_d = din("w_up", [D, DFF])
    wd_d = din("w_down", [DFF, D])
    gsb_d = din("gsb", [64, 8])
    gdil_d = din("gdil", [64, 8])
    gsbp_d = din("gsbp", [128, 4])
    ln1g_d = din("ln1g", [128, D])
    ln1b_d = din("ln1b", [128, D])
    ln2g_d = din("ln2g", [128, D])
    ln2b_d = din("ln2b", [128, D])
    out_d = nc.dram_tensor("out", [S // 2, D], F32, kind="ExternalOutput").ap()
    vscr_d = nc.dram_tensor("vscr", [MARG + S, 130], BF16).ap()
    mixT_d = nc.dram_tensor("mixT", [D, S // 2], BF16).ap()
    wgs_d = nc.dram_tensor("wgs", [NF, 128, 1024], BF16).ap()
    wus_d = nc.dram_tensor("wus", [NF, 128, 1024], BF16).ap()
    wds_d = nc.dram_tensor("wds", [NF, 128, 1024], BF16).ap()

    with ExitStack() as top:
        esems = {e: top.enter_context(nc.semaphore("es_" + e)) for e in ENGS}
        dsems = [top.enter_context(nc.semaphore("ds%d" % i)) for i in range(96)]
        P = Prog(nc, esems, dsems)

        PB = [Buf("bank%d" % i) for i in range(8)]

        vscr_b = Buf("vscr")
        mixT_b = Buf("mixT")
        wsc_b = Buf("wscr")

        with ExitStack() as ph:
            def sb(name, shape, dt):
                return ph.enter_context(nc.sbuf_tensor("a_" + name, list(shape), dt))

            pbig = ph.enter_context(nc.psum_tensor("a_pbig", [128, 1024], F32))
            pbigB = ph.enter_context(nc.psum_tensor("a_pbigB", [128, 1024], F32))
            pbk = [ph.enter_context(nc.psum_tensor("a_pb%d" % i, [128, 512], F32)) for i in range(4, 8)]

            def bank(i):
                if i < 2:
                    return pbig[:, i * 512:(i + 1) * 512]
                if i < 4:
                    return pbigB[:, (i - 2) * 512:(i - 1) * 512]
                return pbk[i - 4][:, :]

            cst = sb("cst", [128, NCST], BF16)
            cst_b = Buf("cst")
            stg = [sb("stg%d" % i, [128, 4096], F32) for i in range(2)]
            stg_b = [Buf("stg%d" % i) for i in range(2)]
            xb = [sb("xb%d" % i, [128, 8, CH], BF16) for i in range(2)]
            xb_b = [Buf("xb%d" % i) for i in range(2)]
            wp = sb("wp", [128, 8, 768], BF16)
            wp_b = Buf("wp")
            KaT = sb("KaT", [128, S], BF16)
            KaT_c = [Buf("KaT%d" % i) for i in range(NV)]
            Va = sb("Va", [128, 64, 128], BF16)
            Va_c = [Buf("Va%d" % i) for i in range(NV)]
            KbT = sb("KbT", [128, MARG + S], BF16)
            KbT_c = [Buf("KbT%d" % i) for i in range(NV)]
            QaT = [sb("QaT%d" % i, [128, CH], BF16) for i in range(2)]
            QaT_b = [Buf("QaT%d" % i) for i in range(2)]
            QbT = [sb("QbT%d" % i, [128, CH], BF16) for i in range(2)]
            QbT_b = [Buf("QbT%d" % i) for i in range(2)]
            kvf = sb("kvf", [128, 64], F32)
            kvb = sb("kvb", [128, 64], BF16)
            kv_b = Buf("kv")
            vst = [sb("vst%d" % i, [128, 4, 130], BF16) for i in range(2)]
            vst_b = [Buf("vst%d" % i) for i in range(2)]
            vb1 = sb("vb1", [128, 5, 130], BF16)
            vb4 = sb("vb4", [128, 8, 130], BF16)
            vb16 = sb("vb16", [128, 32, 130], BF16)
            vb1_b, vb4_b, vb16_b = Buf("vb1"), Buf("vb4"), Buf("vb16")
            e_t = [sb("e_t%d" % i, [128, 2 * CH], F32) for i in range(2)]
            e_b = [Buf("e_t%d" % i) for i in range(2)]
            sp_t = [sb("sp_t%d" % i, [128, 2 * CH], BF16) for i in range(2)]
            sp_b = [Buf("sp_t%d" % i) for i in range(2)]
            w_t = [sb("w_t%d" % i, [128, 2 * CH], BF16) for i in range(2)]
            w_b = [Buf("w_t%d" % i) for i in range(2)]
            Rt = [sb("R%d" % i, [128, 2 * CH], BF16) for i in range(3)]
            R_b = [Buf("R%d" % i) for i in range(3)]
            osbj = sb("osbj", [128, CH], F32)
            osbj_b = Buf("osbj")
            lnvj = sb("lnvj", [128, CH], F32)
            rstdj = sb("rstdj", [128, CH], F32)
            mixtj = sb("mixtj", [128, CH], BF16)
            nj_b = Buf("normj")
            mixtj_b = Buf("mixtj")
            gsbp_t = sb("gsbp_t", [128, 4], F32)
            dtc = [sb("dt%d" % i, [128, CH], F32) for i in range(2)]
            dtc_b = [Buf("dt%d" % i) for i in range(2)]
            etc_ = [[sb("etc%d_%d" % (i, j), [128, CH], BF16) for j in range(2)] for i in range(2)]
            etc_b = [[Buf("etc%d_%d" % (i, j)) for j in range(2)] for i in range(2)]
            sqc = [sb("sqc%d" % i, [128, CH], BF16) for i in range(2)]
            sqc_b = [Buf("sqc%d" % i) for i in range(2)]
            lnvc = [sb("lnvc%d" % i, [64, CH], F32) for i in range(2)]
            lnvc_b = [Buf("lnvc%d" % i) for i in range(2)]
            rstdc = [sb("rstdc%d" % i, [64, CH], F32) for i in range(2)]
            rstdc_b = [Buf("rstdc%d" % i) for i in range(2)]
            mixt = [sb("mixt%d" % i, [64, CH], BF16) for i in range(2)]
            mixt_b = [Buf("mixt%d" % i) for i in range(2)]
            gsb_t = sb("gsb_t", [64, 8], F32)
            gdil_t = sb("gdil_t", [64, 8], F32)
            g_b = Buf("g")

            for i in range(2):
                c0 = i * 3680
                P.op("sync", lambda e, i=i, c0=c0: e.dma_start(out=stg[i][:, 0:3680], in_=cst_d[:, c0:c0 + 3680]),
                     writes=[stg_b[i]], dbuf=stg_b[i])
                P.op("vector", lambda e, i=i, c0=c0: e.tensor_copy(out=cst[:, c0:c0 + 3680], in_=stg[i][:, 0:3680]),
                     reads=[stg_b[i]], writes=[cst_b])
            P.op("sync", lambda e: e.dma_start(out=kvf[:, :], in_=kv_d[:, :]), writes=[kv_b], dbuf=kv_b)
            P.op("vector", lambda e: e.tensor_copy(out=kvb[:, :], in_=kvf[:, :]), reads=[kv_b], writes=[kv_b])
            P.op("sync", lambda e: e.dma_start(out=gsb_t[:, :], in_=gsb_d[:, :]), writes=[g_b], dbuf=g_b)
            P.op("sync", lambda e: e.dma_start(out=gdil_t[:, :], in_=gdil_d[:, :]), writes=[g_b], dbuf=g_b)
            P.op("sync", lambda e: e.dma_start(out=gsbp_t[:, :], in_=gsbp_d[:, :]), writes=[g_b], dbuf=g_b)
            P.op("gpsimd", lambda e: e.memset(KbT[:, :], 0.0), writes=list(KbT_c))
            P.op("gpsimd", lambda e: e.memset(vst[0][:, :, :], 0.0), writes=[vst_b[0]])
            for r0 in range(0, MARG + S, 512):
                P.op("sync", lambda e, r0=r0: e.dma_start(
                    out=vscr_d[r0:r0 + 512, :].rearrange("(t p) c -> p t c", p=128), in_=vst[0][:, :, :]),
                    reads=[vst_b[0]], writes=[vscr_b], dbuf=vst_b[0])

            xT_v = xT_d.rearrange("(c p) t -> p c t", p=128)
            win_v = win_d.rearrange("(c p) n -> p c n", p=128)
            ld_n = [0]

            tb = [sb("tb%d" % i, [128, DFF], BF16) for i in range(2)]
            tb_b = [Buf("tb%d" % i) for i in range(2)]
            w0 = []
            n0 = 0
            for (src, dst) in ((wg_d, wgs_d), (wu_d, wus_d)):
                dview = dst.rearrange("f p (c j) -> p f c j", c=8)
                for kc in range(8):
                    sl0 = n0 % 2
                    n0 += 1

                    def ld(sl0=sl0, src=src, kc=kc):
                        P.op("sync", lambda e: e.dma_start(out=stg[sl0][:, 0:DFF], in_=src[kc * 128:(kc + 1) * 128, :]),
                             writes=[stg_b[sl0]], dbuf=stg_b[sl0])

                    def cs_(sl0=sl0, dview=dview, kc=kc):
                        P.op("vector", lambda e: e.tensor_copy(out=tb[sl0][:, :], in_=stg[sl0][:, 0:DFF]),
                             reads=[stg_b[sl0]], writes=[tb_b[sl0]])
                        P.op("sync", lambda e: e.dma_start(
                            out=dview[:, :, kc, :], in_=tb[sl0][:, :].rearrange("p (f j) -> p f j", j=128)),
                            reads=[tb_b[sl0]], writes=[wsc_b], dbuf=tb_b[sl0])
                    w0.append((ld, cs_))
            for f0 in range(0, NF, 2):
                sl0 = n0 % 2
                n0 += 1

                def ld(sl0=sl0, f0=f0):
                    P.op("sync", lambda e: e.dma_start(
                        out=stg[sl0][:, 0:2048].rearrange("p (f n) -> p f n", f=2),
                        in_=wd_d[f0 * 128:(f0 + 2) * 128, :].rearrange("(f p) n -> p f n", p=128)),
                        writes=[stg_b[sl0]], dbuf=stg_b[sl0])

                def cs_(sl0=sl0, f0=f0):
                    P.op("vector", lambda e: e.tensor_copy(out=tb[sl0][:, 0:2048], in_=stg[sl0][:, 0:2048]),
                         reads=[stg_b[sl0]], writes=[tb_b[sl0]])
                    P.op("sync", lambda e: e.dma_start(
                        out=wds_d[f0:f0 + 2, :, :].rearrange("f p n -> p f n"),
                        in_=tb[sl0][:, 0:2048].rearrange("p (f n) -> p f n", f=2)),
                        reads=[tb_b[sl0]], writes=[wsc_b], dbuf=tb_b[sl0])
                w0.append((ld, cs_))
            def p0_group(pieces):
                out = []
                for i in range(len(pieces) + 1):
                    if i < len(pieces):
                        out.append(pieces[i][0])
                    if i >= 1:
                        out.append(pieces[i - 1][1])
                return out

            def norm_thunks(src, src_b, nrow, h_glob, is_dil, kown, gtile, ch=0, mbk=6):
                th = []
                lcol = C_MEANE if nrow == 65 else C_MEAN

                def t1():
                    P.op("scalar", lambda e: e.activation(out=sqc[ch][0:nrow, :], in_=src[0:nrow, :], func=AF.Square),
                         reads=[src_b], writes=[sqc_b[ch]])
                    P.op("tensor", lambda e: e.matmul(bank(mbk)[0:64, :], lhsT=cst[0:nrow, lcol:lcol + 64], rhs=sqc[ch][0:nrow, :],
                                                      start=True, stop=True),
                         reads=[sqc_b[ch], cst_b], writes=[PB[mbk]])

                def t2():
                    if nrow == 65:
                        P.op("scalar", lambda e: e.activation(out=lnvc[ch][:, :], in_=bank(mbk)[0:64, :], func=AF.Ln),
                             reads=[PB[mbk]], writes=[lnvc_b[ch]])
                    else:
                        P.op("scalar", lambda e: e.activation(out=lnvc[ch][:, :], in_=bank(mbk)[0:64, :], func=AF.Ln,
                                                              bias=RMS_EPS, scale=1.0),
                             reads=[PB[mbk]], writes=[lnvc_b[ch]])
                    P.op("scalar", lambda e: e.activation(out=rstdc[ch][:, :], in_=lnvc[ch][:, :], func=AF.Exp, scale=-0.5),
                         reads=[lnvc_b[ch]], writes=[rstdc_b[ch]])

                def t3():
                    ms = ld_n[0] % 2
                    ld_n[0] += 1
                    P.op("vector", lambda e, ms=ms: e.scalar_tensor_tensor(
                        out=mixt[ms][:, :], in0=src[0:64, :], scalar=gtile[0:64, h_glob:h_glob + 1], in1=rstdc[ch][:, :],
                        op0=ALU.mult, op1=ALU.mult),
                        reads=[src_b, rstdc_b[ch], g_b], writes=[mixt_b[ms]])
                    row0 = (512 if is_dil else 0) + h_glob * 64
                    P.op("sync", lambda e, ms=ms, row0=row0: e.dma_start(
                        out=mixT_d[row0:row0 + 64, kown * CH:(kown + 1) * CH], in_=mixt[ms][:, :]),
                        reads=[mixt_b[ms]], writes=[mixT_b], dbuf=mixt_b[ms])
                return [t1, t2, t3]

            def norm_joint_thunks(p, kown):
                def t1():
                    P.op("scalar", lambda e: e.activation(out=sqc[0][:, :], in_=osbj[:, :], func=AF.Square),
                         reads=[osbj_b], writes=[sqc_b[0]])
                    P.op("tensor", lambda e: e.matmul(bank(6), lhsT=cst[:, C_MEANB:C_MEANB + 128], rhs=sqc[0][:, :],
                                                      start=True, stop=True),
                         reads=[sqc_b[0], cst_b], writes=[PB[6]])

                def t2():
                    P.op("scalar", lambda e: e.activation(out=lnvj[:, :], in_=bank(6), func=AF.Ln, bias=RMS_EPS, scale=1.0),
                         reads=[PB[6]], writes=[nj_b])
                    P.op("scalar", lambda e: e.activation(out=rstdj[:, :], in_=lnvj[:, :], func=AF.Exp, scale=-0.5),
                         reads=[nj_b], writes=[nj_b])

                def t3():
                    P.op("vector", lambda e: e.scalar_tensor_tensor(
                        out=mixtj[:, :], in0=osbj[:, :], scalar=gsbp_t[:, p:p + 1], in1=rstdj[:, :],
                        op0=ALU.mult, op1=ALU.mult),
                        reads=[osbj_b, nj_b, g_b], writes=[mixtj_b])
                    P.op("sync", lambda e: e.dma_start(
                        out=mixT_d[p * 128:(p + 1) * 128, kown * CH:(kown + 1) * CH], in_=mixtj[:, :]),
                        reads=[mixtj_b], writes=[mixT_b], dbuf=mixtj_b)
                return [t1, t2, t3]

            def dil_thunks(v, p, QB, QB_b, kown):
                chains = []
                for hd in range(2):
                    th = []
                    sbk, abk = (6, 7) if (hd == 0 or INTERLEAVE_DIL) else (0, 1)
                    dt_, dt_b = dtc[hd], dtc_b[hd]
                    et, et_b = etc_[hd], etc_b[hd]
                    r = slice(hd * 64, hd * 64 + 64)
                    hg = 2 * p + hd
                    for pi in range(3):
                        ex = -(hg + 1) + 2 * pi
                        sid0 = C_SID + (ex + 8) * 128
                        np0 = C_NP12 if pi < 2 else (C_NP3A if v % 4 == 1 else C_NP3B)
                        G = 4 if pi < 2 else 16
                        W = 128 if pi < 2 else 32
                        for hb in range(2):
                            groups = list(range(hb * G // 2, (hb + 1) * G // 2))

                            def t1(hb=hb, groups=groups, sid0=sid0, np0=np0, pi=pi, W=W, r=r, sbk=sbk):
                                P.op("tensor", lambda e: e.matmul(
                                    bank(sbk), lhsT=cst[:, sid0:sid0 + 128], rhs=cst[:, np0 + hb * 512:np0 + (hb + 1) * 512],
                                    start=True, stop=False, skip_group_check=True),
                                    reads=[cst_b], writes=[PB[sbk]])
                                for g in groups:
                                    for half in range(2):
                                        col0 = g * 2 * W + half * W - hb * 512
                                        if pi == 0:
                                            blk = 4 * v + g - 1 + half
                                            k0 = MARG + blk * 128
                                            kap = KbT[r, k0:k0 + 128]
                                            qap = QB[r, g * 128:(g + 1) * 128]
                                            kbufs = [KbT_c[blk // 4]]
                                        elif pi == 1:
                                            k0 = MARG + (v - 1 + half) * 512 + g
                                            kap = KbT[r, k0:k0 + 509:4]
                                            qap = QB[r, g:g + 509:4]
                                            kbufs = [KbT_c[v - 1 + half]]
                                        else:
                                            U = v // 4 - 1 + half
                                            k0 = MARG + U * 2048 + g
                                            kap = KbT[r, k0:k0 + 2033:16]
                                            qap = QB[r, g:g + 497:16]
                                            kbufs = [KbT_c[c] for c in range(4 * U, 4 * U + 4) if 0 <= c <= v]
                                        P.op("tensor", lambda e, kap=kap, qap=qap, col0=col0: e.matmul(
                                            bank(sbk)[:, col0:col0 + W], lhsT=kap, rhs=qap, start=False, stop=False,
                                            skip_group_check=True),
                                            reads=kbufs + [QB_b], writes=[PB[sbk]])

                            def t2(hb=hb, sbk=sbk, et=et, et_b=et_b):
                                P.op("scalar", lambda e: e.activation(out=et[hb][:, :], in_=bank(sbk), func=AF.Exp),
                                     reads=[PB[sbk]], writes=[et_b[hb]])

                            def t3(hb=hb, groups=groups, pi=pi, W=W, hd=hd, abk=abk, et=et, et_b=et_b):
                                first = (hb == 0)
                                for g in groups:
                                    for half in range(2):
                                        col0 = g * 2 * W + half * W - hb * 512
                                        if pi == 0:
                                            vt = vb1[:, g + half, hd * 65:(hd + 1) * 65]
                                            vbuf = vb1_b
                                        elif pi == 1:
                                            vt = vb4[:, half * 4 + g, hd * 65:(hd + 1) * 65]
                                            vbuf = vb4_b
                                        else:
                                            vt = vb16[:, half * 16 + g, hd * 65:(hd + 1) * 65]
                                            vbuf = vb16_b
                                        P.op("tensor", lambda e, vt=vt, col0=col0, g=g, first=first: e.matmul(
                                            bank(abk)[0:65, g * W:(g + 1) * W], lhsT=vt, rhs=et[hb][:, col0:col0 + W],
                                            start=first, stop=False, skip_group_check=True),
                                            reads=[vbuf, et_b[hb]], writes=[PB[abk]])
                                        first = False
                            th += [t1, t2, t3]

                        def t4(pi=pi, abk=abk, dt_=dt_, dt_b=dt_b):
                            if pi == 0:
                                P.op("vector", lambda e: e.tensor_copy(out=dt_[0:65, :], in_=bank(abk)[0:65, :]),
                                     reads=[PB[abk]], writes=[dt_b])
                            else:
                                rr = 4 if pi == 1 else 16
                                P.op("vector", lambda e: e.tensor_tensor(
                                    out=dt_[0:65, :].rearrange("p (u r) -> p u r", r=rr),
                                    in0=dt_[0:65, :].rearrange("p (u r) -> p u r", r=rr),
                                    in1=bank(abk)[0:65, :].rearrange("p (r u) -> p u r", r=rr), op=ALU.add),
                                    reads=[PB[abk], dt_b], writes=[dt_b])
                        th.append(t4)
                    th += norm_thunks(dt_, dt_b, 65, hg, True, kown, gdil_t, ch=hd, mbk=sbk)
                    chains.append(th)
                if not ZIP_DIL:
                    return chains[0] + chains[1]
                out = []
                for i in range(max(len(c) for c in chains)):
                    for c in chains:
                        if i < len(c):
                            out.append(c[i])
                return out

            def load_x(v):
                s = v % 2
                P.op("sync", lambda e, s=s, v=v: e.dma_start(
                    out=stg[s][:, :].rearrange("p (c t) -> p c t", c=8), in_=xT_v[:, :, v * CH:(v + 1) * CH]),
                    writes=[stg_b[s]], dbuf=stg_b[s])
                P.op("vector", lambda e, s=s: e.tensor_copy(
                    out=xb[s][:, :, :], in_=stg[s][:, :].rearrange("p (c t) -> p c t", c=8)),
                    reads=[stg_b[s]], writes=[xb_b[s]])

            pj_n = [0]

            def proj_thunks(v):
                s = v % 2
                own = (v % 2 == 1)
                qs = ((v - 1) // 2) % 2
                th = []

                def fm(j, evac):
                    bk = 5 if INTERLEAVE_DIL else 6 + pj_n[0] % 2
                    pj_n[0] += 1
                    for k0 in (0, 4):
                        def t_(k0=k0, bk=bk):
                            for kc in range(k0, k0 + 4):
                                P.op("tensor", lambda e, kc=kc: e.matmul(
                                    bank(bk), lhsT=wp[:, kc, j * 128:(j + 1) * 128], rhs=xb[s][:, kc, :],
                                    start=(kc == 0), stop=(kc == 7)),
                                    reads=[wp_b, xb_b[s]], writes=[PB[bk]])
                        th.append(t_)
                    th.append(lambda bk=bk: evac(bk))

                def tm(j, evac):
                    bk = 5 if INTERLEAVE_DIL else 6 + pj_n[0] % 2
                    pj_n[0] += 1
                    for t in range(4):
                        def t_(t=t, bk=bk):
                            for kc in range(8):
                                P.op("tensor", lambda e, kc=kc: e.matmul(
                                    bank(bk)[:, t * 128:(t + 1) * 128], lhsT=xb[s][:, kc, t * 128:(t + 1) * 128],
                                    rhs=wp[:, kc, j * 128:(j + 1) * 128], start=(kc == 0), stop=(kc == 7)),
                                    reads=[wp_b, xb_b[s]], writes=[PB[bk]])
                        th.append(t_)
                    th.append(lambda bk=bk: evac(bk))

                fm(1, lambda bk: P.op("vector", lambda e: e.tensor_copy(out=KaT[:, v * CH:(v + 1) * CH], in_=bank(bk)),
                                      reads=[PB[bk]], writes=[KaT_c[v]]))
                fm(4, lambda bk: P.op("vector", lambda e: e.tensor_copy(
                    out=KbT[:, MARG + v * CH:MARG + (v + 1) * CH], in_=bank(bk)), reads=[PB[bk]], writes=[KbT_c[v]]))
                tm(2, lambda bk: P.op("vector", lambda e: e.tensor_copy(
                    out=Va[:, v * 4:(v + 1) * 4, :], in_=bank(bk).rearrange("p (t n) -> p t n", t=4)),
                    reads=[PB[bk]], writes=[Va_c[v]]))

                def evac_vb(bk):
                    P.op("vector", lambda e: e.tensor_copy(
                        out=vst[s][:, :, :].rearrange("p t (h c) -> p t h c", h=2)[:, :, :, 0:64],
                        in_=bank(bk).rearrange("p (t h c) -> p t h c", t=4, h=2)),
                        reads=[PB[bk]], writes=[vst_b[s]])
                    for hc in (64, 129):
                        P.op("vector", lambda e, hc=hc: e.tensor_copy(
                            out=vst[s][:, :, hc:hc + 1], in_=kvb[:, v * 4:(v + 1) * 4].rearrange("p (t o) -> p t o", o=1)),
                            reads=[kv_b], writes=[vst_b[s]])
                    r0 = MARG + v * CH
                    P.op("sync", lambda e: e.dma_start(
                        out=vscr_d[r0:r0 + 512, :].rearrange("(t p) c -> p t c", p=128), in_=vst[s][:, :, :]),
                        reads=[vst_b[s]], writes=[vscr_b], dbuf=vst_b[s])
                tm(5, evac_vb)
                if own:
                    fm(0, lambda bk: P.op("vector", lambda e: e.tensor_scalar(
                        out=QaT[qs][:, :], in0=bank(bk), scalar1=0.125, scalar2=None, op0=ALU.mult),
                        reads=[PB[bk]], writes=[QaT_b[qs]]))
                    fm(3, lambda bk: P.op("vector", lambda e: e.tensor_scalar(
                        out=QbT[qs][:, :], in0=bank(bk), scalar1=0.125, scalar2=None, op0=ALU.mult),
                        reads=[PB[bk]], writes=[QbT_b[qs]]))
                return th

            def pass_prologue(p):
                th = []
                for j, base in enumerate((0, 512, 1024, 1536, 2048, 2560)):
                    def t_(j=j, base=base):
                        s = ld_n[0] % 2
                        ld_n[0] += 1
                        c0 = base + p * 128
                        P.op("sync", lambda e: e.dma_start(
                            out=stg[s][:, 0:1024].rearrange("p (c n) -> p c n", c=8), in_=win_v[:, :, c0:c0 + 128]),
                            writes=[stg_b[s]], dbuf=stg_b[s])
                        P.op("vector", lambda e: e.tensor_copy(
                            out=wp[:, :, j * 128:(j + 1) * 128], in_=stg[s][:, 0:1024].rearrange("p (c n) -> p c n", c=8)),
                            reads=[stg_b[s]], writes=[wp_b])
                    th.append(t_)
                th.append(lambda: load_x(0))
                th.append(lambda: load_x(1))
                return th

            for p in range(4):
                if p == 0:
                    for t_ in pass_prologue(0):
                        t_()
                for t_ in proj_thunks(0) + proj_thunks(1):
                    t_()
                carry = []
                for v in range(1, NV, 2):
                    kown = (v - 1) // 2
                    qs = kown % 2
                    QA, QB = QaT[qs], QbT[qs]
                    QA_b, QB_b = QaT_b[qs], QbT_b[qs]
                    b1 = MARG + (4 * v - 1) * 128
                    P.op("sync", lambda e, b1=b1: e.dma_start(
                        out=vb1[:, :, :], in_=vscr_d[b1:b1 + 640, :].rearrange("(t p) c -> p t c", p=128)),
                        reads=[vscr_b], writes=[vb1_b], dbuf=vb1_b)
                    for half in range(2):
                        b4 = MARG + (v - 1 + half) * 512
                        P.op("sync", lambda e, b4=b4, half=half: e.dma_start(
                            out=vb4[:, half * 4:(half + 1) * 4, :],
                            in_=vscr_d[b4:b4 + 512, :].rearrange("(s r) c -> s r c", r=4)),
                            reads=[vscr_b], writes=[vb4_b], dbuf=vb4_b)
                        b16 = MARG + (v // 4 - 1 + half) * 2048
                        P.op("sync", lambda e, b16=b16, half=half: e.dma_start(
                            out=vb16[:, half * 16:(half + 1) * 16, :],
                            in_=vscr_d[b16:b16 + 2048, :].rearrange("(s r) c -> s r c", r=16)),
                            reads=[vscr_b], writes=[vb16_b], dbuf=vb16_b)
                    side = carry
                    carry = []
                    L1 = []
                    if v + 2 < NV:
                        load_x(v + 1)
                        load_x(v + 2)
                        L1 = proj_thunks(v + 1) + proj_thunks(v + 2)
                    if v + 2 >= NV and p < 3:
                        side = side + pass_prologue(p + 1)
                    if p == 0:
                        take = len(w0) if v + 2 >= NV else min(len(w0), 4)
                        side = side + p0_group(w0[:take])
                        w0 = w0[take:]
                    side_b = dil_thunks(v, p, QB, QB_b, kown)
                    if INTERLEAVE_DIL:
                        L2 = side_b
                        side_b = []
                        zz = []
                        for i_ in range(max(len(L1), len(L2))):
                            if i_ < len(L1):
                                zz.append(L1[i_])
                            if i_ < len(L2):
                                zz.append(L2[i_])
                        side = side + zz
                    else:
                        side = side + L1

                    nkb = 4 * (v + 1)

                    def st_A(i, b0, last):
                        kb = nkb - 1 - i
                        diag = kb >= 4 * v
                        mb = kb - 4 * v
                        ks = slice(kb * 128, (kb + 1) * 128)
                        for hd in range(2):
                            r = slice(hd * 64, hd * 64 + 64)
                            bk = b0 + hd
                            P.op("tensor", lambda e, bk=bk, r=r, ks=ks, diag=diag, last=last, QA=QA: e.matmul(
                                bank(bk), lhsT=KaT[r, ks], rhs=QA[r, :], start=True, stop=(last and not diag)),
                                reads=[KaT_c[kb // 4], QA_b], writes=[PB[bk]])
                            if diag:
                                P.op("tensor", lambda e, bk=bk, mb=mb, last=last: e.matmul(
                                    bank(bk), lhsT=cst[:, C_ID:C_ID + 128],
                                    rhs=cst[:, C_MASK + mb * 512:C_MASK + (mb + 1) * 512], start=False, stop=last),
                                    reads=[cst_b], writes=[PB[bk]])

                    def st_act1(i):
                        sl = i % 2
                        P.op("scalar", lambda e, sl=sl: e.activation(out=e_t[sl][:, :], in_=pbig[:, :], func=AF.Exp),
                             reads=[PB[0], PB[1]], writes=[e_b[sl]])
                        P.op("scalar", lambda e, sl=sl: e.activation(out=sp_t[sl][:, :], in_=e_t[sl][:, :], func=AF.Ln,
                                                                     bias=1.0, scale=1.0),
                             reads=[e_b[sl]], writes=[sp_b[sl]])

                    def st_R(i):
                        sl = i % 2
                        if i == 0:
                            P.op("vector", lambda e, sl=sl: e.tensor_copy(out=Rt[1][:, :], in_=sp_t[sl][:, :]),
                                 reads=[sp_b[sl]], writes=[R_b[1]])
                        elif i < nkb - 1:
                            P.op("vector", lambda e, sl=sl, i=i: e.tensor_tensor(
                                out=Rt[(i + 1) % 3][:, :], in0=Rt[i % 3][:, :], in1=sp_t[sl][:, :], op=ALU.add),
                                reads=[sp_b[sl], R_b[i % 3]], writes=[R_b[(i + 1) % 3]])

                    def st_B(i):
                        sl = i % 2
                        st_A(i, 2, False)
                        for hd in range(2):
                            cs = slice(hd * CH, (hd + 1) * CH)
                            P.op("tensor", lambda e, sl=sl, i=i, hd=hd, cs=cs: e.matmul(
                                bank(2 + hd), lhsT=cst[:, C_NTRI:C_NTRI + 128], rhs=sp_t[sl][:, cs],
                                start=False, stop=(i == 0)),
                                reads=[cst_b, sp_b[sl]], writes=[PB[2 + hd]])
                            if i > 0:
                                P.op("tensor", lambda e, sl=sl, i=i, hd=hd, cs=cs: e.matmul(
                                    bank(2 + hd), lhsT=cst[:, C_NONES:C_NONES + 128], rhs=Rt[i % 3][:, cs],
                                    start=False, stop=True),
                                    reads=[cst_b, R_b[i % 3]], writes=[PB[2 + hd]])

                    def st_act2(i):
                        sl = i % 2
                        P.op("scalar", lambda e, sl=sl: e.activation(out=w_t[sl][:, :], in_=pbigB[:, :], func=AF.Exp),
                             reads=[PB[2], PB[3]], writes=[w_b[sl]])

                    def st_AV(i):
                        sl = i % 2
                        kb = nkb - 1 - i
                        for hd in range(2):
                            cs = slice(hd * CH, (hd + 1) * CH)
                            P.op("tensor", lambda e, sl=sl, kb=kb, hd=hd, i=i, cs=cs, nkb=nkb: e.matmul(
                                bank(4)[hd * 64:(hd + 1) * 64, :], lhsT=Va[:, kb, hd * 64:(hd + 1) * 64], rhs=w_t[sl][:, cs],
                                start=(i == 0), stop=(i == nkb - 1)),
                                reads=[Va_c[kb // 4], w_b[sl]], writes=[PB[4]])

                    nside = len(side)
                    per = -(-nside // max(1, nkb - 2)) if nside else 0
                    si = 0
                    for st in range(nkb + 2):
                        if st < nkb:
                            st_A(st, 0, True)
                            st_act1(st)
                            st_R(st)
                        if 0 <= st - 1 < nkb:
                            st_B(st - 1)
                            st_act2(st - 1)
                        if 0 <= st - 2 < nkb:
                            st_AV(st - 2)
                        for _ in range(per):
                            if si < nside:
                                side[si]()
                                si += 1
                    while si < nside:
                        side[si]()
                        si += 1
                    for t_ in side_b:
                        t_()
                    P.op("vector", lambda e: e.tensor_copy(out=osbj[:, :], in_=bank(4)),
                         reads=[PB[4]], writes=[osbj_b])
                    carry += norm_joint_thunks(p, kown)
                    if v + 2 >= NV:
                        for t_ in carry:
                            t_()
                        carry = []
            P.flush()

        with ExitStack() as ph:
            def sb(name, shape, dt):
                return ph.enter_context(nc.sbuf_tensor("b_" + name, list(shape), dt))

            pbig = ph.enter_context(nc.psum_tensor("b_pbig", [128, 1024], F32))
            pbk = [ph.enter_context(nc.psum_tensor("b_pb%d" % i, [128, 512], F32)) for i in range(2, 7)]
            pT = ph.enter_context(nc.psum_tensor("b_pT", [128, 1024], BF16))

            def bank(i):
                if i < 2:
                    return pbig[:, i * 512:(i + 1) * 512]
                return pbk[i - 2][:, :]

            idt = sb("idt", [128, 128], BF16)
            idf = sb("idf", [128, 128], F32)
            id_b = Buf("id")
            stg = [sb("stg2_%d" % i, [128, 2048], F32) for i in range(2)]
            stg_b = [Buf("stg2_%d" % i) for i in range(2)]
            wo = sb("wo", [128, 8, D], BF16)
            wo_b = Buf("wo")
            wdr = sb("wdr", [128, NF, D], BF16)
            wdr_b = Buf("wdr")
            lnp = [sb("lnp%d" % i, [128, D], F32) for i in range(4)]
            lnp_b = Buf("lnp")
            wgt = [sb("wgt%d" % i, [128, 8, 128], BF16) for i in range(3)]
            wut = [sb("wut%d" % i, [128, 8, 128], BF16) for i in range(3)]
            wgt_b = [Buf("wgt%d" % i) for i in range(3)]
            wut_b = [Buf("wut%d" % i) for i in range(3)]
            mxs = sb("mxs", [128, 8, CH], BF16)
            mxs_b = Buf("mxs")
            xt = [sb("xt%d" % i, [128, D], F32) for i in range(2)]
            xt_b = [Buf("xt%d" % i) for i in range(2)]
            hhs = [sb("hh%d" % i, [128, D], F32) for i in range(2)]
            hhs_b = [Buf("hh%d" % i) for i in range(2)]
            h1 = sb("h1", [128, 4, D], F32)
            h1_b = [Buf("h1_%d" % i) for i in range(4)]
            h1bs = [sb("h1b%d" % i, [128, D], BF16) for i in range(2)]
            h1bs_b = [Buf("h1b%d" % i) for i in range(2)]
            h1T = sb("h1T", [128, 8, CH], BF16)
            h1T_b = Buf("h1T")
            sg = [sb("sg%d" % i, [128, CH], F32) for i in range(2)]
            sg_b = [Buf("sg%d" % i) for i in range(2)]
            aT = sb("aT", [128, NF, CH], BF16)
            aT_b = Buf("aT")
            ot = [sb("ot%d" % i, [128, D], F32) for i in range(2)]
            ot_b = [Buf("ot%d" % i) for i in range(2)]
            st6 = [sb("st6_%d" % i, [128, 12], F32) for i in range(2)]
            mv = [sb("mv%d" % i, [128, 2], F32) for i in range(2)]
            rs = [sb("rs%d" % i, [128, 1], F32) for i in range(2)]
            st_b = [Buf("stats%d" % i) for i in range(2)]

            P.op("sync", lambda e: e.dma_start(out=idf[:, :], in_=cst_d[:, C_ID:C_ID + 128]), writes=[id_b], dbuf=id_b)
            P.op("vector", lambda e: e.tensor_copy(out=idt[:, :], in_=idf[:, :]), reads=[id_b], writes=[id_b])
            for i, src in enumerate((ln1g_d, ln1b_d, ln2g_d, ln2b_d)):
                P.op("sync", lambda e, i=i, src=src: e.dma_start(out=lnp[i][:, :], in_=src[:, :]),
                     writes=[lnp_b], dbuf=lnp_b)
            wout_v = wout_d.rearrange("(c p) n -> p c n", p=128)
            n = 0
            for c0 in range(0, 8, 2):
                s = n % 2
                n += 1
                P.op("sync", lambda e, s=s, c0=c0: e.dma_start(
                    out=stg[s][:, :].rearrange("p (c n) -> p c n", c=2), in_=wout_v[:, c0:c0 + 2, :]),
                    writes=[stg_b[s]], dbuf=stg_b[s])
                P.op("gpsimd", lambda e, s=s, c0=c0: e.tensor_copy(
                    out=wo[:, c0:c0 + 2, :], in_=stg[s][:, :].rearrange("p (c n) -> p c n", c=2)),
                    reads=[stg_b[s]], writes=[wo_b])
            P.op("sync", lambda e: e.dma_start(out=wdr[:, :, :], in_=wds_d.rearrange("f p n -> p f n")),
                 reads=[wsc_b], writes=[wdr_b], dbuf=wdr_b)

            def layer_norm(src, src_b, gi, dst, dst_b, sl):
                for c in range(2):
                    P.op("vector", lambda e, c=c: e.bn_stats(out=st6[sl][:, c * 6:(c + 1) * 6], in_=src[:, c * 512:(c + 1) * 512]),
                         reads=[src_b], writes=[st_b[sl]])
                P.op("vector", lambda e: e.bn_aggr(out=mv[sl][:, :], in_=st6[sl][:, :]), reads=[st_b[sl]], writes=[st_b[sl]])
                P.op("scalar", lambda e: e.activation(out=rs[sl][:, :], in_=mv[sl][:, 1:2], func=AF.Sqrt, bias=LN_EPS, scale=1.0),
                     reads=[st_b[sl]], writes=[st_b[sl]])
                P.op("vector", lambda e: e.reciprocal(out=rs[sl][:, :], in_=rs[sl][:, :]), reads=[st_b[sl]], writes=[st_b[sl]])
                P.op("vector", lambda e: e.scalar_tensor_tensor(out=src[:, :], in0=src[:, :], scalar=mv[sl][:, 0:1],
                                                                in1=lnp[gi][:, :], op0=ALU.subtract, op1=ALU.mult),
                     reads=[src_b, st_b[sl], lnp_b], writes=[src_b])
                P.op("vector", lambda e: e.scalar_tensor_tensor(out=dst, in0=src[:, :], scalar=rs[sl][:, 0:1],
                                                                in1=lnp[gi + 1][:, :], op0=ALU.mult, op1=ALU.add),
                     reads=[src_b, st_b[sl], lnp_b], writes=[dst_b])

            mix_v = mixT_d.rearrange("(c p) t -> p c t", p=128)
            wn = 0
            for k in range(8):
                P.op("sync", lambda e, k=k: e.dma_start(out=mxs[:, :, :], in_=mix_v[:, :, k * CH:(k + 1) * CH]),
                     reads=[mixT_b], writes=[mxs_b], dbuf=mxs_b)
                for t in range(4):
                    tok0 = (k * 4 + t) * 128
                    xs_ = (k * 4 + t) % 2
                    P.op("sync", lambda e, xs_=xs_, tok0=tok0: e.dma_start(out=xt[xs_][:, :], in_=xn_d[tok0:tok0 + 128, :]),
                         writes=[xt_b[xs_]], dbuf=xt_b[xs_])
                    for half in range(2):
                        bk = 2 + half
                        for kc in range(8):
                            P.op("tensor", lambda e, kc=kc, t=t, half=half, bk=bk: e.matmul(
                                bank(bk), lhsT=mxs[:, kc, t * 128:(t + 1) * 128], rhs=wo[:, kc, half * 512:(half + 1) * 512],
                                start=(kc == 0), stop=(kc == 7)),
                                reads=[mxs_b, wo_b], writes=[PB[bk]])
                        P.op("vector", lambda e, xs_=xs_, half=half, bk=bk: e.scalar_tensor_tensor(
                            out=hhs[xs_][:, half * 512:(half + 1) * 512], in0=xt[xs_][:, half * 512:(half + 1) * 512], scalar=ALPHA,
                            in1=bank(bk), op0=ALU.mult, op1=ALU.add),
                            reads=[xt_b[xs_], PB[bk]], writes=[hhs_b[xs_]])
                    layer_norm(hhs[xs_], hhs_b[xs_], 0, h1[:, t, :], h1_b[t], xs_)
                    P.op("scalar", lambda e, t=t, xs_=xs_: e.activation(out=h1bs[xs_][:, :], in_=h1[:, t, :], func=AF.Copy),
                         reads=[h1_b[t]], writes=[h1bs_b[xs_]])
                    for kc in range(8):
                        P.op("tensor", lambda e, kc=kc, xs_=xs_: e.transpose(
                            out=pT[:, kc * 128:(kc + 1) * 128], in_=h1bs[xs_][:, kc * 128:(kc + 1) * 128], identity=idt[:, :]),
                            reads=[h1bs_b[xs_], id_b], writes=[PB[7]])
                    P.op("vector", lambda e, t=t: e.tensor_copy(
                        out=h1T[:, :, t * 128:(t + 1) * 128], in_=pT[:, :].rearrange("p (c n) -> p c n", c=8)),
                        reads=[PB[7]], writes=[h1T_b])
                for f in range(NF):
                    ws = wn % 3
                    wn += 1
                    P.op("sync", lambda e, ws=ws, f=f: e.dma_start(
                        out=wgt[ws][:, :, :], in_=wgs_d[f, :, :].rearrange("p (c j) -> p c j", c=8)),
                        reads=[wsc_b], writes=[wgt_b[ws]], dbuf=wgt_b[ws])
                    P.op("sync", lambda e, ws=ws, f=f: e.dma_start(
                        out=wut[ws][:, :, :], in_=wus_d[f, :, :].rearrange("p (c j) -> p c j", c=8)),
                        reads=[wsc_b], writes=[wut_b[ws]], dbuf=wut_b[ws])
                    gb = 0 + (f % 2)
                    ub = 4 + (f % 2)
                    for kc in range(8):
                        P.op("tensor", lambda e, kc=kc, ws=ws, gb=gb: e.matmul(
                            bank(gb), lhsT=wgt[ws][:, kc, :], rhs=h1T[:, kc, :], start=(kc == 0), stop=(kc == 7)),
                            reads=[wgt_b[ws], h1T_b], writes=[PB[gb]])
                    for kc in range(8):
                        P.op("tensor", lambda e, kc=kc, ws=ws, ub=ub: e.matmul(
                            bank(ub), lhsT=wut[ws][:, kc, :], rhs=h1T[:, kc, :], start=(kc == 0), stop=(kc == 7)),
                            reads=[wut_b[ws], h1T_b], writes=[PB[ub]])
                    ss = f % 2
                    P.op("scalar", lambda e, ss=ss, gb=gb: e.activation(out=sg[ss][:, :], in_=bank(gb), func=AF.Silu),
                         reads=[PB[gb]], writes=[sg_b[ss]])
                    P.op("vector", lambda e, ss=ss, ub=ub, f=f: e.tensor_tensor(
                        out=aT[:, f, :], in0=sg[ss][:, :], in1=bank(ub), op=ALU.mult),
                        reads=[sg_b[ss], PB[ub]], writes=[aT_b])
                for t in range(4):
                    tok0 = (k * 4 + t) * 128
                    for half in range(2):
                        bk = 2 + half
                        for f in range(NF):
                            P.op("tensor", lambda e, f=f, t=t, half=half, bk=bk: e.matmul(
                                bank(bk), lhsT=aT[:, f, t * 128:(t + 1) * 128], rhs=wdr[:, f, half * 512:(half + 1) * 512],
                                start=(f == 0), stop=(f == NF - 1)),
                                reads=[aT_b, wdr_b], writes=[PB[bk]])
                        os_ = (k * 4 + t) % 2
                        P.op("vector", lambda e, t=t, half=half, bk=bk, os_=os_: e.scalar_tensor_tensor(
                            out=hhs[os_][:, half * 512:(half + 1) * 512], in0=h1[:, t, half * 512:(half + 1) * 512], scalar=ALPHA,
                            in1=bank(bk), op0=ALU.mult, op1=ALU.add),
                            reads=[h1_b[t], PB[bk]], writes=[hhs_b[os_]])
                    os_ = (k * 4 + t) % 2
                    layer_norm(hhs[os_], hhs_b[os_], 2, ot[os_][:, :], ot_b[os_], os_)
                    P.op("sync", lambda e, os_=os_, tok0=tok0: e.dma_start(out=out_d[tok0:tok0 + 128, :], in_=ot[os_][:, :]),
                         reads=[ot_b[os_]], dbuf=ot_b[os_])
            P.flush(final=True)
    return nc


_NC = None


def kernel(x, w_in, g_sb, g_dil, w_out, ln1_g, ln1_b, w_gate, w_up, w_down, ln2_g, ln2_b):
    global _NC
    x = np.asarray(x, np.float32)
    if _NC is None:
        _NC = build_nc()
    nc = _NC
    cst = _consts()
    f = lambda a: np.ascontiguousarray(np.asarray(a, np.float32))
    shared = {
        "cst": cst,
        "w_in": f(w_in[0]), "w_out": f(w_out[0]), "w_gate": f(w_gate[0]), "w_up": f(w_up[0]), "w_down": f(w_down[0]),
        "gsb": f(np.asarray(g_sb[0]).reshape(8, 64).T), "gdil": f(np.asarray(g_dil[0]).reshape(8, 64).T),
        "gsbp": f(np.asarray(g_sb[0]).reshape(4, 128).T),
        "ln1g": f(np.broadcast_to(np.asarray(ln1_g[0]), (128, D))), "ln1b": f(np.broadcast_to(np.asarray(ln1_b[0]), (128, D))),
        "ln2g": f(np.broadcast_to(np.asarray(ln2_g[0]), (128, D))), "ln2b": f(np.broadcast_to(np.asarray(ln2_b[0]), (128, D))),
    }
    in_maps = []
    for c in range(8):
        b, par = c // 2, c % 2
        xb_ = x[b]
        kvv = np.ones(S, np.float32)
        if par == 0:
            xv = np.concatenate([np.zeros((CH, D), np.float32), xb_[:S - CH]], axis=0)
            kvv[:CH] = 0.0
        else:
            xv = xb_
        m = dict(shared)
        m["xT"] = np.ascontiguousarray(xv.T)
        m["xn"] = np.ascontiguousarray(xb_.reshape(NV, CH, D)[par::2].reshape(S // 2, D))
        m["kv"] = np.ascontiguousarray(kvv.reshape(64, 128).T)
        in_maps.append(m)
    res = run_bass_kernel_spmd(nc, in_maps, core_ids=list(range(8)))
    out = np.empty((NB, S, D), np.float32)
    for c in range(8):
        b, par = c // 2, c % 2
        out[b].reshape(NV, CH, D)[par::2] = np.asarray(res.results[c]["out"], np.float32).reshape(8, CH, D)
    return out
```

```python
import numpy as np
from contextlib import ExitStack

import concourse.bass as bass
import concourse.mybir as mybir
from concourse.bass_utils import run_bass_kernel_spmd

F32 = mybir.dt.float32
BF16 = mybir.dt.bfloat16
AF = mybir.ActivationFunctionType
ALU = mybir.AluOpType

D = 1024
S = 8192
NB = 4
DFF = 2816
NF = DFF // 128
CH = 512
NV = S // CH
ALPHA = 2.0 ** 0.25
LN_EPS = 1e-5
RMS_EPS = 1e-6
MARG = 2048
NEG = -30000.0
INTERLEAVE_DIL = True
ZIP_DIL = False
BIGN = 131072.0

C_ID = 0
C_NTRI = 128
C_NONES = 256
C_MASK = 384
C_NP12 = C_MASK + 4 * 512
C_NP3A = C_NP12 + 1024
C_NP3B = C_NP3A + 1024
C_SID = C_NP3B + 1024
C_MEAN = C_SID + 12 * 128
C_ONES = C_MEAN + 64
C_MEANE = C_ONES + 64
C_MEANB = C_MEANE + 64
NCST = C_MEANB + 128

ENGS = ("sync", "tensor", "scalar", "vector", "gpsimd")


def _consts():
    c = np.zeros((128, NCST), np.float32)
    j = np.arange(128)[:, None]
    s = np.arange(128)[None, :]
    c[:, C_ID:C_ID + 128] = (j == s)
    c[:, C_NTRI:C_NTRI + 128] = -(j >= s).astype(np.float32)
    c[:, C_NONES:C_NONES + 128] = -1.0
    q = np.arange(512)[None, :]
    for mb in range(4):
        c[:, C_MASK + mb * 512:C_MASK + (mb + 1) * 512] = np.where(mb * 128 + j >= q, NEG, 0.0)

    def npat(G, W, qs_of):
        out = np.zeros((128, 1024), np.float32)
        for g in range(G):
            for half in range(2):
                for qi in range(W):
                    qs = qs_of(qi)
                    col = g * 2 * W + half * W + qi
                    if half == 1:
                        n = qs - np.arange(128)
                    else:
                        n = qs - np.arange(128) + 128
                    ok = (n >= 0) & (n <= 128)
                    out[:, col] = np.where(ok, n, BIGN)
        return out

    c[:, C_NP12:C_NP12 + 1024] = npat(4, 128, lambda qi: qi)
    c[:, C_NP3A:C_NP3A + 1024] = npat(16, 32, lambda qi: 32 + qi)
    c[:, C_NP3B:C_NP3B + 1024] = npat(16, 32, lambda qi: 96 + qi)
    for e in range(-8, 4):
        c[:, C_SID + (e + 8) * 128:C_SID + (e + 9) * 128] = -(2.0 ** e) * (j == s)
    c[:, C_MEAN:C_MEAN + 64] = 1.0 / 64.0
    c[:, C_ONES:C_ONES + 64] = 1.0
    c[0:64, C_MEANE:C_MEANE + 64] = 1.0 / 64.0
    c[64, C_MEANE:C_MEANE + 64] = RMS_EPS
    c[:, C_MEANB:C_MEANB + 128] = ((j // 64) == (s // 64)) / 64.0
    return c


class Buf:
    __slots__ = ("name", "lw", "rd", "rdd", "sem", "dcnt", "uid")
    _n = [0]

    def __init__(self, name):
        Buf._n[0] += 1
        self.uid = Buf._n[0]
        self.name = name
        self.lw = None
        self.rd = {}
        self.rdd = []
        self.sem = None
        self.dcnt = 0


class Op:
    __slots__ = ("eng", "fn", "deps", "needed", "val", "dbuf", "flushed")

    def __init__(self, eng, fn, dbuf):
        self.eng = eng
        self.fn = fn
        self.deps = []
        self.needed = False
        self.val = None
        self.dbuf = dbuf
        self.flushed = False


class Prog:
    def __init__(self, nc, esems, dsems):
        self.nc = nc
        self.esems = esems
        self.dsems = list(dsems)
        self.ops = {e: [] for e in ENGS}
        self.cnt = {e: 0 for e in ENGS}
        self.waited = {e: {} for e in ENGS}
        self.dma_ops = []

    def op(self, eng, fn, reads=(), writes=(), dbuf=None):
        o = Op(eng, fn, dbuf)
        deps = {}
        for b in reads:
            if b.lw is not None:
                deps[id(b.lw)] = b.lw
        for b in writes:
            if b.lw is not None:
                deps[id(b.lw)] = b.lw
            for r in b.rd.values():
                deps[id(r)] = r
            for r in b.rdd:
                deps[id(r)] = r
        for d in deps.values():
            if d is o:
                continue
            if d.dbuf is None and d.eng == "tensor" and eng == "tensor":
                continue
            o.deps.append(d)
            d.needed = True
        for b in reads:
            if dbuf is not None:
                b.rdd.append(o)
            else:
                b.rd[eng] = o
        for b in writes:
            b.lw = o
            b.rd = {}
            b.rdd = []
        if dbuf is not None:
            if dbuf.sem is None:
                dbuf.sem = self.dsems.pop()
            dbuf.dcnt += 16
            o.val = dbuf.dcnt
            self.dma_ops.append(o)
        self.ops[eng].append(o)
        return o

    def flush(self, final=False):
        nc = self.nc
        for e in ENGS:
            for o in self.ops[e]:
                if o.dbuf is None and o.needed:
                    self.cnt[e] += 1
                    o.val = self.cnt[e]
        pending_dma = [o for o in self.dma_ops]
        self.dma_ops = []
        with nc.Block() as block:
            for e in ENGS:
                ops = self.ops[e]
                if not ops and not (e == "gpsimd"):
                    continue

                def body(eng, e=e, ops=ops):
                    waited = self.waited[e]
                    for o in ops:
                        for d in o.deps:
                            if d.dbuf is not None:
                                key = ("d", d.dbuf.uid)
                                sem = d.dbuf.sem
                            else:
                                if d.val is None:
                                    assert d.flushed
                                    continue
                                key = ("e", d.eng)
                                sem = self.esems[d.eng]
                            if waited.get(key, 0) >= d.val:
                                continue
                            waited[key] = d.val
                            eng.wait_ge(sem, d.val)
                        ins = o.fn(eng)
                        if o.dbuf is not None:
                            ins.then_inc(o.dbuf.sem, 16)
                        elif o.needed:
                            ins.then_inc(self.esems[e], 1)
                    if e == "gpsimd":
                        for o in pending_dma:
                            key = ("d", o.dbuf.uid)
                            if waited.get(key, 0) >= o.val:
                                continue
                            waited[key] = o.val
                            eng.wait_ge(o.dbuf.sem, o.val)

                getattr(block, e)(body)
        for e in ENGS:
            for o in self.ops[e]:
                o.flushed = True
            self.ops[e] = []


def build_nc():
    nc = bass.Bass("TRN2", target_bir_lowering=False)

    def din(name, shape, dt=F32):
        return nc.dram_tensor(name, list(shape), dt, kind="ExternalInput").ap()

    xT_d = din("xT", [D, S])
    xn_d = din("xn", [S // 2, D])
    kv_d = din("kv", [128, 64])
    cst_d = din("cst", [128, NCST])
    win_d = din("w_in", [D, 3 * D])
    wout_d = din("w_out", [D, D])
    wg_d = din("w_gate", [D, DFF])
    wu_d = din("w_up", [D, DFF])
    wd_d = din("w_down", [DFF, D])
    gsb_d = din("gsb", [64, 8])
    gdil_d = din("gdil", [64, 8])
    gsbp_d = din("gsbp", [128, 4])
    ln1g_d = din("ln1g", [128, D])
    ln1b_d = din("ln1b", [128, D])
    ln2g_d = din("ln2g", [128, D])
    ln2b_d = din("ln2b", [128, D])
    out_d = nc.dram_tensor("out", [S // 2, D], F32, kind="ExternalOutput").ap()
    vscr_d = nc.dram_tensor("vscr", [MARG + S, 130], BF16).ap()
    mixT_d = nc.dram_tensor("mixT", [D, S // 2], BF16).ap()
    wgs_d = nc.dram_tensor("wgs", [NF, 128, 1024], BF16).ap()
    wus_d = nc.dram_tensor("wus", [NF, 128, 1024], BF16).ap()
    wds_d = nc.dram_tensor("wds", [NF, 128, 1024], BF16).ap()

    with ExitStack() as top:
        esems = {e: top.enter_context(nc.semaphore("es_" + e)) for e in ENGS}
        dsems = [top.enter_context(nc.semaphore("ds%d" % i)) for i in range(96)]
        P = Prog(nc, esems, dsems)

        PB = [Buf("bank%d" % i) for i in range(8)]

        vscr_b = Buf("vscr")
        mixT_b = Buf("mixT")
        wsc_b = Buf("wscr")

        with ExitStack() as ph:
            def sb(name, shape, dt):
                return ph.enter_context(nc.sbuf_tensor("a_" + name, list(shape), dt))

            pbig = ph.enter_context(nc.psum_tensor("a_pbig", [128, 1024], F32))
            pbigB = ph.enter_context(nc.psum_tensor("a_pbigB", [128, 1024], F32))
            pbk = [ph.enter_context(nc.psum_tensor("a_pb%d" % i, [128, 512], F32)) for i in range(4, 8)]

            def bank(i):
                if i < 2:
                    return pbig[:, i * 512:(i + 1) * 512]
                if i < 4:
                    return pbigB[:, (i - 2) * 512:(i - 1) * 512]
                return pbk[i - 4][:, :]

            cst = sb("cst", [128, NCST], BF16)
            cst_b = Buf("cst")
            stg = [sb("stg%d" % i, [128, 4096], F32) for i in range(2)]
            stg_b = [Buf("stg%d" % i) for i in range(2)]
            xb = [sb("xb%d" % i, [128, 8, CH], BF16) for i in range(2)]
            xb_b = [Buf("xb%d" % i) for i in range(2)]
            wp = sb("wp", [128, 8, 768], BF16)
            wp_b = Buf("wp")
            KaT = sb("KaT", [128, S], BF16)
            KaT_c = [Buf("KaT%d" % i) for i in range(NV)]
            Va = sb("Va", [128, 64, 128], BF16)
            Va_c = [Buf("Va%d" % i) for i in range(NV)]
            KbT = sb("KbT", [128, MARG + S], BF16)
            KbT_c = [Buf("KbT%d" % i) for i in range(NV)]
            QaT = [sb("QaT%d" % i, [128, CH], BF16) for i in range(2)]
            QaT_b = [Buf("QaT%d" % i) for i in range(2)]
            QbT = [sb("QbT%d" % i, [128, CH], BF16) for i in range(2)]
            QbT_b = [Buf("QbT%d" % i) for i in range(2)]
            kvf = sb("kvf", [128, 64], F32)
            kvb = sb("kvb", [128, 64], BF16)
            kv_b = Buf("kv")
            vst = [sb("vst%d" % i, [128, 4, 130], BF16) for i in range(2)]
            vst_b = [Buf("vst%d" % i) for i in range(2)]
            vb1 = sb("vb1", [128, 5, 130], BF16)
            vb4 = sb("vb4", [128, 8, 130], BF16)
            vb16 = sb("vb16", [128, 32, 130], BF16)
            vb1_b, vb4_b, vb16_b = Buf("vb1"), Buf("vb4"), Buf("vb16")
            e_t = [sb("e_t%d" % i, [128, 2 * CH], F32) for i in range(2)]
            e_b = [Buf("e_t%d" % i) for i in range(2)]
            sp_t = [sb("sp_t%d" % i, [128, 2 * CH], BF16) for i in range(2)]
            sp_b = [Buf("sp_t%d" % i) for i in range(2)]
            w_t = [sb("w_t%d" % i, [128, 2 * CH], BF16) for i in range(2)]
            w_b = [Buf("w_t%d" % i) for i in range(2)]
            Rt = [sb("R%d" % i, [128, 2 * CH], BF16) for i in range(3)]
            R_b = [Buf("R%d" % i) for i in range(3)]
            osbj = sb("osbj", [128, CH], F32)
            osbj_b = Buf("osbj")
            lnvj = sb("lnvj", [128, CH], F32)
            rstdj = sb("rstdj", [128, CH], F32)
            mixtj = sb("mixtj", [128, CH], BF16)
            nj_b = Buf("normj")
            mixtj_b = Buf("mixtj")
            gsbp_t = sb("gsbp_t", [128, 4], F32)
            dtc = [sb("dt%d" % i, [128, CH], F32) for i in range(2)]
            dtc_b = [Buf("dt%d" % i) for i in range(2)]
            etc_ = [[sb("etc%d_%d" % (i, j), [128, CH], BF16) for j in range(2)] for i in range(2)]
            etc_b = [[Buf("etc%d_%d" % (i, j)) for j in range(2)] for i in range(2)]
            sqc = [sb("sqc%d" % i, [128, CH], BF16) for i in range(2)]
            sqc_b = [Buf("sqc%d" % i) for i in range(2)]
            lnvc = [sb("lnvc%d" % i, [64, CH], F32) for i in range(2)]
            lnvc_b = [Buf("lnvc%d" % i) for i in range(2)]
            rstdc = [sb("rstdc%d" % i, [64, CH], F32) for i in range(2)]
            rstdc_b = [Buf("rstdc%d" % i) for i in range(2)]
            mixt = [sb("mixt%d" % i, [64, CH], BF16) for i in range(2)]
            mixt_b = [Buf("mixt%d" % i) for i in range(2)]
            gsb_t = sb("gsb_t", [64, 8], F32)
            gdil_t = sb("gdil_t", [64, 8], F32)
            g_b = Buf("g")

            for i in range(2):
                c0 = i * 3680
                P.op("sync", lambda e, i=i, c0=c0: e.dma_start(out=stg[i][:, 0:3680], in_=cst_d[:, c0:c0 + 3680]),
                     writes=[stg_b[i]], dbuf=stg_b[i])
                P.op("vector", lambda e, i=i, c0=c0: e.tensor_copy(out=cst[:, c0:c0 + 3680], in_=stg[i][:, 0:3680]),
                     reads=[stg_b[i]], writes=[cst_b])
            P.op("sync", lambda e: e.dma_start(out=kvf[:, :], in_=kv_d[:, :]), writes=[kv_b], dbuf=kv_b)
            P.op("vector", lambda e: e.tensor_copy(out=kvb[:, :], in_=kvf[:, :]), reads=[kv_b], writes=[kv_b])
            P.op("sync", lambda e: e.dma_start(out=gsb_t[:, :], in_=gsb_d[:, :]), writes=[g_b], dbuf=g_b)
            P.op("sync", lambda e: e.dma_start(out=gdil_t[:, :], in_=gdil_d[:, :]), writes=[g_b], dbuf=g_b)
            P.op("sync", lambda e: e.dma_start(out=gsbp_t[:, :], in_=gsbp_d[:, :]), writes=[g_b], dbuf=g_b)
            P.op("gpsimd", lambda e: e.memset(KbT[:, :], 0.0), writes=list(KbT_c))
            P.op("gpsimd", lambda e: e.memset(vst[0][:, :, :], 0.0), writes=[vst_b[0]])
            for r0 in range(0, MARG + S, 512):
                P.op("sync", lambda e, r0=r0: e.dma_start(
                    out=vscr_d[r0:r0 + 512, :].rearrange("(t p) c -> p t c", p=128), in_=vst[0][:, :, :]),
                    reads=[vst_b[0]], writes=[vscr_b], dbuf=vst_b[0])

            xT_v = xT_d.rearrange("(c p) t -> p c t", p=128)
            win_v = win_d.rearrange("(c p) n -> p c n", p=128)
            ld_n = [0]

            tb = [sb("tb%d" % i, [128, DFF], BF16) for i in range(2)]
            tb_b = [Buf("tb%d" % i) for i in range(2)]
            w0 = []
            n0 = 0
            for (src, dst) in ((wg_d, wgs_d), (wu_d, wus_d)):
                dview = dst.rearrange("f p (c j) -> p f c j", c=8)
                for kc in range(8):
                    sl0 = n0 % 2
                    n0 += 1

                    def ld(sl0=sl0, src=src, kc=kc):
                        P.op("sync", lambda e: e.dma_start(out=stg[sl0][:, 0:DFF], in_=src[kc * 128:(kc + 1) * 128, :]),
                             writes=[stg_b[sl0]], dbuf=stg_b[sl0])

                    def cs_(sl0=sl0, dview=dview, kc=kc):
                        P.op("vector", lambda e: e.tensor_copy(out=tb[sl0][:, :], in_=stg[sl0][:, 0:DFF]),
                             reads=[stg_b[sl0]], writes=[tb_b[sl0]])
                        P.op("sync", lambda e: e.dma_start(
                            out=dview[:, :, kc, :], in_=tb[sl0][:, :].rearrange("p (f j) -> p f j", j=128)),
                            reads=[tb_b[sl0]], writes=[wsc_b], dbuf=tb_b[sl0])
                    w0.append((ld, cs_))
            for f0 in range(0, NF, 2):
                sl0 = n0 % 2
                n0 += 1

                def ld(sl0=sl0, f0=f0):
                    P.op("sync", lambda e: e.dma_start(
                        out=stg[sl0][:, 0:2048].rearrange("p (f n) -> p f n", f=2),
                        in_=wd_d[f0 * 128:(f0 + 2) * 128, :].rearrange("(f p) n -> p f n", p=128)),
                        writes=[stg_b[sl0]], dbuf=stg_b[sl0])

                def cs_(sl0=sl0, f0=f0):
                    P.op("vector", lambda e: e.tensor_copy(out=tb[sl0][:, 0:2048], in_=stg[sl0][:, 0:2048]),
                         reads=[stg_b[sl0]], writes=[tb_b[sl0]])
                    P.op("sync", lambda e: e.dma_start(
                        out=wds_d[f0:f0 + 2, :, :].rearrange("f p n -> p f n"),
                        in_=tb[sl0][:, 0:2048].rearrange("p (f n) -> p f n", f=2)),
                        reads=[tb_b[sl0]], writes=[wsc_b], dbuf=tb_b[sl0])
                w0.append((ld, cs_))
            def p0_group(pieces):
                out = []
                for i in range(len(pieces) + 1):
                    if i < len(pieces):
                        out.append(pieces[i][0])
                    if i >= 1:
                        out.append(pieces[i - 1][1])
                return out

            def norm_thunks(src, src_b, nrow, h_glob, is_dil, kown, gtile, ch=0, mbk=6):
                th = []
                lcol = C_MEANE if nrow == 65 else C_MEAN

                def t1():
                    P.op("scalar", lambda e: e.activation(out=sqc[ch][0:nrow, :], in_=src[0:nrow, :], func=AF.Square),
                         reads=[src_b], writes=[sqc_b[ch]])
                    P.op("tensor", lambda e: e.matmul(bank(mbk)[0:64, :], lhsT=cst[0:nrow, lcol:lcol + 64], rhs=sqc[ch][0:nrow, :],
                                                      start=True, stop=True),
                         reads=[sqc_b[ch], cst_b], writes=[PB[mbk]])

                def t2():
                    if nrow == 65:
                        P.op("scalar", lambda e: e.activation(out=lnvc[ch][:, :], in_=bank(mbk)[0:64, :], func=AF.Ln),
                             reads=[PB[mbk]], writes=[lnvc_b[ch]])
                    else:
                        P.op("scalar", lambda e: e.activation(out=lnvc[ch][:, :], in_=bank(mbk)[0:64, :], func=AF.Ln,
                                                              bias=RMS_EPS, scale=1.0),
                             reads=[PB[mbk]], writes=[lnvc_b[ch]])
                    P.op("scalar", lambda e: e.activation(out=rstdc[ch][:, :], in_=lnvc[ch][:, :], func=AF.Exp, scale=-0.5),
                         reads=[lnvc_b[ch]], writes=[rstdc_b[ch]])

                def t3():
                    ms = ld_n[0] % 2
                    ld_n[0] += 1
                    P.op("vector", lambda e, ms=ms: e.scalar_tensor_tensor(
                        out=mixt[ms][:, :], in0=src[0:64, :], scalar=gtile[0:64, h_glob:h_glob + 1], in1=rstdc[ch][:, :],
                        op0=ALU.mult, op1=ALU.mult),
                        reads=[src_b, rstdc_b[ch], g_b], writes=[mixt_b[ms]])
                    row0 = (512 if is_dil else 0) + h_glob * 64
                    P.op("sync", lambda e, ms=ms, row0=row0: e.dma_start(
                        out=mixT_d[row0:row0 + 64, kown * CH:(kown + 1) * CH], in_=mixt[ms][:, :]),
                        reads=[mixt_b[ms]], writes=[mixT_b], dbuf=mixt_b[ms])
                return [t1, t2, t3]

            def norm_joint_thunks(p, kown):
                def t1():
                    P.op("scalar", lambda e: e.activation(out=sqc[0][:, :], in_=osbj[:, :], func=AF.Square),
                         reads=[osbj_b], writes=[sqc_b[0]])
                    P.op("tensor", lambda e: e.matmul(bank(6), lhsT=cst[:, C_MEANB:C_MEANB + 128], rhs=sqc[0][:, :],
                                                      start=True, stop=True),
                         reads=[sqc_b[0], cst_b], writes=[PB[6]])

                def t2():
                    P.op("scalar", lambda e: e.activation(out=lnvj[:, :], in_=bank(6), func=AF.Ln, bias=RMS_EPS, scale=1.0),
                         reads=[PB[6]], writes=[nj_b])
                    P.op("scalar", lambda e: e.activation(out=rstdj[:, :], in_=lnvj[:, :], func=AF.Exp, scale=-0.5),
                         reads=[nj_b], writes=[nj_b])

                def t3():
                    P.op("vector", lambda e: e.scalar_tensor_tensor(
                        out=mixtj[:, :], in0=osbj[:, :], scalar=gsbp_t[:, p:p + 1], in1=rstdj[:, :],
                        op0=ALU.mult, op1=ALU.mult),
                        reads=[osbj_b, nj_b, g_b], writes=[mixtj_b])
                    P.op("sync", lambda e: e.dma_start(
                        out=mixT_d[p * 128:(p + 1) * 128, kown * CH:(kown + 1) * CH], in_=mixtj[:, :]),
                        reads=[mixtj_b], writes=[mixT_b], dbuf=mixtj_b)
                return [t1, t2, t3]

            def dil_thunks(v, p, QB, QB_b, kown):
                chains = []
                for hd in range(2):
                    th = []
                    sbk, abk = (6, 7) if (hd == 0 or INTERLEAVE_DIL) else (0, 1)
                    dt_, dt_b = dtc[hd], dtc_b[hd]
                    et, et_b = etc_[hd], etc_b[hd]
                    r = slice(hd * 64, hd * 64 + 64)
                    hg = 2 * p + hd
                    for pi in range(3):
                        ex = -(hg + 1) + 2 * pi
                        sid0 = C_SID + (ex + 8) * 128
                        np0 = C_NP12 if pi < 2 else (C_NP3A if v % 4 == 1 else C_NP3B)
                        G = 4 if pi < 2 else 16
                        W = 128 if pi < 2 else 32
                        for hb in range(2):
                            groups = list(range(hb * G // 2, (hb + 1) * G // 2))

                            def t1(hb=hb, groups=groups, sid0=sid0, np0=np0, pi=pi, W=W, r=r, sbk=sbk):
                                P.op("tensor", lambda e: e.matmul(
                                    bank(sbk), lhsT=cst[:, sid0:sid0 + 128], rhs=cst[:, np0 + hb * 512:np0 + (hb + 1) * 512],
                                    start=True, stop=False, skip_group_check=True),
                                    reads=[cst_b], writes=[PB[sbk]])
                                for g in groups:
                                    for half in range(2):
                                        col0 = g * 2 * W + half * W - hb * 512
                                        if pi == 0:
                                            blk = 4 * v + g - 1 + half
                                            k0 = MARG + blk * 128
                                            kap = KbT[r, k0:k0 + 128]
                                            qap = QB[r, g * 128:(g + 1) * 128]
                                            kbufs = [KbT_c[blk // 4]]
                                        elif pi == 1:
                                            k0 = MARG + (v - 1 + half) * 512 + g
                                            kap = KbT[r, k0:k0 + 509:4]
                                            qap = QB[r, g:g + 509:4]
                                            kbufs = [KbT_c[v - 1 + half]]
                                        else:
                                            U = v // 4 - 1 + half
                                            k0 = MARG + U * 2048 + g
                                            kap = KbT[r, k0:k0 + 2033:16]
                                            qap = QB[r, g:g + 497:16]
                                            kbufs = [KbT_c[c] for c in range(4 * U, 4 * U + 4) if 0 <= c <= v]
                                        P.op("tensor", lambda e, kap=kap, qap=qap, col0=col0: e.matmul(
                                            bank(sbk)[:, col0:col0 + W], lhsT=kap, rhs=qap, start=False, stop=False,
                                            skip_group_check=True),
                                            reads=kbufs + [QB_b], writes=[PB[sbk]])

                            def t2(hb=hb, sbk=sbk, et=et, et_b=et_b):
                                P.op("scalar", lambda e: e.activation(out=et[hb][:, :], in_=bank(sbk), func=AF.Exp),
                                     reads=[PB[sbk]], writes=[et_b[hb]])

                            def t3(hb=hb, groups=groups, pi=pi, W=W, hd=hd, abk=abk, et=et, et_b=et_b):
                                first = (hb == 0)
                                for g in groups:
                                    for half in range(2):
                                        col0 = g * 2 * W + half * W - hb * 512
                                        if pi == 0:
                                            vt = vb1[:, g + half, hd * 65:(hd + 1) * 65]
                                            vbuf = vb1_b
                                        elif pi == 1:
                                            vt = vb4[:, half * 4 + g, hd * 65:(hd + 1) * 65]
                                            vbuf = vb4_b
                                        else:
                                            vt = vb16[:, half * 16 + g, hd * 65:(hd + 1) * 65]
                                            vbuf = vb16_b
                                        P.op("tensor", lambda e, vt=vt, col0=col0, g=g, first=first: e.matmul(
                                            bank(abk)[0:65, g * W:(g + 1) * W], lhsT=vt, rhs=et[hb][:, col0:col0 + W],
                                            start=first, stop=False, skip_group_check=True),
                                            reads=[vbuf, et_b[hb]], writes=[PB[abk]])
                                        first = False
                            th += [t1, t2, t3]

                        def t4(pi=pi, abk=abk, dt_=dt_, dt_b=dt_b):
                            if pi == 0:
                                P.op("vector", lambda e: e.tensor_copy(out=dt_[0:65, :], in_=bank(abk)[0:65, :]),
                                     reads=[PB[abk]], writes=[dt_b])
                            else:
                                rr = 4 if pi == 1 else 16
                                P.op("vector", lambda e: e.tensor_tensor(
                                    out=dt_[0:65, :].rearrange("p (u r) -> p u r", r=rr),
                                    in0=dt_[0:65, :].rearrange("p (u r) -> p u r", r=rr),
                                    in1=bank(abk)[0:65, :].rearrange("p (r u) -> p u r", r=rr), op=ALU.add),
                                    reads=[PB[abk], dt_b], writes=[dt_b])
                        th.append(t4)
                    th += norm_thunks(dt_, dt_b, 65, hg, True, kown, gdil_t, ch=hd, mbk=sbk)
                    chains.append(th)
                if not ZIP_DIL:
                    return chains[0] + chains[1]
                out = []
                for i in range(max(len(c) for c in chains)):
                    for c in chains:
                        if i < len(c):
                            out.append(c[i])
                return out

            def load_x(v):
                s = v % 2
                P.op("sync", lambda e, s=s, v=v: e.dma_start(
                    out=stg[s][:, :].rearrange("p (c t) -> p c t", c=8), in_=xT_v[:, :, v * CH:(v + 1) * CH]),
                    writes=[stg_b[s]], dbuf=stg_b[s])
                P.op("vector", lambda e, s=s: e.tensor_copy(
                    out=xb[s][:, :, :], in_=stg[s][:, :].rearrange("p (c t) -> p c t", c=8)),
                    reads=[stg_b[s]], writes=[xb_b[s]])

            pj_n = [0]

            def proj_thunks(v, direct=False):
                s = v % 2
                own = (v % 2 == 1)
                qs = ((v - 1) // 2) % 2
                th = []

                def fm(j, evac):
                    bk = (5 + pj_n[0] % 3) if direct else (5 if INTERLEAVE_DIL else 6 + pj_n[0] % 2)
                    pj_n[0] += 1
                    for k0 in (0, 4):
                        def t_(k0=k0, bk=bk):
                            for kc in range(k0, k0 + 4):
                                P.op("tensor", lambda e, kc=kc: e.matmul(
                                    bank(bk), lhsT=wp[:, kc, j * 128:(j + 1) * 128], rhs=xb[s][:, kc, :],
                                    start=(kc == 0), stop=(kc == 7)),
                                    reads=[wp_b, xb_b[s]], writes=[PB[bk]])
                        th.append(t_)
                    th.append(lambda bk=bk: evac(bk))

                def tm(j, evac):
                    bk = (5 + pj_n[0] % 3) if direct else (5 if INTERLEAVE_DIL else 6 + pj_n[0] % 2)
                    pj_n[0] += 1
                    for t in range(4):
                        def t_(t=t, bk=bk):
                            for kc in range(8):
                                P.op("tensor", lambda e, kc=kc: e.matmul(
                                    bank(bk)[:, t * 128:(t + 1) * 128], lhsT=xb[s][:, kc, t * 128:(t + 1) * 128],
                                    rhs=wp[:, kc, j * 128:(j + 1) * 128], start=(kc == 0), stop=(kc == 7)),
                                    reads=[wp_b, xb_b[s]], writes=[PB[bk]])
                        th.append(t_)
                    th.append(lambda bk=bk: evac(bk))

                fm(1, lambda bk: P.op("vector", lambda e: e.tensor_copy(out=KaT[:, v * CH:(v + 1) * CH], in_=bank(bk)),
                                      reads=[PB[bk]], writes=[KaT_c[v]]))
                fm(4, lambda bk: P.op("vector", lambda e: e.tensor_copy(
                    out=KbT[:, MARG + v * CH:MARG + (v + 1) * CH], in_=bank(bk)), reads=[PB[bk]], writes=[KbT_c[v]]))
                tm(2, lambda bk: P.op("vector", lambda e: e.tensor_copy(
                    out=Va[:, v * 4:(v + 1) * 4, :], in_=bank(bk).rearrange("p (t n) -> p t n", t=4)),
                    reads=[PB[bk]], writes=[Va_c[v]]))

                def evac_vb(bk):
                    P.op("vector", lambda e: e.tensor_copy(
                        out=vst[s][:, :, :].rearrange("p t (h c) -> p t h c", h=2)[:, :, :, 0:64],
                        in_=bank(bk).rearrange("p (t h c) -> p t h c", t=4, h=2)),
                        reads=[PB[bk]], writes=[vst_b[s]])
                    for hc in (64, 129):
                        P.op("vector", lambda e, hc=hc: e.tensor_copy(
                            out=vst[s][:, :, hc:hc + 1], in_=kvb[:, v * 4:(v + 1) * 4].rearrange("p (t o) -> p t o", o=1)),
                            reads=[kv_b], writes=[vst_b[s]])
                    r0 = MARG + v * CH
                    P.op("sync", lambda e: e.dma_start(
                        out=vscr_d[r0:r0 + 512, :].rearrange("(t p) c -> p t c", p=128), in_=vst[s][:, :, :]),
                        reads=[vst_b[s]], writes=[vscr_b], dbuf=vst_b[s])
                tm(5, evac_vb)
                if own:
                    fm(0, lambda bk: P.op("vector", lambda e: e.tensor_scalar(
                        out=QaT[qs][:, :], in0=bank(bk), scalar1=0.125, scalar2=None, op0=ALU.mult),
                        reads=[PB[bk]], writes=[QaT_b[qs]]))
                    fm(3, lambda bk: P.op("vector", lambda e: e.tensor_scalar(
                        out=QbT[qs][:, :], in0=bank(bk), scalar1=0.125, scalar2=None, op0=ALU.mult),
                        reads=[PB[bk]], writes=[QbT_b[qs]]))
                return th

            def pass_prologue(p):
                th = []
                for j, base in enumerate((0, 512, 1024, 1536, 2048, 2560)):
                    def t_(j=j, base=base):
                        s = ld_n[0] % 2
                        ld_n[0] += 1
                        c0 = base + p * 128
                        P.op("sync", lambda e: e.dma_start(
                            out=stg[s][:, 0:1024].rearrange("p (c n) -> p c n", c=8), in_=win_v[:, :, c0:c0 + 128]),
                            writes=[stg_b[s]], dbuf=stg_b[s])
                        P.op("vector", lambda e: e.tensor_copy(
                            out=wp[:, :, j * 128:(j + 1) * 128], in_=stg[s][:, 0:1024].rearrange("p (c n) -> p c n", c=8)),
                            reads=[stg_b[s]], writes=[wp_b])
                    th.append(t_)
                th.append(lambda: load_x(0))
                th.append(lambda: load_x(1))
                return th

            for p in range(4):
                if p == 0:
                    for t_ in pass_prologue(0):
                        t_()
                for t_ in proj_thunks(0, True) + proj_thunks(1, True):
                    t_()
                carry = []
                for v in range(1, NV, 2):
                    kown = (v - 1) // 2
                    qs = kown % 2
                    QA, QB = QaT[qs], QbT[qs]
                    QA_b, QB_b = QaT_b[qs], QbT_b[qs]
                    b1 = MARG + (4 * v - 1) * 128
                    P.op("sync", lambda e, b1=b1: e.dma_start(
                        out=vb1[:, :, :], in_=vscr_d[b1:b1 + 640, :].rearrange("(t p) c -> p t c", p=128)),
                        reads=[vscr_b], writes=[vb1_b], dbuf=vb1_b)
                    for half in range(2):
                        b4 = MARG + (v - 1 + half) * 512
                        P.op("sync", lambda e, b4=b4, half=half: e.dma_start(
                            out=vb4[:, half * 4:(half + 1) * 4, :],
                            in_=vscr_d[b4:b4 + 512, :].rearrange("(s r) c -> s r c", r=4)),
                            reads=[vscr_b], writes=[vb4_b], dbuf=vb4_b)
                        b16 = MARG + (v // 4 - 1 + half) * 2048
                        P.op("sync", lambda e, b16=b16, half=half: e.dma_start(
                            out=vb16[:, half * 16:(half + 1) * 16, :],
                            in_=vscr_d[b16:b16 + 2048, :].rearrange("(s r) c -> s r c", r=16)),
                            reads=[vscr_b], writes=[vb16_b], dbuf=vb16_b)
                    side = carry
                    carry = []
                    L1 = []
                    if v + 2 < NV:
                        load_x(v + 1)
                        load_x(v + 2)
                        L1 = proj_thunks(v + 1) + proj_thunks(v + 2)
                    if v + 2 >= NV and p < 3:
                        side = side + pass_prologue(p + 1)
                    if p == 0:
                        take = len(w0) if v + 2 >= NV else min(len(w0), 4)
                        side = side + p0_group(w0[:take])
                        w0 = w0[take:]
                    side_b = dil_thunks(v, p, QB, QB_b, kown)
                    if INTERLEAVE_DIL:
                        L2 = side_b
                        side_b = []
                        zz = []
                        for i_ in range(max(len(L1), len(L2))):
                            if i_ < len(L1):
                                zz.append(L1[i_])
                            if i_ < len(L2):
                                zz.append(L2[i_])
                        side = side + zz
                    else:
                        side = side + L1

                    nkb = 4 * (v + 1)

                    def st_A(i, b0, last):
                        kb = nkb - 1 - i
                        diag = kb >= 4 * v
                        mb = kb - 4 * v
                        ks = slice(kb * 128, (kb + 1) * 128)
                        for hd in range(2):
                            r = slice(hd * 64, hd * 64 + 64)
                            bk = b0 + hd
                            P.op("tensor", lambda e, bk=bk, r=r, ks=ks, diag=diag, last=last, QA=QA: e.matmul(
                                bank(bk), lhsT=KaT[r, ks], rhs=QA[r, :], start=True, stop=(last and not diag)),
                                reads=[KaT_c[kb // 4], QA_b], writes=[PB[bk]])
                            if diag:
                                P.op("tensor", lambda e, bk=bk, mb=mb, last=last: e.matmul(
                                    bank(bk), lhsT=cst[:, C_ID:C_ID + 128],
                                    rhs=cst[:, C_MASK + mb * 512:C_MASK + (mb + 1) * 512], start=False, stop=last),
                                    reads=[cst_b], writes=[PB[bk]])

                    def st_act1(i):
                        sl = i % 2
                        P.op("scalar", lambda e, sl=sl: e.activation(out=e_t[sl][:, :], in_=pbig[:, :], func=AF.Exp),
                             reads=[PB[0], PB[1]], writes=[e_b[sl]])
                        P.op("scalar", lambda e, sl=sl: e.activation(out=sp_t[sl][:, :], in_=e_t[sl][:, :], func=AF.Ln,
                                                                     bias=1.0, scale=1.0),
                             reads=[e_b[sl]], writes=[sp_b[sl]])

                    def st_R(i):
                        sl = i % 2
                        if i == 0:
                            P.op("vector", lambda e, sl=sl: e.tensor_copy(out=Rt[1][:, :], in_=sp_t[sl][:, :]),
                                 reads=[sp_b[sl]], writes=[R_b[1]])
                        elif i < nkb - 1:
                            P.op("vector", lambda e, sl=sl, i=i: e.tensor_tensor(
                                out=Rt[(i + 1) % 3][:, :], in0=Rt[i % 3][:, :], in1=sp_t[sl][:, :], op=ALU.add),
                                reads=[sp_b[sl], R_b[i % 3]], writes=[R_b[(i + 1) % 3]])

                    def st_B(i):
                        sl = i % 2
                        st_A(i, 2, False)
                        for hd in range(2):
                            cs = slice(hd * CH, (hd + 1) * CH)
                            P.op("tensor", lambda e, sl=sl, i=i, hd=hd, cs=cs: e.matmul(
                                bank(2 + hd), lhsT=cst[:, C_NTRI:C_NTRI + 128], rhs=sp_t[sl][:, cs],
                                start=False, stop=(i == 0)),
                                reads=[cst_b, sp_b[sl]], writes=[PB[2 + hd]])
                            if i > 0:
                                P.op("tensor", lambda e, sl=sl, i=i, hd=hd, cs=cs: e.matmul(
                                    bank(2 + hd), lhsT=cst[:, C_NONES:C_NONES + 128], rhs=Rt[i % 3][:, cs],
                                    start=False, stop=True),
                                    reads=[cst_b, R_b[i % 3]], writes=[PB[2 + hd]])

                    def st_act2(i):
                        sl = i % 2
                        P.op("scalar", lambda e, sl=sl: e.activation(out=w_t[sl][:, :], in_=pbigB[:, :], func=AF.Exp),
                             reads=[PB[2], PB[3]], writes=[w_b[sl]])

                    def st_AV(i):
                        sl = i % 2
                        kb = nkb - 1 - i
                        for hd in range(2):
                            cs = slice(hd * CH, (hd + 1) * CH)
                            P.op("tensor", lambda e, sl=sl, kb=kb, hd=hd, i=i, cs=cs, nkb=nkb: e.matmul(
                                bank(4)[hd * 64:(hd + 1) * 64, :], lhsT=Va[:, kb, hd * 64:(hd + 1) * 64], rhs=w_t[sl][:, cs],
                                start=(i == 0), stop=(i == nkb - 1)),
                                reads=[Va_c[kb // 4], w_b[sl]], writes=[PB[4]])

                    nside = len(side)
                    per = -(-nside // max(1, nkb - 2)) if nside else 0
                    si = 0
                    for st in range(nkb + 2):
                        if st < nkb:
                            st_A(st, 0, True)
                            st_act1(st)
                            st_R(st)
                        if 0 <= st - 1 < nkb:
                            st_B(st - 1)
                            st_act2(st - 1)
                        if 0 <= st - 2 < nkb:
                            st_AV(st - 2)
                        for _ in range(per):
                            if si < nside:
                                side[si]()
                                si += 1
                    while si < nside:
                        side[si]()
                        si += 1
                    for t_ in side_b:
                        t_()
                    P.op("vector", lambda e: e.tensor_copy(out=osbj[:, :], in_=bank(4)),
                         reads=[PB[4]], writes=[osbj_b])
                    carry += norm_joint_thunks(p, kown)
                    if v + 2 >= NV:
                        for t_ in carry:
                            t_()
                        carry = []
            P.flush()

        with ExitStack() as ph:
            def sb(name, shape, dt):
                return ph.enter_context(nc.sbuf_tensor("b_" + name, list(shape), dt))

            pbig = ph.enter_context(nc.psum_tensor("b_pbig", [128, 1024], F32))
            pbk = [ph.enter_context(nc.psum_tensor("b_pb%d" % i, [128, 512], F32)) for i in range(2, 7)]
            pT = ph.enter_context(nc.psum_tensor("b_pT", [128, 1024], BF16))

            def bank(i):
                if i < 2:
                    return pbig[:, i * 512:(i + 1) * 512]
                return pbk[i - 2][:, :]

            idt = sb("idt", [128, 128], BF16)
            idf = sb("idf", [128, 128], F32)
            id_b = Buf("id")
            stg = [sb("stg2_%d" % i, [128, 2048], F32) for i in range(2)]
            stg_b = [Buf("stg2_%d" % i) for i in range(2)]
            wo = sb("wo", [128, 8, D], BF16)
            wo_b = Buf("wo")
            wdr = sb("wdr", [128, NF, D], BF16)
            wdr_b = Buf("wdr")
            lnp = [sb("lnp%d" % i, [128, D], F32) for i in range(4)]
            lnp_b = Buf("lnp")
            wgt = [sb("wgt%d" % i, [128, 8, 128], BF16) for i in range(3)]
            wut = [sb("wut%d" % i, [128, 8, 128], BF16) for i in range(3)]
            wgt_b = [Buf("wgt%d" % i) for i in range(3)]
            wut_b = [Buf("wut%d" % i) for i in range(3)]
            mxs = sb("mxs", [128, 8, CH], BF16)
            mxs_b = Buf("mxs")
            xt = [sb("xt%d" % i, [128, D], F32) for i in range(2)]
            xt_b = [Buf("xt%d" % i) for i in range(2)]
            hhs = [sb("hh%d" % i, [128, D], F32) for i in range(2)]
            hhs_b = [Buf("hh%d" % i) for i in range(2)]
            h1 = sb("h1", [128, 4, D], F32)
            h1_b = [Buf("h1_%d" % i) for i in range(4)]
            h1bs = [sb("h1b%d" % i, [128, D], BF16) for i in range(2)]
            h1bs_b = [Buf("h1b%d" % i) for i in range(2)]
            h1T = sb("h1T", [128, 8, CH], BF16)
            h1T_b = Buf("h1T")
            sg = [sb("sg%d" % i, [128, CH], F32) for i in range(2)]
            sg_b = [Buf("sg%d" % i) for i in range(2)]
            aT = sb("aT", [128, NF, CH], BF16)
            aT_b = Buf("aT")
            ot = [sb("ot%d" % i, [128, D], F32) for i in range(2)]
            ot_b = [Buf("ot%d" % i) for i in range(2)]
            st6 = [sb("st6_%d" % i, [128, 12], F32) for i in range(2)]
            mv = [sb("mv%d" % i, [128, 2], F32) for i in range(2)]
            rs = [sb("rs%d" % i, [128, 1], F32) for i in range(2)]
            st_b = [Buf("stats%d" % i) for i in range(2)]

            P.op("sync", lambda e: e.dma_start(out=idf[:, :], in_=cst_d[:, C_ID:C_ID + 128]), writes=[id_b], dbuf=id_b)
            P.op("vector", lambda e: e.tensor_copy(out=idt[:, :], in_=idf[:, :]), reads=[id_b], writes=[id_b])
            for i, src in enumerate((ln1g_d, ln1b_d, ln2g_d, ln2b_d)):
                P.op("sync", lambda e, i=i, src=src: e.dma_start(out=lnp[i][:, :], in_=src[:, :]),
                     writes=[lnp_b], dbuf=lnp_b)
            wout_v = wout_d.rearrange("(c p) n -> p c n", p=128)
            n = 0
            for c0 in range(0, 8, 2):
                s = n % 2
                n += 1
                P.op("sync", lambda e, s=s, c0=c0: e.dma_start(
                    out=stg[s][:, :].rearrange("p (c n) -> p c n", c=2), in_=wout_v[:, c0:c0 + 2, :]),
                    writes=[stg_b[s]], dbuf=stg_b[s])
                P.op("gpsimd", lambda e, s=s, c0=c0: e.tensor_copy(
                    out=wo[:, c0:c0 + 2, :], in_=stg[s][:, :].rearrange("p (c n) -> p c n", c=2)),
                    reads=[stg_b[s]], writes=[wo_b])
            P.op("sync", lambda e: e.dma_start(out=wdr[:, :, :], in_=wds_d.rearrange("f p n -> p f n")),
                 reads=[wsc_b], writes=[wdr_b], dbuf=wdr_b)

            def layer_norm(src, src_b, gi, dst, dst_b, sl):
                for c in range(2):
                    P.op("vector", lambda e, c=c: e.bn_stats(out=st6[sl][:, c * 6:(c + 1) * 6], in_=src[:, c * 512:(c + 1) * 512]),
                         reads=[src_b], writes=[st_b[sl]])
                P.op("vector", lambda e: e.bn_aggr(out=mv[sl][:, :], in_=st6[sl][:, :]), reads=[st_b[sl]], writes=[st_b[sl]])
                P.op("scalar", lambda e: e.activation(out=rs[sl][:, :], in_=mv[sl][:, 1:2], func=AF.Sqrt, bias=LN_EPS, scale=1.0),
                     reads=[st_b[sl]], writes=[st_b[sl]])
                P.op("vector", lambda e: e.reciprocal(out=rs[sl][:, :], in_=rs[sl][:, :]), reads=[st_b[sl]], writes=[st_b[sl]])
                P.op("vector", lambda e: e.scalar_tensor_tensor(out=src[:, :], in0=src[:, :], scalar=mv[sl][:, 0:1],
                                                                in1=lnp[gi][:, :], op0=ALU.subtract, op1=ALU.mult),
                     reads=[src_b, st_b[sl], lnp_b], writes=[src_b])
                P.op("vector", lambda e: e.scalar_tensor_tensor(out=dst, in0=src[:, :], scalar=rs[sl][:, 0:1],
                                                                in1=lnp[gi + 1][:, :], op0=ALU.mult, op1=ALU.add),
                     reads=[src_b, st_b[sl], lnp_b], writes=[dst_b])

            mix_v = mixT_d.rearrange("(c p) t -> p c t", p=128)
            wn = 0
            for k in range(8):
                P.op("sync", lambda e, k=k: e.dma_start(out=mxs[:, :, :], in_=mix_v[:, :, k * CH:(k + 1) * CH]),
                     reads=[mixT_b], writes=[mxs_b], dbuf=mxs_b)
                for t in range(4):
                    tok0 = (k * 4 + t) * 128
                    xs_ = (k * 4 + t) % 2
                    P.op("sync", lambda e, xs_=xs_, tok0=tok0: e.dma_start(out=xt[xs_][:, :], in_=xn_d[tok0:tok0 + 128, :]),
                         writes=[xt_b[xs_]], dbuf=xt_b[xs_])
                    for half in range(2):
                        bk = 2 + half
                        for kc in range(8):
                            P.op("tensor", lambda e, kc=kc, t=t, half=half, bk=bk: e.matmul(
                                bank(bk), lhsT=mxs[:, kc, t * 128:(t + 1) * 128], rhs=wo[:, kc, half * 512:(half + 1) * 512],
                                start=(kc == 0), stop=(kc == 7)),
                                reads=[mxs_b, wo_b], writes=[PB[bk]])
                        P.op("vector", lambda e, xs_=xs_, half=half, bk=bk: e.scalar_tensor_tensor(
                            out=hhs[xs_][:, half * 512:(half + 1) * 512], in0=xt[xs_][:, half * 512:(half + 1) * 512], scalar=ALPHA,
                            in1=bank(bk), op0=ALU.mult, op1=ALU.add),
                            reads=[xt_b[xs_], PB[bk]], writes=[hhs_b[xs_]])
                    layer_norm(hhs[xs_], hhs_b[xs_], 0, h1[:, t, :], h1_b[t], xs_)
                    P.op("scalar", lambda e, t=t, xs_=xs_: e.activation(out=h1bs[xs_][:, :], in_=h1[:, t, :], func=AF.Copy),
                         reads=[h1_b[t]], writes=[h1bs_b[xs_]])
                    for kc in range(8):
                        P.op("tensor", lambda e, kc=kc, xs_=xs_: e.transpose(
                            out=pT[:, kc * 128:(kc + 1) * 128], in_=h1bs[xs_][:, kc * 128:(kc + 1) * 128], identity=idt[:, :]),
                            reads=[h1bs_b[xs_], id_b], writes=[PB[7]])
                    P.op("vector", lambda e, t=t: e.tensor_copy(
                        out=h1T[:, :, t * 128:(t + 1) * 128], in_=pT[:, :].rearrange("p (c n) -> p c n", c=8)),
                        reads=[PB[7]], writes=[h1T_b])
                for f in range(NF):
                    ws = wn % 3
                    wn += 1
                    P.op("sync", lambda e, ws=ws, f=f: e.dma_start(
                        out=wgt[ws][:, :, :], in_=wgs_d[f, :, :].rearrange("p (c j) -> p c j", c=8)),
                        reads=[wsc_b], writes=[wgt_b[ws]], dbuf=wgt_b[ws])
                    P.op("sync", lambda e, ws=ws, f=f: e.dma_start(
                        out=wut[ws][:, :, :], in_=wus_d[f, :, :].rearrange("p (c j) -> p c j", c=8)),
                        reads=[wsc_b], writes=[wut_b[ws]], dbuf=wut_b[ws])
                    gb = 0 + (f % 2)
                    ub = 4 + (f % 2)
                    for kc in range(8):
                        P.op("tensor", lambda e, kc=kc, ws=ws, gb=gb: e.matmul(
                            bank(gb), lhsT=wgt[ws][:, kc, :], rhs=h1T[:, kc, :], start=(kc == 0), stop=(kc == 7)),
                            reads=[wgt_b[ws], h1T_b], writes=[PB[gb]])
                    for kc in range(8):
                        P.op("tensor", lambda e, kc=kc, ws=ws, ub=ub: e.matmul(
                            bank(ub), lhsT=wut[ws][:, kc, :], rhs=h1T[:, kc, :], start=(kc == 0), stop=(kc == 7)),
                            reads=[wut_b[ws], h1T_b], writes=[PB[ub]])
                    ss = f % 2
                    P.op("scalar", lambda e, ss=ss, gb=gb: e.activation(out=sg[ss][:, :], in_=bank(gb), func=AF.Silu),
                         reads=[PB[gb]], writes=[sg_b[ss]])
                    P.op("vector", lambda e, ss=ss, ub=ub, f=f: e.tensor_tensor(
                        out=aT[:, f, :], in0=sg[ss][:, :], in1=bank(ub), op=ALU.mult),
                        reads=[sg_b[ss], PB[ub]], writes=[aT_b])
                for t in range(4):
                    tok0 = (k * 4 + t) * 128
                    for half in range(2):
                        bk = 2 + half
                        for f in range(NF):
                            P.op("tensor", lambda e, f=f, t=t, half=half, bk=bk: e.matmul(
                                bank(bk), lhsT=aT[:, f, t * 128:(t + 1) * 128], rhs=wdr[:, f, half * 512:(half + 1) * 512],
                                start=(f == 0), stop=(f == NF - 1)),
                                reads=[aT_b, wdr_b], writes=[PB[bk]])
                        os_ = (k * 4 + t) % 2
                        P.op("vector", lambda e, t=t, half=half, bk=bk, os_=os_: e.scalar_tensor_tensor(
                            out=hhs[os_][:, half * 512:(half + 1) * 512], in0=h1[:, t, half * 512:(half + 1) * 512], scalar=ALPHA,
                            in1=bank(bk), op0=ALU.mult, op1=ALU.add),
                            reads=[h1_b[t], PB[bk]], writes=[hhs_b[os_]])
                    os_ = (k * 4 + t) % 2
                    layer_norm(hhs[os_], hhs_b[os_], 2, ot[os_][:, :], ot_b[os_], os_)
                    P.op("sync", lambda e, os_=os_, tok0=tok0: e.dma_start(out=out_d[tok0:tok0 + 128, :], in_=ot[os_][:, :]),
                         reads=[ot_b[os_]], dbuf=ot_b[os_])
            P.flush(final=True)
    return nc


_NC = None


def kernel(x, w_in, g_sb, g_dil, w_out, ln1_g, ln1_b, w_gate, w_up, w_down, ln2_g, ln2_b):
    global _NC
    x = np.asarray(x, np.float32)
    if _NC is None:
        _NC = build_nc()
    nc = _NC
    cst = _consts()
    f = lambda a: np.ascontiguousarray(np.asarray(a, np.float32))
    shared = {
        "cst": cst,
        "w_in": f(w_in[0]), "w_out": f(w_out[0]), "w_gate": f(w_gate[0]), "w_up": f(w_up[0]), "w_down": f(w_down[0]),
        "gsb": f(np.asarray(g_sb[0]).reshape(8, 64).T), "gdil": f(np.asarray(g_dil[0]).reshape(8, 64).T),
        "gsbp": f(np.asarray(g_sb[0]).reshape(4, 128).T),
        "ln1g": f(np.broadcast_to(np.asarray(ln1_g[0]), (128, D))), "ln1b": f(np.broadcast_to(np.asarray(ln1_b[0]), (128, D))),
        "ln2g": f(np.broadcast_to(np.asarray(ln2_g[0]), (128, D))), "ln2b": f(np.broadcast_to(np.asarray(ln2_b[0]), (128, D))),
    }
    in_maps = []
    for c in range(8):
        b, par = c // 2, c % 2
        xb_ = x[b]
        kvv = np.ones(S, np.float32)
        if par == 0:
            xv = np.concatenate([np.zeros((CH, D), np.float32), xb_[:S - CH]], axis=0)
            kvv[:CH] = 0.0
        else:
            xv = xb_
        m = dict(shared)
        m["xT"] = np.ascontiguousarray(xv.T)
        m["xn"] = np.ascontiguousarray(xb_.reshape(NV, CH, D)[par::2].reshape(S // 2, D))
        m["kv"] = np.ascontiguousarray(kvv.reshape(64, 128).T)
        in_maps.append(m)
    res = run_bass_kernel_spmd(nc, in_maps, core_ids=list(range(8)))
    out = np.empty((NB, S, D), np.float32)
    for c in range(8):
        b, par = c // 2, c % 2
        out[b].reshape(NV, CH, D)[par::2] = np.asarray(res.results[c]["out"], np.float32).reshape(8, CH, D)
    return out
```
